# Optimizing a Trainium2 kernel written in Bass

```python
import math
import jax, jax.numpy as jnp
from jax import lax
import numpy as np

D_MODEL = 1024
BATCH = 16
SEQ = 4096
DEPTH = 1

ATTN_WIDTH = D_MODEL // 2
RNN_WIDTH = D_MODEL - ATTN_WIDTH
ATTN_HEADS = 4
ATTN_HALF_DIM = ATTN_WIDTH // (2 * ATTN_HEADS)
ATTN_V_DIM = 2 * ATTN_HALF_DIM
RNN_HEADS = 4
RNN_HEAD_DIM = RNN_WIDTH // RNN_HEADS
D_FF = 2816
N_BUCKETS = 32
MAX_DISTANCE = 128
Q_BLOCK = 128
CHUNK = 64
EPS = 1e-6
IN_COLS = 3 * ATTN_WIDTH + 5 * RNN_WIDTH

kernel_name = "hymba_diffattn_hgrn2_macaron_encoder"


def rmsnorm(x, w):
    xf = x.astype(jnp.float32)
    y = xf * lax.rsqrt(jnp.mean(xf * xf, axis=-1, keepdims=True) + EPS)
    return (y * w.astype(jnp.float32)).astype(x.dtype)


def rel_bucket(rel):
    nb = N_BUCKETS // 2
    max_exact = nb // 2
    side = jnp.where(rel > 0, nb, 0)
    n = jnp.abs(rel)
    nf = jnp.maximum(n, 1).astype(jnp.float32)
    large = max_exact + (jnp.log(nf / max_exact) / math.log(MAX_DISTANCE / max_exact)
                         * (nb - max_exact)).astype(jnp.int32)
    large = jnp.minimum(large, nb - 1)
    return side + jnp.where(n < max_exact, n, large)


def diff_attention(q, k, v, rel_bias, lam):
    B, S = q.shape[0], q.shape[1]
    nblk = S // Q_BLOCK
    qb = jnp.moveaxis(q.reshape(B, nblk, Q_BLOCK, ATTN_HEADS, 2, ATTN_HALF_DIM), 1, 0)
    starts = jnp.arange(nblk, dtype=jnp.int32) * Q_BLOCK
    k_pos = jnp.arange(S, dtype=jnp.int32)
    scale = ATTN_HALF_DIM ** -0.5

    def block(args):
        q_blk, start = args
        q_pos = start + jnp.arange(Q_BLOCK, dtype=jnp.int32)
        bias = jnp.take(rel_bias, rel_bucket(k_pos[None, :] - q_pos[:, None]), axis=0)
        bias = jnp.transpose(bias, (2, 0, 1)).astype(jnp.float32)
        s = jnp.einsum('bqhcd,bkhcd->bhcqk', q_blk, k).astype(jnp.float32) * scale + bias[None, :, None]
        p = jax.nn.softmax(s, axis=-1)
        w = (p[:, :, 0] - lam * p[:, :, 1]).astype(v.dtype)
        return jnp.einsum('bhqk,bkhe->bqhe', w, v)

    o = lax.map(block, (qb, starts))
    return jnp.moveaxis(o, 0, 1).reshape(B, S, ATTN_HEADS, ATTN_V_DIM)


def hgrn2_chunk_scan(q, k, v, g):
    B, S, H, dk = q.shape
    dv = v.shape[-1]
    n = S // CHUNK

    def chunks(t):
        return jnp.moveaxis(t.reshape(B, n, CHUNK, H, t.shape[-1]), 1, 0)

    qc, kc, vc, gc = chunks(q), chunks(k), chunks(v), chunks(g)
    G = jnp.cumsum(gc, axis=2)
    G_last = G[:, :, -1]
    G_mid = G[:, :, CHUNK // 2 - 1:CHUNK // 2]
    q_intra = qc * jnp.exp(G - G_mid)
    k_intra = kc * jnp.exp(G_mid - G)
    lower = jnp.tril(jnp.ones((CHUNK, CHUNK), dtype=bool))
    A = jnp.einsum('nbthk,nbshk->nbhts', q_intra, k_intra)
    A = jnp.where(lower, A, 0.0)
    o_intra = jnp.einsum('nbhts,nbshv->nbthv', A, vc)
    k_dec = kc * jnp.exp(G_last[:, :, None] - G)

    def step(state, xs):
        kd, vv, gl = xs
        new = jnp.exp(gl)[..., None] * state + jnp.einsum('bchk,bchv->bhkv', kd, vv).astype(jnp.float32)
        return new, state

    s0 = jnp.zeros((B, H, dk, dv), jnp.float32)
    _, s_prev = lax.scan(step, s0, (k_dec, vc, G_last))
    o_inter = jnp.einsum('nbthk,nbhkv->nbthv', qc * jnp.exp(G), s_prev)
    o = o_intra + o_inter
    return jnp.moveaxis(o, 0, 1).reshape(B, S, H, dv)


def hgrn2_gates(f_logits, lb):
    sig = jax.nn.sigmoid(f_logits.astype(jnp.float32))
    log_f = jnp.log(lb + (1.0 - lb) * sig)
    one_minus_f = (1.0 - lb) * (1.0 - sig)
    return log_f, one_minus_f


def swiglu_half_step(h, pre, w_in, w_out, post):
    u = rmsnorm(h, pre)
    gate, up = jnp.split(u @ w_in, 2, axis=-1)
    y = (jax.nn.silu(gate) * up) @ w_out
    return h + 0.5 * rmsnorm(y, post)


def setup_inputs(seed: int = 0) -> dict:
    key = jax.random.key(seed)
    ks = jax.random.split(key, 24)
    f32 = jnp.float32

    def w(k, shape, fan_in):
        return jax.random.normal(k, shape, f32) * fan_in ** -0.5

    def gain(k, shape):
        return 1.0 + 0.02 * jax.random.normal(k, shape, f32)

    L = DEPTH
    return {
        "x": jax.random.normal(ks[0], (BATCH, SEQ, D_MODEL), f32),
        "rel_bias": 0.1 * jax.random.normal(ks[1], (N_BUCKETS, ATTN_HEADS), f32),
        "lb_logits": 0.1 * jax.random.normal(ks[2], (2, DEPTH + 1, RNN_WIDTH), f32),
        "ffn1_pre_norm": gain(ks[3], (L, D_MODEL)),
        "ffn1_w_in": w(ks[4], (L, D_MODEL, 2 * D_FF), D_MODEL),
        "ffn1_w_out": w(ks[5], (L, D_FF, D_MODEL), D_FF),
        "ffn1_post_norm": gain(ks[6], (L, D_MODEL)),
        "mix_pre_norm": gain(ks[7], (L, D_MODEL)),
        "w_mix_in": w(ks[8], (L, D_MODEL, IN_COLS), D_MODEL),
        "lambda_q1": 0.1 * jax.random.normal(ks[9], (L, ATTN_HALF_DIM), f32),
        "lambda_k1": 0.1 * jax.random.normal(ks[10], (L, ATTN_HALF_DIM), f32),
        "lambda_q2": 0.1 * jax.random.normal(ks[11], (L, ATTN_HALF_DIM), f32),
        "lambda_k2": 0.1 * jax.random.normal(ks[12], (L, ATTN_HALF_DIM), f32),
        "attn_head_norm": gain(ks[13], (L, ATTN_V_DIM)),
        "rnn_head_norm": gain(ks[14], (L, RNN_HEAD_DIM)),
        "w_mix_out": w(ks[15], (L, ATTN_WIDTH + RNN_WIDTH, D_MODEL), ATTN_WIDTH + RNN_WIDTH),
        "mix_post_norm": gain(ks[16], (L, D_MODEL)),
        "ffn2_pre_norm": gain(ks[17], (L, D_MODEL)),
        "ffn2_w_in": w(ks[18], (L, D_MODEL, 2 * D_FF), D_MODEL),
        "ffn2_w_out": w(ks[19], (L, D_FF, D_MODEL), D_FF),
        "ffn2_post_norm": gain(ks[20], (L, D_MODEL)),
    }


def reference(x, rel_bias, lb_logits, ffn1_pre_norm, ffn1_w_in, ffn1_w_out, ffn1_post_norm,
              mix_pre_norm, w_mix_in, lambda_q1, lambda_k1, lambda_q2, lambda_k2,
              attn_head_norm, rnn_head_norm, w_mix_out, mix_post_norm,
              ffn2_pre_norm, ffn2_w_in, ffn2_w_out, ffn2_post_norm):
    B, S, _ = x.shape
    splits = [ATTN_WIDTH, 2 * ATTN_WIDTH, 3 * ATTN_WIDTH]
    splits += [3 * ATTN_WIDTH + j * RNN_WIDTH for j in range(1, 5)]
    lb_all = jnp.cumsum(jax.nn.softmax(lb_logits.astype(jnp.float32), axis=1), axis=1)
    h = x
    for layer in range(DEPTH):
        h = swiglu_half_step(h, ffn1_pre_norm[layer], ffn1_w_in[layer], ffn1_w_out[layer],
                             ffn1_post_norm[layer])

        u = rmsnorm(h, mix_pre_norm[layer])
        proj = u @ w_mix_in[layer]
        q_a, k_a, v_a, q_r, i_r, f_fw, f_bw, g_r = jnp.split(proj, splits, axis=-1)

        lambda_init = 0.8 - 0.6 * math.exp(-0.3 * layer)
        lam = (jnp.exp(jnp.sum(lambda_q1[layer] * lambda_k1[layer]).astype(jnp.float32))
               - jnp.exp(jnp.sum(lambda_q2[layer] * lambda_k2[layer]).astype(jnp.float32))
               + lambda_init)
        qa = q_a.reshape(B, S, ATTN_HEADS, 2, ATTN_HALF_DIM)
        ka = k_a.reshape(B, S, ATTN_HEADS, 2, ATTN_HALF_DIM)
        va = v_a.reshape(B, S, ATTN_HEADS, ATTN_V_DIM)
        o_a = diff_attention(qa, ka, va, rel_bias, lam)
        o_a = rmsnorm(o_a, attn_head_norm[layer]) * (1.0 - lambda_init)
        o_a = o_a.reshape(B, S, ATTN_WIDTH)

        lb = lb_all[:, layer]
        g_fw, k_fw = hgrn2_gates(f_fw, lb[0])
        g_bw, k_bw = hgrn2_gates(f_bw, lb[1])
        heads = lambda t: t.reshape(B, S, RNN_HEADS, RNN_HEAD_DIM)
        qr = heads(jax.nn.silu(q_r))
        ir = heads(i_r)
        o_fw = hgrn2_chunk_scan(qr, heads(k_fw), ir, heads(g_fw))
        flip = lambda t: jnp.flip(t, axis=1)
        o_bw = flip(hgrn2_chunk_scan(flip(qr), flip(heads(k_bw)), flip(ir), flip(heads(g_bw))))
        o_r = rmsnorm(o_fw + o_bw, rnn_head_norm[layer]) * jax.nn.silu(heads(g_r))
        o_r = o_r.reshape(B, S, RNN_WIDTH)

        mixed = jnp.concatenate([o_a, o_r.astype(o_a.dtype)], axis=-1) @ w_mix_out[layer]
        h = h + rmsnorm(mixed, mix_post_norm[layer])

        h = swiglu_half_step(h, ffn2_pre_norm[layer], ffn2_w_in[layer], ffn2_w_out[layer],
                             ffn2_post_norm[layer])
    return h.astype(x.dtype)
```

```python
import math
from contextlib import ExitStack

import numpy as np
import concourse.bass as bass
import concourse.mybir as mybir
from concourse.bass_utils import run_bass_kernel_spmd

F32 = mybir.dt.float32
BF16 = mybir.dt.bfloat16
AF = mybir.ActivationFunctionType
ALU = mybir.AluOpType
AX = mybir.AxisListType

D = 1024
DC = 8
DFF = 2816
FC = 22
NH = 4
EPS = 1e-6
N_CORES = 8
ENGS = ("pe", "act", "dve", "pool", "sp")
EPOCH = 30000


class Buf:
    __slots__ = ("name", "lw", "rd", "chan")

    def __init__(self, name=""):
        self.name = name
        self.lw = None
        self.rd = {}
        self.chan = None


def bufs(n, name=""):
    return [Buf(f"{name}{i}") for i in range(n)]


class Chan:
    __slots__ = ("sem", "count", "id")
    _n = 0

    def __init__(self):
        self.sem = None
        self.count = 0
        Chan._n += 1
        self.id = Chan._n


class Prog:
    def __init__(self, nc):
        self.nc = nc
        self.ops = {e: [] for e in ENGS}
        self.known = {e: {} for e in ENGS}
        self.chans = []
        self.free_chans = []
        self.live_chans = []

    def _need(self, eng, tok, kind, waits):
        if tok is None:
            return
        if tok[0] == "e":
            _, f, i = tok
            if f == eng:
                if eng == "pe" or kind != "raw":
                    return
            key = ("e", f)
            val = i
        else:
            _, ch, v = tok
            key = ("d", ch.id)
            val = v
        kn = self.known[eng]
        if kn.get(key, -1) >= val:
            return
        kn[key] = val
        waits.append(tok)
        if tok[0] == "e":
            self.ops[tok[1]][tok[2]]["inc"] = True

    def _mk(self, eng, fn, reads, writes, dma_chan=None):
        idx = len(self.ops[eng])
        waits = []
        for b in reads:
            self._need(eng, b.lw, "raw", waits)
        for b in writes:
            self._need(eng, b.lw, "waw", waits)
            for t in b.rd.values():
                self._need(eng, t, "war", waits)
        rec = dict(fn=fn, waits=waits, inc=False, chan=dma_chan, val=None)
        self.ops[eng].append(rec)
        if dma_chan is None:
            tok = ("e", eng, idx)
        else:
            dma_chan.count += 16
            tok = ("d", dma_chan, dma_chan.count)
        for b in writes:
            b.lw = tok
            b.rd = {}
        for b in reads:
            k = ("e", tok[1]) if tok[0] == "e" else ("d", tok[1].id)
            b.rd[k] = tok
        return tok

    def op(self, eng, fn, reads=(), writes=()):
        return self._mk(eng, fn, list(reads), list(writes))

    def dma(self, q, out_ap, in_ap, sb, reads=(), writes=()):
        if sb.chan is None:
            if self.free_chans:
                sb.chan = self.free_chans.pop()
            else:
                sb.chan = Chan()
                self.chans.append(sb.chan)
            self.live_chans.append(sb.chan)
        return self._mk(q, lambda e: e.dma_start(out=out_ap, in_=in_ap),
                        list(reads), list(writes), dma_chan=sb.chan)

    def barrier(self):
        bb = Buf("barrier")
        waits = []
        for e in ENGS:
            if e != "sp" and self.ops[e]:
                self._need("sp", ("e", e, len(self.ops[e]) - 1), "raw", waits)
        for ch in self.chans:
            if ch.count:
                self._need("sp", ("d", ch, ch.count), "raw", waits)
        idx = len(self.ops["sp"])
        self.ops["sp"].append(dict(fn=lambda e: e.nop(), waits=waits, inc=False, chan=None, val=None))
        tok = ("e", "sp", idx)
        bb.lw = tok
        for e in ENGS:
            if e != "sp":
                self._mk(e, lambda en: en.nop(), [bb], [])
        self.free_chans.extend(self.live_chans)
        self.live_chans = []

    def emit(self, stack):
        nc = self.nc
        esems = {}
        for e in ENGS:
            n = 0
            for rec in self.ops[e]:
                if rec["inc"]:
                    rec["val"] = (n // EPOCH, n % EPOCH + 1)
                    n += 1
            nep = (n + EPOCH - 1) // EPOCH
            esems[e] = [stack.enter_context(nc.semaphore(f"s_{e}{k}")) for k in range(max(nep, 1))]
        for ch in self.chans:
            assert ch.count < 60000, ch.count
            ch.sem = stack.enter_context(nc.semaphore(f"s_ch{ch.id}"))
        ops = self.ops

        def run(eng_name):
            def body(eng):
                for rec in ops[eng_name]:
                    for tok in rec["waits"]:
                        if tok[0] == "e":
                            ep, v = ops[tok[1]][tok[2]]["val"]
                            eng.wait_ge(esems[tok[1]][ep], v)
                        else:
                            eng.wait_ge(tok[1].sem, tok[2])
                    ins = rec["fn"](eng)
                    if rec["chan"] is not None:
                        ins.then_inc(rec["chan"].sem, 16)
                    elif rec["inc"]:
                        ins.then_inc(esems[eng_name][rec["val"][0]], 1)
            return body

        with nc.Block() as block:
            block.sync(run("sp"))
            block.scalar(run("act"))
            block.vector(run("dve"))
            block.gpsimd(run("pool"))
            block.tensor(run("pe"))


class K:
    pass


def build(NB, S, stage="full", debug=False):
    nc = bass.Bass("TRN2", target_bir_lowering=False)
    NTOK = NB * S
    P = Prog(nc)
    k = K()
    k.nc, k.P, k.NB, k.S, k.NTOK = nc, P, NB, S, NTOK
    import os
    k.attn_prefetch = os.environ.get('ATTN_PREFETCH', '1') == '1'

    def din(name, shape, dt=F32):
        return nc.dram_tensor(name, list(shape), dt, kind="ExternalInput").ap()

    def dscr(name, shape, dt):
        return nc.dram_tensor(name, list(shape), dt, kind="ExternalOutput" if debug else "Internal").ap()

    k.x = din("x", [NTOK, D])
    k.out = nc.dram_tensor("out", [NTOK, D], F32, kind="ExternalOutput").ap()
    k.w = {}
    for nm, shp in (("ffn1_w_in", [D, 2 * DFF]), ("ffn1_w_out", [DFF, D]),
                    ("ffn2_w_in", [D, 2 * DFF]), ("ffn2_w_out", [DFF, D]),
                    ("w_mix_in", [D, 4096]), ("w_mix_out", [D, D])):
        k.w[nm] = din(nm, shp)
    k.nv = din("normvecs", [128, 7 * DC])
    k.ident_f = din("ident_f", [128, 128])
    k.ident_b = din("ident_b", [128, 128], BF16)
    k.ones_b = din("ones_b", [128, 128], BF16)
    k.gbias = din("gbias", [128, NH * 1152])
    k.cfar = din("cfar", [128, 2 * NH])
    k.lamv = din("lamv", [128, 4 * 64])
    k.lbl = din("lbl", [128, 2 * 2 * 512])
    k.hnorm = din("hnorm", [128, 2 * 128])
    k.hmat = din("hmat", [128, 2 * 256 + 2 * 128], BF16)
    k.hmask = din("hmask", [128, 2 * 128])
    k.h1T = dscr("h1T", [DC, 128, NTOK], F32)
    k.h2T = dscr("h2T", [DC, 128, NTOK], F32)
    k.QT_s = dscr("QT_s", [NB, NH, 128, S], BF16)
    k.KT_s = dscr("KT_s", [NB, NH, 128, S], BF16)
    k.V_s = dscr("V_s", [NB, S, 512], BF16)
    k.qrT_s = dscr("qrT_s", [NB, NH, 128, S], BF16)
    k.kT_s = [dscr(f"kT_s{d}", [NB, NH, 128, S], BF16) for d in range(2)]
    k.k_s = [dscr(f"k_s{d}", [NB, S, 512], BF16) for d in range(2)]
    k.ghi_s = [dscr(f"ghi_s{d}", [NB, S, 512], BF16) for d in range(2)]
    k.glo_s = [dscr(f"glo_s{d}", [NB, S, 512], BF16) for d in range(2)]
    k.ir_s = dscr("ir_s", [NB, S, 512], BF16)
    k.gate_s = dscr("gate_s", [NB, S, 512], BF16)
    k.mixT_s = dscr("mixT_s", [DC, 128, NTOK], BF16)

    with ExitStack() as gs:
        psall = gs.enter_context(nc.psum_tensor("psall", [128, 4096], F32))
        k.psall = psall
        k.ps = [psall[:, i * 512:(i + 1) * 512] for i in range(8)]
        k.psb = bufs(8, "psb")
        k.nv_t = gs.enter_context(nc.sbuf_tensor("nv_t", [128, 7 * DC], F32))
        k.nvh_t = gs.enter_context(nc.sbuf_tensor("nvh_t", [128, 7 * DC], F32))
        k.idf_t = gs.enter_context(nc.sbuf_tensor("idf_t", [128, 128], F32))
        k.idb_t = gs.enter_context(nc.sbuf_tensor("idb_t", [128, 128], BF16))
        k.ones_t = gs.enter_context(nc.sbuf_tensor("ones_t", [128, 128], BF16))
        k.eps_t = gs.enter_context(nc.sbuf_tensor("eps_t", [128, 1], F32))
        k.cb = Buf("consts")
        cb2 = bufs(4, "cld")
        P.dma("sp", k.nv_t[:], k.nv, cb2[0], writes=[cb2[0]])
        P.dma("sp", k.idf_t[:], k.ident_f, cb2[1], writes=[cb2[1]])
        P.dma("sp", k.idb_t[:], k.ident_b, cb2[2], writes=[cb2[2]])
        P.dma("sp", k.ones_t[:], k.ones_b, cb2[3], writes=[cb2[3]])
        P.op("dve", lambda e: e.memset(k.eps_t[:], EPS), writes=[k.cb])
        P.op("dve", lambda e: e.tensor_scalar(k.nvh_t[:], k.nv_t[:], 0.5, None, ALU.mult),
             reads=cb2, writes=[k.cb])

        if stage == "ffn1":
            ffn_phase(k, "ffn1", ("tm", k.x), ("tm", k.out), 0, 1)
        elif stage.startswith("mix"):
            parts = stage.split(":")[1] if ":" in stage else "paho"
            ffn_load_only(k, ("tm", k.x), ("fm", k.h1T))
            if "p" in parts:
                proj_phase(k)
            if "a" in parts:
                attn_phase(k)
            if "h" in parts:
                hgrn_phase(k)
            if "o" in parts:
                outproj_phase(k)
            ffn_load_only(k, ("fm", k.h2T if "o" in parts else k.h1T), ("tm", k.out))
        else:
            ffn_phase(k, "ffn1", ("tm", k.x), ("fm", k.h1T), 0, 1)
            proj_phase(k)
            attn_phase(k)
            hgrn_phase(k)
            outproj_phase(k)
            ffn_phase(k, "ffn2", ("fm", k.h2T), ("tm", k.out), 5, 6)

        P.barrier()
        P.emit(gs)
    return nc


def ffn_phase(k, wname, src, dst, iv_pre, iv_post):
    nc, P = k.nc, k.P
    T = 256
    NT = k.NTOK // T
    TT = T // 128
    passthru = wname is None
    if passthru:
        k.pt_ctr = getattr(k, "pt_ctr", 0) + 1
        wname = f"pt{k.pt_ctr}"
    else:
        w_in_d = k.w[wname + "_w_in"]
        w_out_d = k.w[wname + "_w_out"]
    with ExitStack() as st:
        sb = lambda name, shape, dt: st.enter_context(nc.sbuf_tensor(f"sb_{wname}_{name}", shape, dt))
        w_in = sb("w_in", [128, DC, 2 * DFF if not passthru else 2], BF16)
        w_out = sb("w_out", [128, FC, D if not passthru else 2], BF16)
        xtm = [sb(f"xtm{i}", [128, D], F32) for i in range(3)]
        xT = [sb(f"xT{i}", [128, DC, T], F32) for i in range(2)]
        uT = [sb(f"uT{i}", [128, DC, T], BF16) for i in range(2)]
        sq = sb("sq", [128, DC, T], BF16)
        hid = sb("hid", [128, FC, T], BF16)
        yT = sb("yT", [128, DC, T], F32)
        sg = [sb(f"sg{i}", [128, T], BF16) for i in range(2)]
        rstd = [sb(f"rstd{i}", [128, T], F32) for i in range(2)]
        b_win = bufs(4, "win")
        b_wout = bufs(2, "wout")
        wgrp = lambda fc: min(fc // 6, 3)
        wogrp = lambda fc: 0 if fc < 11 else 1
        b_xtm = bufs(3, "xtm")
        b_xT = [bufs(DC, f"xT{i}_") for i in range(2)]
        b_uT = [bufs(DC, f"uT{i}_") for i in range(2)]
        b_sq = bufs(DC, "sq")
        b_hid = bufs(FC, "hid")
        b_yT = bufs(DC, "yT")
        b_sg = bufs(2, "sg")
        b_rstd = bufs(2, "rstd")
        ps, psb = k.ps, k.psb
        PS_G, PS_U, PS_Y, PS_SS, PS_TR = (0, 1), (2, 3), (4, 5), 6, 7

        for g in range(4 if not passthru else 0):
            c0, c1 = g * 6 * 128, min((g + 1) * 6, FC) * 128
            for half in (0, DFF):
                for dc in range(DC):
                    P.dma("pool", w_in[:, dc, half + c0: half + c1],
                          w_in_d[dc * 128:(dc + 1) * 128, half + c0: half + c1], b_win[g], writes=[b_win[g]])
        for fc in range(FC if not passthru else 0):
            P.dma("pool", w_out[:, fc, :], w_out_d[fc * 128:(fc + 1) * 128, :], b_wout[wogrp(fc)],
                  writes=[b_wout[wogrp(fc)]])

        xt_ctr = [0]

        def prologue(c):
            s = c % 2
            tok0 = c * T
            if src[0] == "tm":
                for tt in range(TT):
                    r = xt_ctr[0] % 3
                    xt_ctr[0] += 1
                    P.dma("sp", xtm[r][:], src[1][tok0 + tt * 128: tok0 + (tt + 1) * 128, :], b_xtm[r],
                          writes=[b_xtm[r]])
                    for g in range(2):
                        for j in range(4):
                            dc = g * 4 + j
                            P.op("pe", lambda e, r=r, dc=dc, j=j: e.transpose(
                                ps[PS_TR][:, j * 128:(j + 1) * 128], xtm[r][:, dc * 128:(dc + 1) * 128], k.idf_t[:]),
                                reads=[b_xtm[r], k.cb], writes=[psb[PS_TR]])
                        P.op("act", lambda e, s=s, g=g, tt=tt: e.activation(
                            xT[s][:, g * 4:(g + 1) * 4, tt * 128:(tt + 1) * 128],
                            ps[PS_TR][:].rearrange("p (j t) -> p j t", j=4), AF.Copy),
                            reads=[psb[PS_TR]], writes=b_xT[s][g * 4:(g + 1) * 4])
            else:
                for dc in range(DC):
                    P.dma("sp", xT[s][:, dc, :], src[1][dc, :, tok0:tok0 + T], b_xT[s][dc], writes=[b_xT[s][dc]])
            if passthru:
                return
            rms_scale(k, xT[s], b_xT[s], sq, b_sq, rstd[s], b_rstd[s], PS_SS, T)
            for dc in range(DC):
                P.op("dve", lambda e, s=s, dc=dc: e.scalar_tensor_tensor(
                    uT[s][:, dc, :], xT[s][:, dc, :], k.nv_t[:, iv_pre * DC + dc: iv_pre * DC + dc + 1], rstd[s][:],
                    ALU.mult, ALU.mult),
                    reads=[b_xT[s][dc], b_rstd[s], k.cb], writes=[b_uT[s][dc]])

        def gate_up(c):
            s = c % 2
            for fc in range(FC):
                pg, pu = PS_G[fc % 2], PS_U[fc % 2]
                for dc in range(DC):
                    P.op("pe", lambda e, pg=pg, dc=dc, fc=fc, s=s: e.matmul(
                        ps[pg][:, 0:T], w_in[:, dc, fc * 128:(fc + 1) * 128], uT[s][:, dc, :],
                        start=(dc == 0), stop=(dc == DC - 1)),
                        reads=[b_win[wgrp(fc)], b_uT[s][dc]], writes=[psb[pg]])
                for dc in range(DC):
                    P.op("pe", lambda e, pu=pu, dc=dc, fc=fc, s=s: e.matmul(
                        ps[pu][:, 0:T], w_in[:, dc, DFF + fc * 128: DFF + (fc + 1) * 128], uT[s][:, dc, :],
                        start=(dc == 0), stop=(dc == DC - 1)),
                        reads=[b_win[wgrp(fc)], b_uT[s][dc]], writes=[psb[pu]])
                P.op("act", lambda e, pg=pg, fc=fc: e.activation(sg[fc % 2][:], ps[pg][:, 0:T], AF.Silu),
                     reads=[psb[pg]], writes=[b_sg[fc % 2]])
                P.op("dve", lambda e, pu=pu, fc=fc: e.tensor_tensor(
                    hid[:, fc, :], sg[fc % 2][:], ps[pu][:, 0:T], ALU.mult),
                    reads=[b_sg[fc % 2], psb[pu]], writes=[b_hid[fc]])

        def down(c):
            for dc in range(DC):
                py = PS_Y[dc % 2]
                for fc in range(FC):
                    P.op("pe", lambda e, py=py, dc=dc, fc=fc: e.matmul(
                        ps[py][:, 0:T], w_out[:, fc, dc * 128:(dc + 1) * 128], hid[:, fc, :],
                        start=(fc == 0), stop=(fc == FC - 1)),
                        reads=[b_wout[wogrp(fc)], b_hid[fc]], writes=[psb[py]])
                P.op("act", lambda e, py=py, dc=dc: e.activation(yT[:, dc, :], ps[py][:, 0:T], AF.Copy),
                     reads=[psb[py]], writes=[b_yT[dc]])

        def epilogue(c):
            s = c % 2
            tok0 = c * T
            r2, b_r2 = rstd[s], b_rstd[s]
            if not passthru:
                rms_scale(k, yT, b_yT, sq, b_sq, r2, b_r2, PS_SS, T)
            for dc in range(DC if not passthru else 0):
                P.op("dve", lambda e, dc=dc: e.scalar_tensor_tensor(
                    yT[:, dc, :], yT[:, dc, :], k.nvh_t[:, iv_post * DC + dc: iv_post * DC + dc + 1], r2[:],
                    ALU.mult, ALU.mult),
                    reads=[b_yT[dc], b_r2, k.cb], writes=[b_yT[dc]])
                P.op("pool", lambda e, dc=dc, s=s: e.tensor_tensor(
                    xT[s][:, dc, :], xT[s][:, dc, :], yT[:, dc, :], ALU.add),
                    reads=[b_yT[dc], b_xT[s][dc]], writes=[b_xT[s][dc]])
            if dst[0] == "fm":
                for dc in range(DC):
                    P.dma("sp", dst[1][dc, :, tok0:tok0 + T], xT[s][:, dc, :], b_xT[s][dc], reads=[b_xT[s][dc]])
            else:
                for tt in range(TT):
                    r = xt_ctr[0] % 3
                    xt_ctr[0] += 1
                    for g in range(2):
                        for j in range(4):
                            dc = g * 4 + j
                            P.op("pe", lambda e, s=s, dc=dc, j=j, tt=tt: e.transpose(
                                ps[PS_TR][:, j * 128:(j + 1) * 128], xT[s][:, dc, tt * 128:(tt + 1) * 128], k.idf_t[:]),
                                reads=[b_xT[s][dc], k.cb], writes=[psb[PS_TR]])
                        P.op("act", lambda e, r=r, g=g: e.activation(
                            xtm[r][:, g * 512:(g + 1) * 512], ps[PS_TR][:], AF.Copy),
                            reads=[psb[PS_TR]], writes=[b_xtm[r]])
                    P.dma("sp", dst[1][tok0 + tt * 128: tok0 + (tt + 1) * 128, :], xtm[r][:], b_xtm[r],
                          reads=[b_xtm[r]])

        prologue(0)
        for c in range(NT):
            if not passthru:
                gate_up(c)
            if c + 1 < NT:
                prologue(c + 1)
            if not passthru:
                down(c)
            epilogue(c)
        P.barrier()


def ffn_load_only(k, src, dst):
    ffn_phase(k, None, src, dst, None, None)


def rms_scale(k, src, b_src, sq, b_sq, rstd, b_rstd, ps_i, T, lnexp=False):
    P = k.P
    ps, psb = k.ps, k.psb
    for dc in range(DC):
        P.op("act", lambda e, dc=dc: e.activation(sq[:, dc, :], src[:, dc, :], AF.Square),
             reads=[b_src[dc]], writes=[b_sq[dc]])
    for dc in range(DC):
        P.op("pe", lambda e, dc=dc: e.matmul(ps[ps_i][:, 0:T], k.ones_t[:], sq[:, dc, :],
                                              start=(dc == 0), stop=(dc == DC - 1)),
             reads=[b_sq[dc], k.cb], writes=[psb[ps_i]])
    if lnexp:
        P.op("act", lambda e: e.activation(rstd[:], ps[ps_i][:, 0:T], AF.Ln, bias=k.eps_t[:], scale=1.0 / D),
             reads=[psb[ps_i], k.cb], writes=[b_rstd])
        P.op("act", lambda e: e.activation(rstd[:], rstd[:], AF.Exp, scale=-0.5),
             reads=[b_rstd], writes=[b_rstd])
        return
    P.op("act", lambda e: e.activation(rstd[:], ps[ps_i][:, 0:T], AF.Sqrt, bias=k.eps_t[:], scale=1.0 / D),
         reads=[psb[ps_i], k.cb], writes=[b_rstd])
    P.op("dve", lambda e: e.reciprocal(rstd[:], rstd[:]),
         reads=[b_rstd], writes=[b_rstd])


class Ring:
    def __init__(self, tiles, name):
        self.t = tiles
        self.b = bufs(len(tiles), name)
        self.i = 0

    def next(self):
        r = self.i % len(self.t)
        self.i += 1
        return self.t[r], self.b[r]


def _ps_bf16(k, bank):
    return k.psall.bitcast(BF16)[:, bank * 1024:(bank + 1) * 1024]


def proj_phase(k):
    nc, P = k.nc, k.P
    T = 512
    NT = k.NTOK // T
    S = k.S
    wd = k.w["w_mix_in"]
    ps, psb = k.ps, k.psb
    with ExitStack() as st:
        sb = lambda name, shape, dt: st.enter_context(nc.sbuf_tensor(f"sb_pj_{name}", shape, dt))
        w = sb("w", [128, DC, 4096], BF16)
        xT = [sb(f"xT{i}", [128, DC, T], F32) for i in range(2)]
        uT = [sb(f"uT{i}", [128, DC, T], BF16) for i in range(2)]
        sq = sb("sq", [128, DC, T], BF16)
        rstd = [sb(f"rstd{i}", [128, T], F32) for i in range(2)]
        lbl_t = sb("lbl", [128, 2048], F32)
        lb_t = sb("lb", [128, 2, 512], F32)
        oml_t = sb("oml", [128, 2, 512], F32)
        fmbR = Ring([sb(f"fmb{i}", [128, T], BF16) for i in range(4)], "fmb")
        tmbR = Ring([sb(f"tmb{i}", [128, 512], BF16) for i in range(10)], "tmb")
        tmfR = Ring([sb(f"tmf{i}", [128, 512], F32) for i in range(3)], "tmf")
        tmpR = Ring([sb(f"tmp{i}", [128, 512], F32) for i in range(4)], "tmp")
        b_w = bufs(2, "pjw")
        b_xT = bufs(2, "pjxT")
        b_uT = bufs(2, "pjuT")
        b_sq = bufs(DC, "pjsq")
        b_rstd = bufs(2, "pjrstd")
        b_lbl, b_lb, b_oml = Buf("lbl"), Buf("lb"), Buf("oml")
        psT = _ps_bf16(k, 5)
        ctr = {"fm": 0, "tm": 0}

        for g in range(2):
            for dc in range(DC):
                P.dma("pool", w[:, dc, g * 2048:(g + 1) * 2048], wd[dc * 128:(dc + 1) * 128, g * 2048:(g + 1) * 2048],
                      b_w[g], writes=[b_w[g]])
        wg = lambda col: b_w[col // 2048]

        P.dma("sp", lbl_t[:], k.lbl, b_lbl, writes=[b_lbl])
        lv = lbl_t[:].rearrange("p (d l f) -> p d l f", d=2, l=2)
        P.op("dve", lambda e: e.tensor_tensor(lb_t[:], lv[:, :, 0, :], lv[:, :, 1, :], ALU.subtract),
             reads=[b_lbl], writes=[b_lb])
        P.op("act", lambda e: e.activation(lb_t[:], lb_t[:], AF.Exp, scale=-1.0), reads=[b_lb], writes=[b_lb])
        P.op("dve", lambda e: e.tensor_scalar(lb_t[:], lb_t[:], 1.0, None, ALU.add), reads=[b_lb], writes=[b_lb])
        P.op("dve", lambda e: e.reciprocal(lb_t[:], lb_t[:]), reads=[b_lb], writes=[b_lb])
        P.op("dve", lambda e: e.tensor_scalar(oml_t[:], lb_t[:], -1.0, 1.0, ALU.mult, ALU.add),
             reads=[b_lb], writes=[b_oml])

        def silu_evac(psrc, b_psrc, dst, b_dst):
            t, bt = tmpR.next()
            P.op("act", lambda e: e.activation(t[:], psrc, AF.Exp, scale=-1.0), reads=[b_psrc], writes=[bt])
            P.op("dve", lambda e: e.tensor_scalar(t[:], t[:], 1.0, None, ALU.add), reads=[bt], writes=[bt])
            P.op("dve", lambda e: e.reciprocal(t[:], t[:]), reads=[bt], writes=[bt])
            P.op("dve", lambda e: e.tensor_tensor(dst, psrc, t[:], ALU.mult), reads=[bt, b_psrc], writes=[b_dst])

        def prologue(c):
            s = c % 2
            tok0 = c * T
            for dc in range(DC):
                P.dma("sp", xT[s][:, dc, :], k.h1T[dc, :, tok0:tok0 + T], b_xT[s], writes=[b_xT[s]])
            rms_scale(k, xT[s], [b_xT[s]] * DC, sq, b_sq, rstd[s], b_rstd[s], 7, T, lnexp=True)
            for dc in range(DC):
                P.op("dve", lambda e, s=s, dc=dc: e.scalar_tensor_tensor(
                    uT[s][:, dc, :], xT[s][:, dc, :], k.nv_t[:, 2 * DC + dc: 2 * DC + dc + 1], rstd[s][:],
                    ALU.mult, ALU.mult),
                    reads=[b_xT[s], b_rstd[s], k.cb], writes=[b_uT[s]])

        def fm_part(c):
            s = c % 2
            tok0 = c * T
            b, sl = tok0 // S, tok0 % S
            for kind, colbase, dstT in (("qa", 0, k.QT_s), ("ka", 512, k.KT_s), ("qr", 1536, k.qrT_s)):
                for h in range(NH):
                    pb = ctr["fm"] % 2
                    ctr["fm"] += 1
                    col = colbase + h * 128
                    for dc in range(DC):
                        P.op("pe", lambda e, pb=pb, dc=dc, col=col, s=s: e.matmul(
                            ps[pb][:, 0:T], w[:, dc, col:col + 128], uT[s][:, dc, :],
                            start=(dc == 0), stop=(dc == DC - 1)),
                            reads=[wg(col), b_uT[s]], writes=[psb[pb]])
                    o, bo = fmbR.next()
                    if kind == "qr":
                        silu_evac(ps[pb][:, 0:T], psb[pb], o[:], bo)
                    else:
                        P.op("act", lambda e, o=o, pb=pb: e.activation(o[:], ps[pb][:, 0:T], AF.Copy),
                             reads=[psb[pb]], writes=[bo])
                    P.dma("sp", dstT[b, h, :, sl:sl + T], o[:], bo, reads=[bo])

        def tm_part(c):
            s = c % 2
            tok0 = c * T
            b, sl = tok0 // S, tok0 % S
            for tt in range(T // 128):
                sl2 = sl + tt * 128
                for kind, colbase in (("v", 1024), ("ir", 2048), ("f0", 2560), ("f1", 3072), ("gr", 3584)):
                    pb = 2 + ctr["tm"] % 3
                    ctr["tm"] += 1
                    for dc in range(DC):
                        P.op("pe", lambda e, pb=pb, dc=dc, colbase=colbase, s=s, tt=tt: e.matmul(
                            ps[pb][:, :], uT[s][:, dc, tt * 128:(tt + 1) * 128], w[:, dc, colbase:colbase + 512],
                            start=(dc == 0), stop=(dc == DC - 1)),
                            reads=[wg(colbase), b_uT[s]], writes=[psb[pb]])
                    if kind in ("v", "ir"):
                        dst = k.V_s if kind == "v" else k.ir_s
                        o, bo = tmbR.next()
                        P.op("act", lambda e, o=o, pb=pb: e.activation(o[:], ps[pb][:, :], AF.Copy),
                             reads=[psb[pb]], writes=[bo])
                        P.dma("sp", dst[b, sl2:sl2 + 128, :], o[:], bo, reads=[bo])
                    elif kind == "gr":
                        o, bo = tmbR.next()
                        silu_evac(ps[pb][:, :], psb[pb], o[:], bo)
                        P.dma("sp", k.gate_s[b, sl2:sl2 + 128, :], o[:], bo, reads=[bo])
                    else:
                        d = int(kind[1])
                        t, bt = tmpR.next()
                        P.op("act", lambda e, t=t, pb=pb: e.activation(t[:], ps[pb][:, :], AF.Exp, scale=-1.0),
                             reads=[psb[pb]], writes=[bt])
                        P.op("dve", lambda e, t=t: e.tensor_scalar(t[:], t[:], 1.0, None, ALU.add),
                             reads=[bt], writes=[bt])
                        P.op("dve", lambda e, t=t: e.reciprocal(t[:], t[:]), reads=[bt], writes=[bt])
                        P.op("pool", lambda e, t=t, d=d: e.tensor_tensor(t[:], t[:], oml_t[:, d, :], ALU.mult),
                             reads=[bt, b_oml], writes=[bt])
                        P.op("pool", lambda e, t=t, d=d: e.tensor_tensor(t[:], t[:], lb_t[:, d, :], ALU.add),
                             reads=[bt, b_lb], writes=[bt])
                        g, bg = tmfR.next()
                        P.op("act", lambda e, t=t, g=g: e.activation(g[:], t[:], AF.Ln), reads=[bt], writes=[bg])
                        ghi, bghi = tmbR.next()
                        glo, bglo = tmbR.next()
                        P.op("pool", lambda e, g=g, ghi=ghi: e.tensor_copy(ghi[:], g[:]), reads=[bg], writes=[bghi])
                        P.op("pool", lambda e, g=g, ghi=ghi, glo=glo: e.tensor_tensor(glo[:], g[:], ghi[:], ALU.subtract),
                             reads=[bg, bghi], writes=[bglo])
                        P.dma("sp", k.ghi_s[d][b, sl2:sl2 + 128, :], ghi[:], bghi, reads=[bghi])
                        P.dma("sp", k.glo_s[d][b, sl2:sl2 + 128, :], glo[:], bglo, reads=[bglo])
                        kb, bkb = tmbR.next()
                        P.op("pool", lambda e, t=t, kb=kb: e.tensor_scalar(kb[:], t[:], -1.0, 1.0, ALU.mult, ALU.add),
                             reads=[bt], writes=[bkb])
                        P.dma("sp", k.k_s[d][b, sl2:sl2 + 128, :], kb[:], bkb, reads=[bkb])
                        for h in range(NH):
                            P.op("pe", lambda e, kb=kb, h=h: e.transpose(
                                psT[:, h * 128:(h + 1) * 128], kb[:, h * 128:(h + 1) * 128], k.idb_t[:]),
                                reads=[bkb, k.cb], writes=[psb[5]])
                        o, bo = fmbR.next()
                        P.op("act", lambda e, o=o: e.activation(o[:], psT[:, 0:512], AF.Copy),
                             reads=[psb[5]], writes=[bo])
                        P.dma("sp", k.kT_s[d][b, :, :, sl2:sl2 + 128].rearrange("h p t -> p h t"),
                              o[:].rearrange("p (h t) -> p h t", h=NH), bo, reads=[bo])

        prologue(0)
        for c in range(NT):
            fm_part(c)
            if c + 1 < NT:
                prologue(c + 1)
            tm_part(c)
        P.barrier()


def attn_phase(k):
    nc, P = k.nc, k.P
    S = k.S
    NJ = S // 128
    NQC = S // 512
    SCALE = 64 ** -0.5
    ps, psb = k.ps, k.psb
    with ExitStack() as st:
        sb = lambda name, shape, dt: st.enter_context(nc.sbuf_tensor(f"sb_at_{name}", shape, dt))
        QT = [sb(f"QT{i}", [128, S], BF16) for i in range(2)]
        KT = [sb(f"KT{i}", [128, S], BF16) for i in range(2)]
        V = [sb(f"V{i}", [128, NJ, 132], BF16) for i in range(2)]
        b_in = bufs(2, "atin")
        b_vone = bufs(2, "vone")
        G = sb("G", [128, NH, 1152], F32)
        cfar = sb("cfar", [128, 2 * NH], F32)
        lamv = sb("lamv", [128, 4, 64], F32)
        lamt = sb("lamt", [128, 8], F32)
        ljunk = sb("ljunk", [128, 64], F32)
        ahn = sb("ahn", [128, 128], F32)
        b_G, b_cfar, b_lamv, b_lam, b_ahn = Buf("G"), Buf("cfar"), Buf("lamv"), Buf("lam"), Buf("ahn")
        PTR = Ring([sb(f"PT{i}", [128, 1024], BF16) for i in range(3)], "PT")
        TNR = Ring([sb(f"TN{i}", [128, 1024], F32) for i in range(2)], "TN")
        OsbR = Ring([sb(f"Osb{i}", [128, 3, 396], F32) for i in range(2)], "Osb")
        rrR = Ring([sb(f"rr{i}", [128, 2], F32) for i in range(4)], "rr")
        t0R = Ring([sb(f"t0{i}", [128, 128], F32) for i in range(4)], "t0")
        odR = Ring([sb(f"od{i}", [128, 128], F32) for i in range(4)], "od")
        jkR = Ring([sb(f"jk{i}", [128, 128], F32) for i in range(2)], "jk")
        ssR = Ring([sb(f"ss{i}", [128, 1], F32) for i in range(4)], "ss")
        oabR = Ring([sb(f"oab{i}", [128, 128], BF16) for i in range(4)], "oab")
        fmbR = Ring([sb(f"fmb{i}", [128, 512], BF16) for i in range(2)], "atfmb")
        psT = _ps_bf16(k, 7)

        for i in range(2):
            P.op("pool", lambda e, i=i: e.memset(V[i][:, :, 128:132], 1.0), writes=[b_vone[i]])
        P.dma("sp", G[:].rearrange("p h m -> p (h m)"), k.gbias, b_G, writes=[b_G])
        P.dma("sp", cfar[:], k.cfar, b_cfar, writes=[b_cfar])
        P.dma("sp", lamv[:].rearrange("p a f -> p (a f)"), k.lamv, b_lamv, writes=[b_lamv])
        P.dma("sp", ahn[:], k.hnorm[:, 0:128], b_ahn, writes=[b_ahn])
        for i in range(2):
            P.op("dve", lambda e, i=i: e.tensor_tensor(ljunk[:], lamv[:, 2 * i, :], lamv[:, 2 * i + 1, :], ALU.mult),
                 reads=[b_lamv], writes=[b_lam])
            P.op("dve", lambda e, i=i: e.reduce_sum(lamt[:, i:i + 1], ljunk[:], AX.X), reads=[b_lam], writes=[b_lam])
        P.op("act", lambda e: e.activation(lamt[:, 2:4], lamt[:, 0:2], AF.Exp), reads=[b_lam], writes=[b_lam])
        P.op("dve", lambda e: e.tensor_tensor(lamt[:, 4:5], lamt[:, 2:3], lamt[:, 3:4], ALU.subtract),
             reads=[b_lam], writes=[b_lam])
        P.op("dve", lambda e: e.tensor_scalar(lamt[:, 5:6], lamt[:, 4:5], 0.2, -1.0, ALU.add, ALU.mult),
             reads=[b_lam], writes=[b_lam])
        P.op("dve", lambda e: e.tensor_scalar(ahn[:], ahn[:], 0.8, None, ALU.mult), reads=[b_ahn], writes=[b_ahn])

        def load(idx):
            b, h = divmod(idx, NH)
            sl = idx % 2
            P.dma("sp", QT[sl][:], k.QT_s[b, h, :, :], b_in[sl], writes=[b_in[sl]])
            P.dma("sp", KT[sl][:], k.KT_s[b, h, :, :], b_in[sl], writes=[b_in[sl]])
            P.dma("sp", V[sl][:, :, 0:128], k.V_s[b, :, h * 128:(h + 1) * 128].rearrange("(j p) e -> p j e", p=128),
                  b_in[sl], writes=[b_in[sl]])

        def qk(sl, qc, j):
            p = j % 2
            for c in range(2):
                P.op("pe", lambda e, c=c, p=p: e.matmul(
                    ps[2 * p + c][:, :], KT[sl][c * 64:(c + 1) * 64, j * 128:(j + 1) * 128],
                    QT[sl][c * 64:(c + 1) * 64, qc * 512:(qc + 1) * 512], start=True, stop=True),
                    reads=[b_in[sl]], writes=[psb[2 * p + c]])

        def epilogue(b, h, qc):
            Osb, bO = OsbR.next()
            for bk in range(3):
                n = 396 if bk < 2 else 264
                P.op("dve", lambda e, bk=bk, n=n: e.tensor_copy(Osb[:, bk, 0:n], ps[4 + bk][:, 0:n]),
                     reads=[psb[4 + bk]], writes=[bO])
            for qt in range(4):
                def acc(c):
                    a = c * 4 + qt
                    return Osb[:, a // 3, (a % 3) * 132:(a % 3) * 132 + 129]
                O0, O1 = acc(0), acc(1)
                rr, brr = rrR.next()
                t0, bt0 = t0R.next()
                od, bod = odR.next()
                jk, bjk = jkR.next()
                ss, bss = ssR.next()
                oab, boab = oabR.next()
                P.op("dve", lambda e, rr=rr, O0=O0: e.reciprocal(rr[:, 0:1], O0[:, 128:129]), reads=[bO], writes=[brr])
                P.op("dve", lambda e, rr=rr, O1=O1: e.reciprocal(rr[:, 1:2], O1[:, 128:129]), reads=[bO], writes=[brr])
                P.op("dve", lambda e, rr=rr: e.tensor_tensor(rr[:, 1:2], rr[:, 1:2], lamt[:, 5:6], ALU.mult),
                     reads=[brr, b_lam], writes=[brr])
                P.op("dve", lambda e, rr=rr, t0=t0, O0=O0: e.tensor_scalar(t0[:], O0[:, 0:128], rr[:, 0:1], None, ALU.mult),
                     reads=[brr, bO], writes=[bt0])
                P.op("dve", lambda e, rr=rr, t0=t0, od=od, O1=O1: e.scalar_tensor_tensor(
                    od[:], O1[:, 0:128], rr[:, 1:2], t0[:], ALU.mult, ALU.add),
                    reads=[brr, bO, bt0], writes=[bod])
                P.op("act", lambda e, jk=jk, od=od, ss=ss: e.activation(jk[:], od[:], AF.Square, accum_out=ss[:]),
                     reads=[bod], writes=[bjk, bss])
                P.op("act", lambda e, ss=ss: e.activation(ss[:], ss[:], AF.Ln, bias=k.eps_t[:], scale=1.0 / 128),
                     reads=[bss, k.cb], writes=[bss])
                P.op("act", lambda e, ss=ss: e.activation(ss[:], ss[:], AF.Exp, scale=-0.5), reads=[bss], writes=[bss])
                P.op("dve", lambda e, oab=oab, od=od, ss=ss: e.scalar_tensor_tensor(
                    oab[:], od[:], ss[:], ahn[:], ALU.mult, ALU.mult),
                    reads=[bod, bss, b_ahn], writes=[boab])
                P.op("pe", lambda e, oab=oab, qt=qt: e.transpose(psT[:, qt * 128:(qt + 1) * 128], oab[:], k.idb_t[:]),
                     reads=[boab, k.cb], writes=[psb[7]])
            o, bo = fmbR.next()
            P.op("act", lambda e, o=o: e.activation(o[:], psT[:, 0:512], AF.Copy), reads=[psb[7]], writes=[bo])
            tok0 = b * S + qc * 512
            P.dma("sp", k.mixT_s[h, :, tok0:tok0 + 512], o[:], bo, reads=[bo])

        NBH = k.NB * NH
        PREFETCH = getattr(k, "attn_prefetch", True)
        if PREFETCH:
            load(0)
        for idx in range(NBH):
            b, h = divmod(idx, NH)
            sl = idx % 2
            if PREFETCH:
                if idx + 1 < NBH:
                    load(idx + 1)
            else:
                load(idx)
            for qc in range(NQC):
                qk(sl, qc, 0)
                for j in range(NJ):
                    if j + 1 < NJ:
                        qk(sl, qc, j + 1)
                    p = j % 2
                    d = j - 4 * qc
                    PT, bPT = PTR.next()
                    pair = k.psall[:, p * 1024:(p + 1) * 1024]
                    if -1 <= d <= 4:
                        TN, bTN = TNR.next()
                        g0 = 512 - 128 * d
                        for c in range(2):
                            P.op("dve", lambda e, TN=TN, c=c, p=p, g0=g0, h=h: e.scalar_tensor_tensor(
                                TN[:, c * 512:(c + 1) * 512], ps[2 * p + c][:, :], SCALE, G[:, h, g0:g0 + 512],
                                ALU.mult, ALU.add),
                                reads=[psb[2 * p + c], b_G], writes=[bTN])
                        P.op("act", lambda e, PT=PT, TN=TN: e.activation(PT[:], TN[:], AF.Exp),
                             reads=[bTN], writes=[bPT])
                    else:
                        side = 0 if d < 0 else 1
                        P.op("act", lambda e, PT=PT, pair=pair, side=side, h=h: e.activation(
                            PT[:], pair, AF.Exp, bias=cfar[:, 2 * h + side: 2 * h + side + 1], scale=SCALE),
                            reads=[psb[2 * p], psb[2 * p + 1], b_cfar], writes=[bPT])
                    for c in range(2):
                        for qt in range(4):
                            a = c * 4 + qt
                            bank, col = 4 + a // 3, (a % 3) * 132
                            P.op("pe", lambda e, PT=PT, c=c, qt=qt, bank=bank, col=col, j=j, a=a, sl=sl: e.matmul(
                                ps[bank][:, col:col + 129], PT[:, c * 512 + qt * 128: c * 512 + (qt + 1) * 128],
                                V[sl][:, j, 0:129], start=(j == 0 and a % 3 == 0), stop=(j == NJ - 1),
                                skip_group_check=True),
                                reads=[bPT, b_in[sl], b_vone[sl]], writes=[psb[bank]])
                epilogue(b, h, qc)
        P.barrier()


def outproj_phase(k):
    nc, P = k.nc, k.P
    T = 512
    NT = k.NTOK // T
    ps, psb = k.ps, k.psb
    wd = k.w["w_mix_out"]
    with ExitStack() as st:
        sb = lambda name, shape, dt: st.enter_context(nc.sbuf_tensor(f"sb_op_{name}", shape, dt))
        w = sb("w", [128, DC, D], BF16)
        mT = [sb(f"mT{i}", [128, DC, T], BF16) for i in range(2)]
        xT = [sb(f"xT{i}", [128, DC, T], F32) for i in range(2)]
        yT = sb("yT", [128, DC, T], F32)
        sq = sb("sq", [128, DC, T], BF16)
        rstd = sb("rstd", [128, T], F32)
        b_w = Buf("opw")
        b_mT = bufs(2, "opmT")
        b_xT = [bufs(DC, f"opxT{i}_") for i in range(2)]
        b_yT = bufs(DC, "opyT")
        b_sq = bufs(DC, "opsq")
        b_rstd = Buf("oprstd")
        for fc in range(DC):
            P.dma("pool", w[:, fc, :], wd[fc * 128:(fc + 1) * 128, :], b_w, writes=[b_w])

        def load(c):
            s = c % 2
            tok0 = c * T
            for fc in range(DC):
                P.dma("sp", mT[s][:, fc, :], k.mixT_s[fc, :, tok0:tok0 + T], b_mT[s], writes=[b_mT[s]])
            for dc in range(DC):
                P.dma("sp", xT[s][:, dc, :], k.h1T[dc, :, tok0:tok0 + T], b_xT[s][dc], writes=[b_xT[s][dc]])

        load(0)
        for c in range(NT):
            s = c % 2
            tok0 = c * T
            if c + 1 < NT:
                load(c + 1)
            for dc in range(DC):
                pb = dc % 2
                for fc in range(DC):
                    P.op("pe", lambda e, pb=pb, dc=dc, fc=fc, s=s: e.matmul(
                        ps[pb][:, 0:T], w[:, fc, dc * 128:(dc + 1) * 128], mT[s][:, fc, :],
                        start=(fc == 0), stop=(fc == DC - 1)),
                        reads=[b_w, b_mT[s]], writes=[psb[pb]])
                P.op("act", lambda e, pb=pb, dc=dc: e.activation(yT[:, dc, :], ps[pb][:, 0:T], AF.Copy),
                     reads=[psb[pb]], writes=[b_yT[dc]])
            rms_scale(k, yT, b_yT, sq, b_sq, rstd, b_rstd, 7, T, lnexp=True)
            for dc in range(DC):
                P.op("dve", lambda e, dc=dc: e.scalar_tensor_tensor(
                    yT[:, dc, :], yT[:, dc, :], k.nv_t[:, 3 * DC + dc: 3 * DC + dc + 1], rstd[:],
                    ALU.mult, ALU.mult),
                    reads=[b_yT[dc], b_rstd, k.cb], writes=[b_yT[dc]])
                P.op("pool", lambda e, dc=dc, s=s: e.tensor_tensor(
                    xT[s][:, dc, :], xT[s][:, dc, :], yT[:, dc, :], ALU.add),
                    reads=[b_yT[dc], b_xT[s][dc]], writes=[b_xT[s][dc]])
                P.dma("sp", k.h2T[dc, :, tok0:tok0 + T], xT[s][:, dc, :], b_xT[s][dc], reads=[b_xT[s][dc]])
        P.barrier()


def hgrn_phase(k):
    nc, P = k.nc, k.P
    S = k.S
    NCH = S // 128
    ps = k.ps
    with ExitStack() as st:
        sb = lambda name, shape, dt: st.enter_context(nc.sbuf_tensor(f"sb_hg_{name}", shape, dt))
        oacc = sb("oacc", [128, NCH, 512], F32)
        b_oacc = bufs(NCH, "oacc")
        hmat = sb("hmat", [128, 768], BF16)
        hmask = sb("hmask", [128, 256], F32)
        rhn = sb("rhn", [128, 128], F32)
        b_c = bufs(3, "hgc")
        Sf = sb("Sf", [128, NH, 128], F32)
        Sb = sb("Sb", [128, NH, 128], BF16)
        b_Sf = bufs(NH, "Sf")
        b_Sb = bufs(NH, "Sb")
        NIN = 3
        in_t = [dict(ghi=sb(f"ghi{i}", [128, 512], BF16), glo=sb(f"glo{i}", [128, 512], BF16),
                     ktm=sb(f"ktm{i}", [128, 512], BF16),
                     v=sb(f"v{i}", [128, 512], BF16), qT=sb(f"qT{i}", [128, NH, 128], BF16),
                     kT=sb(f"kT{i}", [128, NH, 128], BF16), gate=sb(f"gate{i}", [128, 512], BF16))
                for i in range(NIN)]
        b_inr = bufs(NIN, "hgin")
        EdR = Ring([sb(f"Ed{i}", [128, 512], F32) for i in range(2)], "Ed")
        kdecR = Ring([sb(f"kdec{i}", [128, 512], BF16) for i in range(2)], "kdec")
        EqiR = Ring([sb(f"Eqi{i}", [128, 256], F32) for i in range(8)], "Eqi")
        EkR = Ring([sb(f"Ek{i}", [128, 128], F32) for i in range(4)], "Ek")
        qinR = Ring([sb(f"qin{i}", [128, 128], BF16) for i in range(8)], "qin")
        qitR = Ring([sb(f"qit{i}", [128, 128], BF16) for i in range(8)], "qit")
        kinR = Ring([sb(f"kin{i}", [128, 128], BF16) for i in range(8)], "kin")
        ATmR = Ring([sb(f"ATm{i}", [128, 128], BF16) for i in range(8)], "ATm")
        osumR = Ring([sb(f"osum{i}", [128, 512], F32) for i in range(2)], "osum")
        jkR = Ring([sb(f"jk{i}", [128, 128], F32) for i in range(2)], "hjk")
        ss4R = Ring([sb(f"ss4{i}", [128, 4], F32) for i in range(2)], "ss4")
        onR = Ring([sb(f"on{i}", [128, 512], F32) for i in range(2)], "on")
        oabR = Ring([sb(f"oab{i}", [128, 512], BF16) for i in range(2)], "hoab")
        fmbR = Ring([sb(f"fmb{i}", [128, 512], BF16) for i in range(2)], "hfmb")
        psT = _ps_bf16(k, 6)
        b_gd = Buf("psgd")
        b_gg = bufs(2, "psgg")
        b_at = Buf("psat")
        b_o = Buf("pso")
        b_ds = Buf("psds")
        b_tr = Buf("pstr")

        P.dma("sp", hmat[:], k.hmat, b_c[0], writes=[b_c[0]])
        P.dma("sp", hmask[:], k.hmask, b_c[1], writes=[b_c[1]])
        P.dma("sp", rhn[:], k.hnorm[:, 128:256], b_c[2], writes=[b_c[2]])
        ictr = [0]
        import os
        HG_LEVEL = int(os.environ.get("HG_LEVEL", "5"))
        HG_SKIP = os.environ.get("HG_SKIP", "")

        def step(b, n, d, first, last):
            sl = n * 128
            r = ictr[0] % NIN
            ictr[0] += 1
            tl, bi = in_t[r], b_inr[r]
            P.dma("sp", tl["ghi"][:], k.ghi_s[d][b, sl:sl + 128, :], bi, writes=[bi])
            P.dma("sp", tl["glo"][:], k.glo_s[d][b, sl:sl + 128, :], bi, writes=[bi])
            P.dma("sp", tl["ktm"][:], k.k_s[d][b, sl:sl + 128, :], bi, writes=[bi])
            P.dma("sp", tl["v"][:], k.ir_s[b, sl:sl + 128, :], bi, writes=[bi])
            P.dma("sp", tl["qT"][:], k.qrT_s[b, :, :, sl:sl + 128].rearrange("h p t -> p h t"), bi, writes=[bi])
            P.dma("sp", tl["kT"][:], k.kT_s[d][b, :, :, sl:sl + 128].rearrange("h p t -> p h t"), bi, writes=[bi])
            if d == 1:
                P.dma("sp", tl["gate"][:], k.gate_s[b, sl:sl + 128, :], bi, writes=[bi])
            ghi, glo, ktm, v, qT, kT, gate = (tl["ghi"], tl["glo"], tl["ktm"], tl["v"], tl["qT"], tl["kT"],
                                              tl["gate"])
            M_d = hmat[:, d * 256:(d + 1) * 256]
            U_d = hmat[:, 512 + d * 128: 512 + (d + 1) * 128]
            mask_d = hmask[:, d * 128:(d + 1) * 128]
            if HG_LEVEL < 1:
                return
            P.op("pe", lambda e: e.matmul(ps[0][:, :], U_d, ghi[:], start=True, stop=False),
                 reads=[bi, b_c[0]], writes=[b_gd])
            P.op("pe", lambda e: e.matmul(ps[0][:, :], U_d, glo[:], start=False, stop=True),
                 reads=[bi, b_c[0]], writes=[b_gd])
            Ed, bEd = EdR.next()
            P.op("act", lambda e: e.activation(Ed[:], ps[0][:, :], AF.Exp), reads=[b_gd], writes=[bEd])
            kdec, bkd = kdecR.next()
            P.op("pool", lambda e: e.tensor_tensor(kdec[:], ktm[:], Ed[:], ALU.mult), reads=[bi, bEd], writes=[bkd])
            if HG_LEVEL < 2:
                return
            hs = []
            for pr in range(2):
                for h in (2 * pr, 2 * pr + 1):
                    gg = ps[1 + pr][:, (h % 2) * 256:(h % 2) * 256 + 256]
                    P.op("pe", lambda e, gg=gg, h=h: e.matmul(gg, ghi[:, h * 128:(h + 1) * 128], M_d, start=True, stop=False),
                         reads=[bi, b_c[0]], writes=[b_gg[pr]])
                    P.op("pe", lambda e, gg=gg, h=h: e.matmul(gg, glo[:, h * 128:(h + 1) * 128], M_d, start=False, stop=True),
                         reads=[bi, b_c[0]], writes=[b_gg[pr]])
            for h in range(NH):
                pr = h // 2
                gg = ps[1 + pr][:, (h % 2) * 256:(h % 2) * 256 + 256]
                Eqi, bEqi = EqiR.next()
                Ek, bEk = EkR.next()
                P.op("act", lambda e, gg=gg, Eqi=Eqi: e.activation(Eqi[:], gg, AF.Exp), reads=[b_gg[pr]], writes=[bEqi])
                P.op("act", lambda e, gg=gg, Ek=Ek: e.activation(Ek[:], gg[:, 0:128], AF.Exp, scale=-1.0),
                     reads=[b_gg[pr]], writes=[bEk])
                qin, bqin = qinR.next()
                qit, bqit = qitR.next()
                kin, bkin = kinR.next()
                P.op("dve", lambda e, qin=qin, Eqi=Eqi, h=h: e.tensor_tensor(qin[:], qT[:, h, :], Eqi[:, 0:128], ALU.mult),
                     reads=[bi, bEqi], writes=[bqin])
                P.op("dve", lambda e, qit=qit, Eqi=Eqi, h=h: e.tensor_tensor(qit[:], qT[:, h, :], Eqi[:, 128:256], ALU.mult),
                     reads=[bi, bEqi], writes=[bqit])
                P.op("pool", lambda e, kin=kin, Ek=Ek, h=h: e.tensor_tensor(kin[:], kT[:, h, :], Ek[:], ALU.mult),
                     reads=[bi, bEk], writes=[bkin])
                hs.append((Eqi, bEqi, qin, bqin, qit, bqit, kin, bkin))
            if HG_LEVEL < 3:
                return
            ats = []
            for h in range(NH):
                Eqi, bEqi, qin, bqin, qit, bqit, kin, bkin = hs[h]
                at = ps[3][:, h * 128:(h + 1) * 128]
                P.op("pe", lambda e, at=at, kin=kin, qin=qin: e.matmul(at, kin[:], qin[:], start=True, stop=True),
                     reads=[bkin, bqin], writes=[b_at])
            for h in range(NH):
                at = ps[3][:, h * 128:(h + 1) * 128]
                ATm, bATm = ATmR.next()
                P.op("dve", lambda e, at=at, ATm=ATm: e.tensor_tensor(ATm[:], at, mask_d, ALU.mult),
                     reads=[b_at, b_c[1]], writes=[bATm])
                ats.append((ATm, bATm))
            if HG_LEVEL < 4:
                return
            for h in range(NH):
                Eqi, bEqi, qin, bqin, qit, bqit, kin, bkin = hs[h]
                ATm, bATm = ats[h]
                hsl = slice(h * 128, (h + 1) * 128)
                P.op("pe", lambda e, ATm=ATm, hsl=hsl: e.matmul(ps[4][:, hsl], ATm[:], v[:, hsl], start=True, stop=first),
                     reads=[bATm, bi], writes=[b_o])
                if not first:
                    P.op("pe", lambda e, qit=qit, hsl=hsl, h=h: e.matmul(ps[4][:, hsl], qit[:], Sb[:, h, :],
                                                                   start=False, stop=True),
                         reads=[bqit, b_Sb[h]], writes=[b_o])
            if not last:
                for h in range(NH):
                    hsl = slice(h * 128, (h + 1) * 128)
                    P.op("pe", lambda e, hsl=hsl: e.matmul(ps[5][:, hsl], kdec[:, hsl], v[:, hsl], start=True, stop=True),
                         reads=[bkd, bi], writes=[b_ds])
                for h in range(NH):
                    Eqi, bEqi = hs[h][0], hs[h][1]
                    hsl = slice(h * 128, (h + 1) * 128)
                    if first:
                        P.op("dve", lambda e, hsl=hsl, h=h: e.tensor_copy(Sf[:, h, :], ps[5][:, hsl]),
                             reads=[b_ds], writes=[b_Sf[h]])
                    else:
                        dc_ = 255 if d == 0 else 128
                        P.op("dve", lambda e, hsl=hsl, h=h, Eqi=Eqi, dc_=dc_: e.scalar_tensor_tensor(
                            Sf[:, h, :], Sf[:, h, :], Eqi[:, dc_:dc_ + 1], ps[5][:, hsl], ALU.mult, ALU.add),
                            reads=[b_Sf[h], bEqi, b_ds], writes=[b_Sf[h]])
                    P.op("act", lambda e, h=h: e.activation(Sb[:, h, :], Sf[:, h, :], AF.Copy),
                         reads=[b_Sf[h]], writes=[b_Sb[h]])
            if HG_LEVEL < 5:
                return
            if d == 0:
                P.op("act", lambda e: e.activation(oacc[:, n, :], ps[4][:, :], AF.Copy), reads=[b_o], writes=[b_oacc[n]])
                return
            osum, bos = osumR.next()
            P.op("dve", lambda e: e.tensor_tensor(osum[:], oacc[:, n, :], ps[4][:, :], ALU.add),
                 reads=[b_o, b_oacc[n]], writes=[bos])
            ss4, bss = ss4R.next()
            jk, bjk = jkR.next()
            for h in range(NH):
                P.op("act", lambda e, h=h: e.activation(jk[:], osum[:, h * 128:(h + 1) * 128], AF.Square,
                                                        accum_out=ss4[:, h:h + 1]),
                     reads=[bos], writes=[bjk, bss])
            P.op("act", lambda e: e.activation(ss4[:], ss4[:], AF.Ln, bias=k.eps_t[:], scale=1.0 / 128),
                 reads=[bss, k.cb], writes=[bss])
            P.op("act", lambda e: e.activation(ss4[:], ss4[:], AF.Exp, scale=-0.5), reads=[bss], writes=[bss])
            on, bon = onR.next()
            for h in range(NH):
                P.op("dve", lambda e, h=h: e.scalar_tensor_tensor(
                    on[:, h * 128:(h + 1) * 128], osum[:, h * 128:(h + 1) * 128], ss4[:, h:h + 1], rhn[:],
                    ALU.mult, ALU.mult), reads=[bos, bss, b_c[2]], writes=[bon])
            oab, boab = oabR.next()
            P.op("pool", lambda e: e.tensor_tensor(oab[:], on[:], gate[:], ALU.mult), reads=[bon, bi], writes=[boab])
            for h in range(NH):
                P.op("pe", lambda e, h=h: e.transpose(psT[:, h * 128:(h + 1) * 128], oab[:, h * 128:(h + 1) * 128],
                                                      k.idb_t[:]),
                     reads=[boab, k.cb], writes=[b_tr])
            o, bo = fmbR.next()
            P.op("act", lambda e: e.activation(o[:], psT[:, 0:512], AF.Copy), reads=[b_tr], writes=[bo])
            tok0 = b * S + sl
            P.dma("sp", k.mixT_s[4:8, :, tok0:tok0 + 128].rearrange("h p t -> p h t"),
                  o[:].rearrange("p (h t) -> p h t", h=NH), bo, reads=[bo])

        for b in range(k.NB):
            for n in range(NCH):
                step(b, n, 0, n == 0, n == NCH - 1)
            for n in range(NCH - 1, -1, -1):
                step(b, n, 1, n == NCH - 1, n == 0)
        P.barrier()


_NC_CACHE = {}


def _consts():
    import ml_dtypes
    bf = ml_dtypes.bfloat16
    return {
        "ident_f": np.eye(128, dtype=np.float32),
        "ident_b": np.eye(128, dtype=np.float32).astype(bf),
        "ones_b": np.ones((128, 128), dtype=np.float32).astype(bf),
    }


def _pack_normvecs(vecs):
    a = np.stack([np.asarray(v, np.float32).reshape(DC, 128) for v in vecs], 0)
    return np.ascontiguousarray(a.transpose(2, 0, 1).reshape(128, 7 * DC))


def _rel_bucket_np(rel):
    nb, max_exact = 16, 8
    side = np.where(rel > 0, nb, 0)
    n = np.abs(rel)
    nf = np.maximum(n, 1).astype(np.float32)
    large = max_exact + (np.log(nf / np.float32(max_exact)) / np.float32(math.log(128 / max_exact))
                         * np.float32(nb - max_exact)).astype(np.int32)
    large = np.minimum(large, nb - 1)
    return side + np.where(n < max_exact, n, large)


def _hgrn_consts():
    s_ = np.arange(128)[:, None]
    t_ = np.arange(128)[None, :]
    f = np.float32
    Lf = (s_ <= t_).astype(f)
    Mqf = Lf - (s_ <= 63).astype(f)
    Lb = (s_ >= t_).astype(f)
    Mqb = Lb - (s_ >= 64).astype(f)
    Uf = (s_ > t_).astype(f)
    Ub = (s_ < t_).astype(f)
    hmat = np.concatenate([Mqf, Lf, Mqb, Lb, Uf, Ub], axis=1)
    hmask = np.concatenate([Lf, Lb], axis=1)
    import ml_dtypes
    return np.ascontiguousarray(hmat).astype(ml_dtypes.bfloat16), np.ascontiguousarray(hmask)


def _rep(v, n=128):
    v = np.asarray(v, np.float32).reshape(1, -1)
    return np.ascontiguousarray(np.repeat(v, n, axis=0))


def run(inputs, NB, S, n_cores, stage="full", debug=False):
    key = (NB, S, stage, debug)
    if key not in _NC_CACHE:
        _NC_CACHE[key] = build(NB, S, stage, debug)
    nc = _NC_CACHE[key]
    x = np.asarray(inputs["x"], np.float32)
    xs = x.reshape(n_cores, NB * S, D)
    shared = dict(_consts())
    for nm in ("ffn1_w_in", "ffn1_w_out", "ffn2_w_in", "ffn2_w_out", "w_mix_in", "w_mix_out"):
        shared[nm] = np.ascontiguousarray(np.asarray(inputs[nm], np.float32)[0])
    shared["normvecs"] = _pack_normvecs([
        inputs["ffn1_pre_norm"][0], inputs["ffn1_post_norm"][0], inputs["mix_pre_norm"][0],
        inputs["mix_post_norm"][0], inputs["mix_post_norm"][0], inputs["ffn2_pre_norm"][0],
        inputs["ffn2_post_norm"][0]])
    rb = np.asarray(inputs["rel_bias"], np.float32)
    kl = np.arange(128)[:, None]
    m = np.arange(1152)[None, :]
    bidx = _rel_bucket_np(kl - (m - 512))
    gb = rb[bidx]
    shared["gbias"] = np.ascontiguousarray(gb.transpose(0, 2, 1).reshape(128, NH * 1152))
    shared["cfar"] = _rep(np.stack([rb[15], rb[31]], axis=1).reshape(-1))
    shared["lamv"] = _rep(np.concatenate([np.asarray(inputs[n_], np.float32)[0] for n_ in
                                          ("lambda_q1", "lambda_k1", "lambda_q2", "lambda_k2")]))
    shared["lbl"] = _rep(np.asarray(inputs["lb_logits"], np.float32).reshape(-1))
    shared["hnorm"] = _rep(np.concatenate([np.asarray(inputs["attn_head_norm"], np.float32)[0],
                                           np.asarray(inputs["rnn_head_norm"], np.float32)[0]]))
    shared["hmat"], shared["hmask"] = _hgrn_consts()
    in_maps = []
    for c in range(n_cores):
        m_ = dict(shared)
        m_["x"] = np.ascontiguousarray(xs[c])
        in_maps.append(m_)
    res = run_bass_kernel_spmd(nc, in_maps, core_ids=list(range(n_cores)))
    outs = [np.asarray(r["out"], np.float32).reshape(NB, S, D) for r in res.results]
    if debug:
        return np.concatenate(outs, 0), res.results
    return np.concatenate(outs, 0)


def kernel(**inputs):
    return run(inputs, 2, 4096, N_CORES, "full")
```

```python
import math
from contextlib import ExitStack

import numpy as np
import concourse.bass as bass
import concourse.mybir as mybir
from concourse.bass_utils import run_bass_kernel_spmd

F32 = mybir.dt.float32
BF16 = mybir.dt.bfloat16
AF = mybir.ActivationFunctionType
ALU = mybir.AluOpType
AX = mybir.AxisListType

D = 1024
DC = 8
DFF = 2816
FC = 22
NH = 4
EPS = 1e-6
N_CORES = 8
ENGS = ("pe", "act", "dve", "pool", "sp")
EPOCH = 30000


class Buf:
    __slots__ = ("name", "lw", "rd", "chan")

    def __init__(self, name=""):
        self.name = name
        self.lw = None
        self.rd = {}
        self.chan = None


def bufs(n, name=""):
    return [Buf(f"{name}{i}") for i in range(n)]


class Chan:
    __slots__ = ("sem", "count", "id")
    _n = 0

    def __init__(self):
        self.sem = None
        self.count = 0
        Chan._n += 1
        self.id = Chan._n


class Prog:
    def __init__(self, nc):
        self.nc = nc
        self.ops = {e: [] for e in ENGS}
        self.known = {e: {} for e in ENGS}
        self.chans = []
        self.free_chans = []
        self.live_chans = []

    def _need(self, eng, tok, kind, waits, cur_chan=None):
        if tok is None:
            return
        if tok[0] == "d" and kind == "waw" and tok[1] is cur_chan:
            return
        if tok[0] == "e":
            _, f, i = tok
            if f == eng:
                if eng == "pe" or kind != "raw":
                    return
            key = ("e", f)
            val = i
        else:
            _, ch, v = tok
            key = ("d", ch.id)
            val = v
        kn = self.known[eng]
        if kn.get(key, -1) >= val:
            return
        kn[key] = val
        waits.append(tok)
        if tok[0] == "e":
            self.ops[tok[1]][tok[2]]["inc"] = True

    def _mk(self, eng, fn, reads, writes, dma_chan=None):
        idx = len(self.ops[eng])
        waits = []
        for b in reads:
            self._need(eng, b.lw, "raw", waits)
        for b in writes:
            self._need(eng, b.lw, "waw", waits, dma_chan)
            for t in b.rd.values():
                self._need(eng, t, "war", waits)
        rec = dict(fn=fn, waits=waits, inc=False, chan=dma_chan, val=None)
        self.ops[eng].append(rec)
        if dma_chan is None:
            tok = ("e", eng, idx)
        else:
            dma_chan.count += 16
            tok = ("d", dma_chan, dma_chan.count)
        for b in writes:
            b.lw = tok
            b.rd = {}
        for b in reads:
            k = ("e", tok[1]) if tok[0] == "e" else ("d", tok[1].id)
            b.rd[k] = tok
        return tok

    def op(self, eng, fn, reads=(), writes=()):
        return self._mk(eng, fn, list(reads), list(writes))

    def dma(self, q, out_ap, in_ap, sb, reads=(), writes=()):
        if sb.chan is None:
            if self.free_chans:
                sb.chan = self.free_chans.pop()
            else:
                sb.chan = Chan()
                self.chans.append(sb.chan)
            self.live_chans.append(sb.chan)
        return self._mk(q, lambda e: e.dma_start(out=out_ap, in_=in_ap),
                        list(reads), list(writes), dma_chan=sb.chan)

    def barrier(self):
        bb = Buf("barrier")
        waits = []
        for e in ENGS:
            if e != "sp" and self.ops[e]:
                self._need("sp", ("e", e, len(self.ops[e]) - 1), "raw", waits)
        for ch in self.chans:
            if ch.count:
                self._need("sp", ("d", ch, ch.count), "raw", waits)
        idx = len(self.ops["sp"])
        self.ops["sp"].append(dict(fn=lambda e: e.nop(), waits=waits, inc=False, chan=None, val=None))
        tok = ("e", "sp", idx)
        bb.lw = tok
        for e in ENGS:
            if e != "sp":
                self._mk(e, lambda en: en.nop(), [bb], [])
        self.free_chans.extend(self.live_chans)
        self.live_chans = []

    def emit(self, stack):
        nc = self.nc
        esems = {}
        for e in ENGS:
            n = 0
            for rec in self.ops[e]:
                if rec["inc"]:
                    rec["val"] = (n // EPOCH, n % EPOCH + 1)
                    n += 1
            nep = (n + EPOCH - 1) // EPOCH
            esems[e] = [stack.enter_context(nc.semaphore(f"s_{e}{k}")) for k in range(max(nep, 1))]
        for ch in self.chans:
            assert ch.count < 60000, ch.count
            ch.sem = stack.enter_context(nc.semaphore(f"s_ch{ch.id}"))
        ops = self.ops

        def run(eng_name):
            def body(eng):
                for rec in ops[eng_name]:
                    for tok in rec["waits"]:
                        if tok[0] == "e":
                            ep, v = ops[tok[1]][tok[2]]["val"]
                            eng.wait_ge(esems[tok[1]][ep], v)
                        else:
                            eng.wait_ge(tok[1].sem, tok[2])
                    ins = rec["fn"](eng)
                    if rec["chan"] is not None:
                        ins.then_inc(rec["chan"].sem, 16)
                    elif rec["inc"]:
                        ins.then_inc(esems[eng_name][rec["val"][0]], 1)
            return body

        with nc.Block() as block:
            block.sync(run("sp"))
            block.scalar(run("act"))
            block.vector(run("dve"))
            block.gpsimd(run("pool"))
            block.tensor(run("pe"))


class K:
    pass


def build(NB, S, stage="full", debug=False):
    nc = bass.Bass("TRN2", target_bir_lowering=False)
    NTOK = NB * S
    P = Prog(nc)
    k = K()
    k.nc, k.P, k.NB, k.S, k.NTOK = nc, P, NB, S, NTOK
    import os
    k.attn_prefetch = os.environ.get('ATTN_PREFETCH', '1') == '1'

    def din(name, shape, dt=F32):
        return nc.dram_tensor(name, list(shape), dt, kind="ExternalInput").ap()

    def dscr(name, shape, dt):
        return nc.dram_tensor(name, list(shape), dt, kind="ExternalOutput" if debug else "Internal").ap()

    k.x = din("x", [NTOK, D])
    k.out = nc.dram_tensor("out", [NTOK, D], F32, kind="ExternalOutput").ap()
    k.w = {}
    for nm, shp in (("ffn1_w_in", [D, 2 * DFF]), ("ffn1_w_out", [DFF, D]),
                    ("ffn2_w_in", [D, 2 * DFF]), ("ffn2_w_out", [DFF, D]),
                    ("w_mix_in", [D, 4096]), ("w_mix_out", [D, D])):
        k.w[nm] = din(nm, shp)
    k.nv = din("normvecs", [128, 7 * DC])
    k.ident_f = din("ident_f", [128, 128])
    k.ident_b = din("ident_b", [128, 128], BF16)
    k.ones_b = din("ones_b", [128, 128], BF16)
    k.gbias = din("gbias", [128, NH * 1152])
    k.cfar = din("cfar", [128, 2 * NH])
    k.lamv = din("lamv", [128, 4 * 64])
    k.lbl = din("lbl", [128, 2 * 2 * 512])
    k.hnorm = din("hnorm", [128, 2 * 128])
    k.hmat = din("hmat", [128, 2 * 256 + 2 * 128], BF16)
    k.hmask = din("hmask", [128, 2 * 128])
    k.h1T = dscr("h1T", [DC, 128, NTOK], F32)
    k.h2T = dscr("h2T", [DC, 128, NTOK], F32)
    k.QT_s = dscr("QT_s", [NB, NH, 128, S], BF16)
    k.KT_s = dscr("KT_s", [NB, NH, 128, S], BF16)
    k.V_s = dscr("V_s", [NB, S, 512], BF16)
    k.qrT_s = dscr("qrT_s", [NB, NH, 128, S], BF16)
    k.kT_s = [dscr(f"kT_s{d}", [NB, NH, 128, S], BF16) for d in range(2)]
    k.k_s = [dscr(f"k_s{d}", [NB, S, 512], BF16) for d in range(2)]
    k.ghi_s = [dscr(f"ghi_s{d}", [NB, S, 512], BF16) for d in range(2)]
    k.glo_s = [dscr(f"glo_s{d}", [NB, S, 512], BF16) for d in range(2)]
    k.ir_s = dscr("ir_s", [NB, S, 512], BF16)
    k.gate_s = dscr("gate_s", [NB, S, 512], BF16)
    k.mixT_s = dscr("mixT_s", [DC, 128, NTOK], BF16)

    with ExitStack() as gs:
        psall = gs.enter_context(nc.psum_tensor("psall", [128, 4096], F32))
        k.psall = psall
        k.ps = [psall[:, i * 512:(i + 1) * 512] for i in range(8)]
        k.psb = bufs(8, "psb")
        k.nv_t = gs.enter_context(nc.sbuf_tensor("nv_t", [128, 7 * DC], F32))
        k.nvh_t = gs.enter_context(nc.sbuf_tensor("nvh_t", [128, 7 * DC], F32))
        k.idf_t = gs.enter_context(nc.sbuf_tensor("idf_t", [128, 128], F32))
        k.idb_t = gs.enter_context(nc.sbuf_tensor("idb_t", [128, 128], BF16))
        k.ones_t = gs.enter_context(nc.sbuf_tensor("ones_t", [128, 128], BF16))
        k.eps_t = gs.enter_context(nc.sbuf_tensor("eps_t", [128, 1], F32))
        k.cb = Buf("consts")
        cb2 = bufs(4, "cld")
        P.dma("sp", k.nv_t[:], k.nv, cb2[0], writes=[cb2[0]])
        P.dma("sp", k.idf_t[:], k.ident_f, cb2[1], writes=[cb2[1]])
        P.dma("sp", k.idb_t[:], k.ident_b, cb2[2], writes=[cb2[2]])
        P.dma("sp", k.ones_t[:], k.ones_b, cb2[3], writes=[cb2[3]])
        P.op("dve", lambda e: e.memset(k.eps_t[:], EPS), writes=[k.cb])
        P.op("dve", lambda e: e.tensor_scalar(k.nvh_t[:], k.nv_t[:], 0.5, None, ALU.mult),
             reads=cb2, writes=[k.cb])

        if stage == "ffn1":
            ffn_phase(k, "ffn1", ("tm", k.x), ("tm", k.out), 0, 1)
        elif stage.startswith("mix"):
            parts = stage.split(":")[1] if ":" in stage else "paho"
            ffn_load_only(k, ("tm", k.x), ("fm", k.h1T))
            if "p" in parts:
                proj_phase(k)
            if "a" in parts:
                attn_phase(k)
            if "h" in parts:
                hgrn_phase(k)
            if "o" in parts:
                outproj_phase(k)
            ffn_load_only(k, ("fm", k.h2T if "o" in parts else k.h1T), ("tm", k.out))
        else:
            ffn_phase(k, "ffn1", ("tm", k.x), ("fm", k.h1T), 0, 1)
            proj_phase(k)
            attn_phase(k)
            hgrn_phase(k)
            outproj_phase(k)
            ffn_phase(k, "ffn2", ("fm", k.h2T), ("tm", k.out), 5, 6)

        P.barrier()
        P.emit(gs)
    return nc


def ffn_phase(k, wname, src, dst, iv_pre, iv_post):
    nc, P = k.nc, k.P
    T = 256
    NT = k.NTOK // T
    TT = T // 128
    passthru = wname is None
    if passthru:
        k.pt_ctr = getattr(k, "pt_ctr", 0) + 1
        wname = f"pt{k.pt_ctr}"
    else:
        w_in_d = k.w[wname + "_w_in"]
        w_out_d = k.w[wname + "_w_out"]
    with ExitStack() as st:
        sb = lambda name, shape, dt: st.enter_context(nc.sbuf_tensor(f"sb_{wname}_{name}", shape, dt))
        w_in = sb("w_in", [128, DC, 2 * DFF if not passthru else 2], BF16)
        w_out = sb("w_out", [128, FC, D if not passthru else 2], BF16)
        xtm = [sb(f"xtm{i}", [128, D], F32) for i in range(3)]
        xT = [sb(f"xT{i}", [128, DC, T], F32) for i in range(2)]
        uT = [sb(f"uT{i}", [128, DC, T], BF16) for i in range(2)]
        sq = sb("sq", [128, DC, T], BF16)
        hid = sb("hid", [128, FC, T], BF16)
        yT = sb("yT", [128, DC, T], F32)
        sg = [sb(f"sg{i}", [128, T], BF16) for i in range(2)]
        rstd = [sb(f"rstd{i}", [128, T], F32) for i in range(2)]
        b_win = bufs(4, "win")
        b_wout = bufs(2, "wout")
        wgrp = lambda fc: min(fc // 6, 3)
        wogrp = lambda fc: 0 if fc < 11 else 1
        b_xtm = bufs(3, "xtm")
        b_xT = [bufs(DC, f"xT{i}_") for i in range(2)]
        b_uT = [bufs(DC, f"uT{i}_") for i in range(2)]
        b_sq = bufs(DC, "sq")
        b_hid = bufs(FC, "hid")
        b_yT = bufs(DC, "yT")
        b_sg = bufs(2, "sg")
        b_rstd = bufs(2, "rstd")
        ps, psb = k.ps, k.psb
        PS_G, PS_U, PS_Y, PS_SS, PS_TR = (0, 1), (2, 3), (4, 5), 6, 7

        for g in range(4 if not passthru else 0):
            c0, c1 = g * 6 * 128, min((g + 1) * 6, FC) * 128
            for half in (0, DFF):
                for dc in range(DC):
                    P.dma("pool", w_in[:, dc, half + c0: half + c1],
                          w_in_d[dc * 128:(dc + 1) * 128, half + c0: half + c1], b_win[g], writes=[b_win[g]])
        for fc in range(FC if not passthru else 0):
            P.dma("pool", w_out[:, fc, :], w_out_d[fc * 128:(fc + 1) * 128, :], b_wout[wogrp(fc)],
                  writes=[b_wout[wogrp(fc)]])

        xt_ctr = [0]

        def prologue(c):
            s = c % 2
            tok0 = c * T
            if src[0] == "tm":
                for tt in range(TT):
                    r = xt_ctr[0] % 3
                    xt_ctr[0] += 1
                    P.dma("sp", xtm[r][:], src[1][tok0 + tt * 128: tok0 + (tt + 1) * 128, :], b_xtm[r],
                          writes=[b_xtm[r]])
                    for g in range(2):
                        for j in range(4):
                            dc = g * 4 + j
                            P.op("pe", lambda e, r=r, dc=dc, j=j: e.transpose(
                                ps[PS_TR][:, j * 128:(j + 1) * 128], xtm[r][:, dc * 128:(dc + 1) * 128], k.idf_t[:]),
                                reads=[b_xtm[r], k.cb], writes=[psb[PS_TR]])
                        P.op("act", lambda e, s=s, g=g, tt=tt: e.activation(
                            xT[s][:, g * 4:(g + 1) * 4, tt * 128:(tt + 1) * 128],
                            ps[PS_TR][:].rearrange("p (j t) -> p j t", j=4), AF.Copy),
                            reads=[psb[PS_TR]], writes=b_xT[s][g * 4:(g + 1) * 4])
            else:
                for dc in range(DC):
                    P.dma("sp", xT[s][:, dc, :], src[1][dc, :, tok0:tok0 + T], b_xT[s][dc], writes=[b_xT[s][dc]])
            if passthru:
                return
            rms_scale(k, xT[s], b_xT[s], sq, b_sq, rstd[s], b_rstd[s], PS_SS, T)
            for dc in range(DC):
                P.op("dve", lambda e, s=s, dc=dc: e.scalar_tensor_tensor(
                    uT[s][:, dc, :], xT[s][:, dc, :], k.nv_t[:, iv_pre * DC + dc: iv_pre * DC + dc + 1], rstd[s][:],
                    ALU.mult, ALU.mult),
                    reads=[b_xT[s][dc], b_rstd[s], k.cb], writes=[b_uT[s][dc]])

        def gate_up(c):
            s = c % 2
            for fc in range(FC):
                pg, pu = PS_G[fc % 2], PS_U[fc % 2]
                for dc in range(DC):
                    P.op("pe", lambda e, pg=pg, dc=dc, fc=fc, s=s: e.matmul(
                        ps[pg][:, 0:T], w_in[:, dc, fc * 128:(fc + 1) * 128], uT[s][:, dc, :],
                        start=(dc == 0), stop=(dc == DC - 1)),
                        reads=[b_win[wgrp(fc)], b_uT[s][dc]], writes=[psb[pg]])
                for dc in range(DC):
                    P.op("pe", lambda e, pu=pu, dc=dc, fc=fc, s=s: e.matmul(
                        ps[pu][:, 0:T], w_in[:, dc, DFF + fc * 128: DFF + (fc + 1) * 128], uT[s][:, dc, :],
                        start=(dc == 0), stop=(dc == DC - 1)),
                        reads=[b_win[wgrp(fc)], b_uT[s][dc]], writes=[psb[pu]])
                P.op("act", lambda e, pg=pg, fc=fc: e.activation(sg[fc % 2][:], ps[pg][:, 0:T], AF.Silu),
                     reads=[psb[pg]], writes=[b_sg[fc % 2]])
                P.op("dve", lambda e, pu=pu, fc=fc: e.tensor_tensor(
                    hid[:, fc, :], sg[fc % 2][:], ps[pu][:, 0:T], ALU.mult),
                    reads=[b_sg[fc % 2], psb[pu]], writes=[b_hid[fc]])

        def down(c):
            for dc in range(DC):
                py = PS_Y[dc % 2]
                for fc in range(FC):
                    P.op("pe", lambda e, py=py, dc=dc, fc=fc: e.matmul(
                        ps[py][:, 0:T], w_out[:, fc, dc * 128:(dc + 1) * 128], hid[:, fc, :],
                        start=(fc == 0), stop=(fc == FC - 1)),
                        reads=[b_wout[wogrp(fc)], b_hid[fc]], writes=[psb[py]])
                P.op("act", lambda e, py=py, dc=dc: e.activation(yT[:, dc, :], ps[py][:, 0:T], AF.Copy),
                     reads=[psb[py]], writes=[b_yT[dc]])

        def epilogue(c):
            s = c % 2
            tok0 = c * T
            r2, b_r2 = rstd[s], b_rstd[s]
            if not passthru:
                rms_scale(k, yT, b_yT, sq, b_sq, r2, b_r2, PS_SS, T)
            for dc in range(DC if not passthru else 0):
                P.op("dve", lambda e, dc=dc: e.scalar_tensor_tensor(
                    yT[:, dc, :], yT[:, dc, :], k.nvh_t[:, iv_post * DC + dc: iv_post * DC + dc + 1], r2[:],
                    ALU.mult, ALU.mult),
                    reads=[b_yT[dc], b_r2, k.cb], writes=[b_yT[dc]])
                P.op("pool", lambda e, dc=dc, s=s: e.tensor_tensor(
                    xT[s][:, dc, :], xT[s][:, dc, :], yT[:, dc, :], ALU.add),
                    reads=[b_yT[dc], b_xT[s][dc]], writes=[b_xT[s][dc]])
            if dst[0] == "fm":
                for dc in range(DC):
                    P.dma("sp", dst[1][dc, :, tok0:tok0 + T], xT[s][:, dc, :], b_xT[s][dc], reads=[b_xT[s][dc]])
            else:
                for tt in range(TT):
                    r = xt_ctr[0] % 3
                    xt_ctr[0] += 1
                    for g in range(2):
                        for j in range(4):
                            dc = g * 4 + j
                            P.op("pe", lambda e, s=s, dc=dc, j=j, tt=tt: e.transpose(
                                ps[PS_TR][:, j * 128:(j + 1) * 128], xT[s][:, dc, tt * 128:(tt + 1) * 128], k.idf_t[:]),
                                reads=[b_xT[s][dc], k.cb], writes=[psb[PS_TR]])
                        P.op("act", lambda e, r=r, g=g: e.activation(
                            xtm[r][:, g * 512:(g + 1) * 512], ps[PS_TR][:], AF.Copy),
                            reads=[psb[PS_TR]], writes=[b_xtm[r]])
                    P.dma("sp", dst[1][tok0 + tt * 128: tok0 + (tt + 1) * 128, :], xtm[r][:], b_xtm[r],
                          reads=[b_xtm[r]])

        prologue(0)
        for c in range(NT):
            if not passthru:
                gate_up(c)
            if c + 1 < NT:
                prologue(c + 1)
            if not passthru:
                down(c)
            epilogue(c)
        P.barrier()


def ffn_load_only(k, src, dst):
    ffn_phase(k, None, src, dst, None, None)


def rms_scale(k, src, b_src, sq, b_sq, rstd, b_rstd, ps_i, T, lnexp=False):
    P = k.P
    ps, psb = k.ps, k.psb
    for dc in range(DC):
        P.op("act", lambda e, dc=dc: e.activation(sq[:, dc, :], src[:, dc, :], AF.Square),
             reads=[b_src[dc]], writes=[b_sq[dc]])
    for dc in range(DC):
        P.op("pe", lambda e, dc=dc: e.matmul(ps[ps_i][:, 0:T], k.ones_t[:], sq[:, dc, :],
                                              start=(dc == 0), stop=(dc == DC - 1)),
             reads=[b_sq[dc], k.cb], writes=[psb[ps_i]])
    if lnexp:
        P.op("act", lambda e: e.activation(rstd[:], ps[ps_i][:, 0:T], AF.Ln, bias=k.eps_t[:], scale=1.0 / D),
             reads=[psb[ps_i], k.cb], writes=[b_rstd])
        P.op("act", lambda e: e.activation(rstd[:], rstd[:], AF.Exp, scale=-0.5),
             reads=[b_rstd], writes=[b_rstd])
        return
    P.op("act", lambda e: e.activation(rstd[:], ps[ps_i][:, 0:T], AF.Sqrt, bias=k.eps_t[:], scale=1.0 / D),
         reads=[psb[ps_i], k.cb], writes=[b_rstd])
    P.op("dve", lambda e: e.reciprocal(rstd[:], rstd[:]),
         reads=[b_rstd], writes=[b_rstd])


class Ring:
    def __init__(self, tiles, name):
        self.t = tiles
        self.b = bufs(len(tiles), name)
        self.i = 0

    def next(self):
        r = self.i % len(self.t)
        self.i += 1
        return self.t[r], self.b[r]


def _ps_bf16(k, bank):
    return k.psall.bitcast(BF16)[:, bank * 1024:(bank + 1) * 1024]


def proj_phase(k):
    nc, P = k.nc, k.P
    T = 512
    NT = k.NTOK // T
    S = k.S
    wd = k.w["w_mix_in"]
    ps, psb = k.ps, k.psb
    with ExitStack() as st:
        sb = lambda name, shape, dt: st.enter_context(nc.sbuf_tensor(f"sb_pj_{name}", shape, dt))
        w = sb("w", [128, DC, 4096], BF16)
        xT = [sb(f"xT{i}", [128, DC, T], F32) for i in range(2)]
        uT = [sb(f"uT{i}", [128, DC, T], BF16) for i in range(2)]
        sq = sb("sq", [128, DC, T], BF16)
        rstd = [sb(f"rstd{i}", [128, T], F32) for i in range(2)]
        lbl_t = sb("lbl", [128, 2048], F32)
        lb_t = sb("lb", [128, 2, 512], F32)
        oml_t = sb("oml", [128, 2, 512], F32)
        fmbR = Ring([sb(f"fmb{i}", [128, T], BF16) for i in range(4)], "fmb")
        tmbR = Ring([sb(f"tmb{i}", [128, 512], BF16) for i in range(10)], "tmb")
        tmfR = Ring([sb(f"tmf{i}", [128, 512], F32) for i in range(3)], "tmf")
        tmpR = Ring([sb(f"tmp{i}", [128, 512], F32) for i in range(4)], "tmp")
        b_w = bufs(2, "pjw")
        b_xT = bufs(2, "pjxT")
        b_uT = bufs(2, "pjuT")
        b_sq = bufs(DC, "pjsq")
        b_rstd = bufs(2, "pjrstd")
        b_lbl, b_lb, b_oml = Buf("lbl"), Buf("lb"), Buf("oml")
        psT = _ps_bf16(k, 5)
        ctr = {"fm": 0, "tm": 0}

        for g in range(2):
            for dc in range(DC):
                P.dma("pool", w[:, dc, g * 2048:(g + 1) * 2048], wd[dc * 128:(dc + 1) * 128, g * 2048:(g + 1) * 2048],
                      b_w[g], writes=[b_w[g]])
        wg = lambda col: b_w[col // 2048]

        P.dma("sp", lbl_t[:], k.lbl, b_lbl, writes=[b_lbl])
        lv = lbl_t[:].rearrange("p (d l f) -> p d l f", d=2, l=2)
        P.op("dve", lambda e: e.tensor_tensor(lb_t[:], lv[:, :, 0, :], lv[:, :, 1, :], ALU.subtract),
             reads=[b_lbl], writes=[b_lb])
        P.op("act", lambda e: e.activation(lb_t[:], lb_t[:], AF.Exp, scale=-1.0), reads=[b_lb], writes=[b_lb])
        P.op("dve", lambda e: e.tensor_scalar(lb_t[:], lb_t[:], 1.0, None, ALU.add), reads=[b_lb], writes=[b_lb])
        P.op("dve", lambda e: e.reciprocal(lb_t[:], lb_t[:]), reads=[b_lb], writes=[b_lb])
        P.op("dve", lambda e: e.tensor_scalar(oml_t[:], lb_t[:], -1.0, 1.0, ALU.mult, ALU.add),
             reads=[b_lb], writes=[b_oml])

        def silu_evac(psrc, b_psrc, dst, b_dst):
            t, bt = tmpR.next()
            P.op("act", lambda e: e.activation(t[:], psrc, AF.Exp, scale=-1.0), reads=[b_psrc], writes=[bt])
            P.op("dve", lambda e: e.tensor_scalar(t[:], t[:], 1.0, None, ALU.add), reads=[bt], writes=[bt])
            P.op("dve", lambda e: e.reciprocal(t[:], t[:]), reads=[bt], writes=[bt])
            P.op("dve", lambda e: e.tensor_tensor(dst, psrc, t[:], ALU.mult), reads=[bt, b_psrc], writes=[b_dst])

        def pload(c):
            s = c % 2
            tok0 = c * T
            for dc in range(DC):
                P.dma("sp", xT[s][:, dc, :], k.h1T[dc, :, tok0:tok0 + T], b_xT[s], writes=[b_xT[s]])

        def prologue(c):
            s = c % 2
            rms_scale(k, xT[s], [b_xT[s]] * DC, sq, b_sq, rstd[s], b_rstd[s], 7, T, lnexp=True)
            for dc in range(DC):
                P.op("dve", lambda e, s=s, dc=dc: e.scalar_tensor_tensor(
                    uT[s][:, dc, :], xT[s][:, dc, :], k.nv_t[:, 2 * DC + dc: 2 * DC + dc + 1], rstd[s][:],
                    ALU.mult, ALU.mult),
                    reads=[b_xT[s], b_rstd[s], k.cb], writes=[b_uT[s]])

        def fm_part(c):
            s = c % 2
            tok0 = c * T
            b, sl = tok0 // S, tok0 % S
            for kind, colbase, dstT in (("qa", 0, k.QT_s), ("ka", 512, k.KT_s), ("qr", 1536, k.qrT_s)):
                for h in range(NH):
                    pb = ctr["fm"] % 2
                    ctr["fm"] += 1
                    col = colbase + h * 128
                    for dc in range(DC):
                        P.op("pe", lambda e, pb=pb, dc=dc, col=col, s=s: e.matmul(
                            ps[pb][:, 0:T], w[:, dc, col:col + 128], uT[s][:, dc, :],
                            start=(dc == 0), stop=(dc == DC - 1)),
                            reads=[wg(col), b_uT[s]], writes=[psb[pb]])
                    o, bo = fmbR.next()
                    if kind == "qr":
                        silu_evac(ps[pb][:, 0:T], psb[pb], o[:], bo)
                    else:
                        P.op("act", lambda e, o=o, pb=pb: e.activation(o[:], ps[pb][:, 0:T], AF.Copy),
                             reads=[psb[pb]], writes=[bo])
                    P.dma("sp", dstT[b, h, :, sl:sl + T], o[:], bo, reads=[bo])

        def tm_part(c):
            s = c % 2
            tok0 = c * T
            b, sl = tok0 // S, tok0 % S
            for tt in range(T // 128):
                sl2 = sl + tt * 128
                for kind, colbase in (("v", 1024), ("ir", 2048), ("f0", 2560), ("f1", 3072), ("gr", 3584)):
                    pb = 2 + ctr["tm"] % 3
                    ctr["tm"] += 1
                    for dc in range(DC):
                        P.op("pe", lambda e, pb=pb, dc=dc, colbase=colbase, s=s, tt=tt: e.matmul(
                            ps[pb][:, :], uT[s][:, dc, tt * 128:(tt + 1) * 128], w[:, dc, colbase:colbase + 512],
                            start=(dc == 0), stop=(dc == DC - 1)),
                            reads=[wg(colbase), b_uT[s]], writes=[psb[pb]])
                    if kind in ("v", "ir"):
                        dst = k.V_s if kind == "v" else k.ir_s
                        o, bo = tmbR.next()
                        P.op("act", lambda e, o=o, pb=pb: e.activation(o[:], ps[pb][:, :], AF.Copy),
                             reads=[psb[pb]], writes=[bo])
                        P.dma("sp", dst[b, sl2:sl2 + 128, :], o[:], bo, reads=[bo])
                    elif kind == "gr":
                        o, bo = tmbR.next()
                        silu_evac(ps[pb][:, :], psb[pb], o[:], bo)
                        P.dma("sp", k.gate_s[b, sl2:sl2 + 128, :], o[:], bo, reads=[bo])
                    else:
                        d = int(kind[1])
                        t, bt = tmpR.next()
                        P.op("act", lambda e, t=t, pb=pb: e.activation(t[:], ps[pb][:, :], AF.Exp, scale=-1.0),
                             reads=[psb[pb]], writes=[bt])
                        P.op("dve", lambda e, t=t: e.tensor_scalar(t[:], t[:], 1.0, None, ALU.add),
                             reads=[bt], writes=[bt])
                        P.op("dve", lambda e, t=t: e.reciprocal(t[:], t[:]), reads=[bt], writes=[bt])
                        P.op("pool", lambda e, t=t, d=d: e.tensor_tensor(t[:], t[:], oml_t[:, d, :], ALU.mult),
                             reads=[bt, b_oml], writes=[bt])
                        P.op("pool", lambda e, t=t, d=d: e.tensor_tensor(t[:], t[:], lb_t[:, d, :], ALU.add),
                             reads=[bt, b_lb], writes=[bt])
                        g, bg = tmfR.next()
                        P.op("act", lambda e, t=t, g=g: e.activation(g[:], t[:], AF.Ln), reads=[bt], writes=[bg])
                        ghi, bghi = tmbR.next()
                        glo, bglo = tmbR.next()
                        P.op("pool", lambda e, g=g, ghi=ghi: e.tensor_copy(ghi[:], g[:]), reads=[bg], writes=[bghi])
                        P.op("pool", lambda e, g=g, ghi=ghi, glo=glo: e.tensor_tensor(glo[:], g[:], ghi[:], ALU.subtract),
                             reads=[bg, bghi], writes=[bglo])
                        P.dma("sp", k.ghi_s[d][b, sl2:sl2 + 128, :], ghi[:], bghi, reads=[bghi])
                        P.dma("sp", k.glo_s[d][b, sl2:sl2 + 128, :], glo[:], bglo, reads=[bglo])
                        kb, bkb = tmbR.next()
                        P.op("pool", lambda e, t=t, kb=kb: e.tensor_scalar(kb[:], t[:], -1.0, 1.0, ALU.mult, ALU.add),
                             reads=[bt], writes=[bkb])
                        P.dma("sp", k.k_s[d][b, sl2:sl2 + 128, :], kb[:], bkb, reads=[bkb])
                        for h in range(NH):
                            P.op("pe", lambda e, kb=kb, h=h: e.transpose(
                                psT[:, h * 128:(h + 1) * 128], kb[:, h * 128:(h + 1) * 128], k.idb_t[:]),
                                reads=[bkb, k.cb], writes=[psb[5]])
                        o, bo = fmbR.next()
                        P.op("act", lambda e, o=o: e.activation(o[:], psT[:, 0:512], AF.Copy),
                             reads=[psb[5]], writes=[bo])
                        P.dma("sp", k.kT_s[d][b, :, :, sl2:sl2 + 128].rearrange("h p t -> p h t"),
                              o[:].rearrange("p (h t) -> p h t", h=NH), bo, reads=[bo])

        pload(0)
        prologue(0)
        for c in range(NT):
            if c + 1 < NT:
                pload(c + 1)
            fm_part(c)
            if c + 1 < NT:
                prologue(c + 1)
            tm_part(c)
        P.barrier()


def attn_phase(k):
    nc, P = k.nc, k.P
    S = k.S
    NJ = S // 128
    NQC = S // 512
    SCALE = 64 ** -0.5
    ps, psb = k.ps, k.psb
    with ExitStack() as st:
        sb = lambda name, shape, dt: st.enter_context(nc.sbuf_tensor(f"sb_at_{name}", shape, dt))
        QT = [sb(f"QT{i}", [128, S], BF16) for i in range(2)]
        KT = [sb(f"KT{i}", [128, S], BF16) for i in range(2)]
        V = [sb(f"V{i}", [128, NJ, 132], BF16) for i in range(2)]
        b_in = bufs(2, "atin")
        b_vone = bufs(2, "vone")
        G = sb("G", [128, NH, 1152], F32)
        cfar = sb("cfar", [128, 2 * NH], F32)
        lamv = sb("lamv", [128, 4, 64], F32)
        lamt = sb("lamt", [128, 8], F32)
        ljunk = sb("ljunk", [128, 64], F32)
        ahn = sb("ahn", [128, 128], F32)
        b_G, b_cfar, b_lamv, b_lam, b_ahn = Buf("G"), Buf("cfar"), Buf("lamv"), Buf("lam"), Buf("ahn")
        PTR = Ring([sb(f"PT{i}", [128, 1024], BF16) for i in range(3)], "PT")
        TNR = Ring([sb(f"TN{i}", [128, 1024], F32) for i in range(2)], "TN")
        OsbR = Ring([sb(f"Osb{i}", [128, 3, 396], F32) for i in range(2)], "Osb")
        rrR = Ring([sb(f"rr{i}", [128, 2], F32) for i in range(4)], "rr")
        t0R = Ring([sb(f"t0{i}", [128, 128], F32) for i in range(4)], "t0")
        odR = Ring([sb(f"od{i}", [128, 128], F32) for i in range(4)], "od")
        jkR = Ring([sb(f"jk{i}", [128, 128], F32) for i in range(2)], "jk")
        ssR = Ring([sb(f"ss{i}", [128, 1], F32) for i in range(4)], "ss")
        oabR = Ring([sb(f"oab{i}", [128, 128], BF16) for i in range(4)], "oab")
        fmbR = Ring([sb(f"fmb{i}", [128, 512], BF16) for i in range(2)], "atfmb")
        psT = _ps_bf16(k, 7)

        for i in range(2):
            P.op("pool", lambda e, i=i: e.memset(V[i][:, :, 128:132], 1.0), writes=[b_vone[i]])
        P.dma("sp", G[:].rearrange("p h m -> p (h m)"), k.gbias, b_G, writes=[b_G])
        P.dma("sp", cfar[:], k.cfar, b_cfar, writes=[b_cfar])
        P.dma("sp", lamv[:].rearrange("p a f -> p (a f)"), k.lamv, b_lamv, writes=[b_lamv])
        P.dma("sp", ahn[:], k.hnorm[:, 0:128], b_ahn, writes=[b_ahn])
        for i in range(2):
            P.op("dve", lambda e, i=i: e.tensor_tensor(ljunk[:], lamv[:, 2 * i, :], lamv[:, 2 * i + 1, :], ALU.mult),
                 reads=[b_lamv], writes=[b_lam])
            P.op("dve", lambda e, i=i: e.reduce_sum(lamt[:, i:i + 1], ljunk[:], AX.X), reads=[b_lam], writes=[b_lam])
        P.op("act", lambda e: e.activation(lamt[:, 2:4], lamt[:, 0:2], AF.Exp), reads=[b_lam], writes=[b_lam])
        P.op("dve", lambda e: e.tensor_tensor(lamt[:, 4:5], lamt[:, 2:3], lamt[:, 3:4], ALU.subtract),
             reads=[b_lam], writes=[b_lam])
        P.op("dve", lambda e: e.tensor_scalar(lamt[:, 5:6], lamt[:, 4:5], 0.2, -1.0, ALU.add, ALU.mult),
             reads=[b_lam], writes=[b_lam])
        P.op("dve", lambda e: e.tensor_scalar(ahn[:], ahn[:], 0.8, None, ALU.mult), reads=[b_ahn], writes=[b_ahn])

        def load(idx):
            b, h = divmod(idx, NH)
            sl = idx % 2
            P.dma("sp", QT[sl][:], k.QT_s[b, h, :, :], b_in[sl], writes=[b_in[sl]])
            P.dma("sp", KT[sl][:], k.KT_s[b, h, :, :], b_in[sl], writes=[b_in[sl]])
            P.dma("sp", V[sl][:, :, 0:128], k.V_s[b, :, h * 128:(h + 1) * 128].rearrange("(j p) e -> p j e", p=128),
                  b_in[sl], writes=[b_in[sl]])

        def qk(sl, qc, j):
            p = j % 2
            for c in range(2):
                P.op("pe", lambda e, c=c, p=p: e.matmul(
                    ps[2 * p + c][:, :], KT[sl][c * 64:(c + 1) * 64, j * 128:(j + 1) * 128],
                    QT[sl][c * 64:(c + 1) * 64, qc * 512:(qc + 1) * 512], start=True, stop=True),
                    reads=[b_in[sl]], writes=[psb[2 * p + c]])

        def epi_copy(b, h, qc):
            Osb, bO = OsbR.next()
            for bk in range(3):
                n = 396 if bk < 2 else 264
                P.op("dve", lambda e, bk=bk, n=n, Osb=Osb: e.tensor_copy(Osb[:, bk, 0:n], ps[4 + bk][:, 0:n]),
                     reads=[psb[4 + bk]], writes=[bO])
            return (b, h, qc, Osb, bO)

        def epi_rest(ctx):
            b, h, qc, Osb, bO = ctx
            for qt in range(4):
                def acc(c):
                    a = c * 4 + qt
                    return Osb[:, a // 3, (a % 3) * 132:(a % 3) * 132 + 129]
                O0, O1 = acc(0), acc(1)
                rr, brr = rrR.next()
                t0, bt0 = t0R.next()
                od, bod = odR.next()
                jk, bjk = jkR.next()
                ss, bss = ssR.next()
                oab, boab = oabR.next()
                P.op("dve", lambda e, rr=rr, O0=O0: e.reciprocal(rr[:, 0:1], O0[:, 128:129]), reads=[bO], writes=[brr])
                P.op("dve", lambda e, rr=rr, O1=O1: e.reciprocal(rr[:, 1:2], O1[:, 128:129]), reads=[bO], writes=[brr])
                P.op("dve", lambda e, rr=rr: e.tensor_tensor(rr[:, 1:2], rr[:, 1:2], lamt[:, 5:6], ALU.mult),
                     reads=[brr, b_lam], writes=[brr])
                P.op("dve", lambda e, rr=rr, t0=t0, O0=O0: e.tensor_scalar(t0[:], O0[:, 0:128], rr[:, 0:1], None, ALU.mult),
                     reads=[brr, bO], writes=[bt0])
                P.op("dve", lambda e, rr=rr, t0=t0, od=od, O1=O1: e.scalar_tensor_tensor(
                    od[:], O1[:, 0:128], rr[:, 1:2], t0[:], ALU.mult, ALU.add),
                    reads=[brr, bO, bt0], writes=[bod])
                P.op("act", lambda e, jk=jk, od=od, ss=ss: e.activation(jk[:], od[:], AF.Square, accum_out=ss[:]),
                     reads=[bod], writes=[bjk, bss])
                P.op("act", lambda e, ss=ss: e.activation(ss[:], ss[:], AF.Ln, bias=k.eps_t[:], scale=1.0 / 128),
                     reads=[bss, k.cb], writes=[bss])
                P.op("act", lambda e, ss=ss: e.activation(ss[:], ss[:], AF.Exp, scale=-0.5), reads=[bss], writes=[bss])
                P.op("dve", lambda e, oab=oab, od=od, ss=ss: e.scalar_tensor_tensor(
                    oab[:], od[:], ss[:], ahn[:], ALU.mult, ALU.mult),
                    reads=[bod, bss, b_ahn], writes=[boab])
                P.op("pe", lambda e, oab=oab, qt=qt: e.transpose(psT[:, qt * 128:(qt + 1) * 128], oab[:], k.idb_t[:]),
                     reads=[boab, k.cb], writes=[psb[7]])
            o, bo = fmbR.next()
            P.op("act", lambda e, o=o: e.activation(o[:], psT[:, 0:512], AF.Copy), reads=[psb[7]], writes=[bo])
            tok0 = b * S + qc * 512
            P.dma("sp", k.mixT_s[h, :, tok0:tok0 + 512], o[:], bo, reads=[bo])

        NBH = k.NB * NH
        pending = [None]
        PREFETCH = getattr(k, "attn_prefetch", True)
        if PREFETCH:
            load(0)
        for idx in range(NBH):
            b, h = divmod(idx, NH)
            sl = idx % 2
            if PREFETCH:
                if idx + 1 < NBH:
                    load(idx + 1)
            else:
                load(idx)
            for qc in range(NQC):
                qk(sl, qc, 0)
                for j in range(NJ):
                    if j + 1 < NJ:
                        qk(sl, qc, j + 1)
                    if j == min(3, NJ - 1) and pending[0] is not None:
                        epi_rest(pending[0])
                        pending[0] = None
                    p = j % 2
                    d = j - 4 * qc
                    PT, bPT = PTR.next()
                    pair = k.psall[:, p * 1024:(p + 1) * 1024]
                    if -1 <= d <= 4:
                        TN, bTN = TNR.next()
                        g0 = 512 - 128 * d
                        for c in range(2):
                            P.op("dve", lambda e, TN=TN, c=c, p=p, g0=g0, h=h: e.scalar_tensor_tensor(
                                TN[:, c * 512:(c + 1) * 512], ps[2 * p + c][:, :], SCALE, G[:, h, g0:g0 + 512],
                                ALU.mult, ALU.add),
                                reads=[psb[2 * p + c], b_G], writes=[bTN])
                        P.op("act", lambda e, PT=PT, TN=TN: e.activation(PT[:], TN[:], AF.Exp),
                             reads=[bTN], writes=[bPT])
                    else:
                        side = 0 if d < 0 else 1
                        P.op("act", lambda e, PT=PT, pair=pair, side=side, h=h: e.activation(
                            PT[:], pair, AF.Exp, bias=cfar[:, 2 * h + side: 2 * h + side + 1], scale=SCALE),
                            reads=[psb[2 * p], psb[2 * p + 1], b_cfar], writes=[bPT])
                    for c in range(2):
                        for qt in range(4):
                            a = c * 4 + qt
                            bank, col = 4 + a // 3, (a % 3) * 132
                            P.op("pe", lambda e, PT=PT, c=c, qt=qt, bank=bank, col=col, j=j, a=a, sl=sl: e.matmul(
                                ps[bank][:, col:col + 129], PT[:, c * 512 + qt * 128: c * 512 + (qt + 1) * 128],
                                V[sl][:, j, 0:129], start=(j == 0 and a % 3 == 0), stop=(j == NJ - 1),
                                skip_group_check=True),
                                reads=[bPT, b_in[sl], b_vone[sl]], writes=[psb[bank]])
                pending[0] = epi_copy(b, h, qc)
        epi_rest(pending[0])
        P.barrier()


def outproj_phase(k):
    nc, P = k.nc, k.P
    T = 512
    NT = k.NTOK // T
    ps, psb = k.ps, k.psb
    wd = k.w["w_mix_out"]
    with ExitStack() as st:
        sb = lambda name, shape, dt: st.enter_context(nc.sbuf_tensor(f"sb_op_{name}", shape, dt))
        w = sb("w", [128, DC, D], BF16)
        mT = [sb(f"mT{i}", [128, DC, T], BF16) for i in range(2)]
        xT = [sb(f"xT{i}", [128, DC, T], F32) for i in range(2)]
        yT = sb("yT", [128, DC, T], F32)
        sq = sb("sq", [128, DC, T], BF16)
        rstd = sb("rstd", [128, T], F32)
        b_w = Buf("opw")
        b_mT = bufs(2, "opmT")
        b_xT = [bufs(DC, f"opxT{i}_") for i in range(2)]
        b_yT = bufs(DC, "opyT")
        b_sq = bufs(DC, "opsq")
        b_rstd = Buf("oprstd")
        for fc in range(DC):
            P.dma("pool", w[:, fc, :], wd[fc * 128:(fc + 1) * 128, :], b_w, writes=[b_w])

        def load(c):
            s = c % 2
            tok0 = c * T
            for fc in range(DC):
                P.dma("sp", mT[s][:, fc, :], k.mixT_s[fc, :, tok0:tok0 + T], b_mT[s], writes=[b_mT[s]])
            for dc in range(DC):
                P.dma("sp", xT[s][:, dc, :], k.h1T[dc, :, tok0:tok0 + T], b_xT[s][dc], writes=[b_xT[s][dc]])

        load(0)
        for c in range(NT):
            s = c % 2
            tok0 = c * T
            if c + 1 < NT:
                load(c + 1)
            for dc in range(DC):
                pb = dc % 2
                for fc in range(DC):
                    P.op("pe", lambda e, pb=pb, dc=dc, fc=fc, s=s: e.matmul(
                        ps[pb][:, 0:T], w[:, fc, dc * 128:(dc + 1) * 128], mT[s][:, fc, :],
                        start=(fc == 0), stop=(fc == DC - 1)),
                        reads=[b_w, b_mT[s]], writes=[psb[pb]])
                P.op("act", lambda e, pb=pb, dc=dc: e.activation(yT[:, dc, :], ps[pb][:, 0:T], AF.Copy),
                     reads=[psb[pb]], writes=[b_yT[dc]])
            rms_scale(k, yT, b_yT, sq, b_sq, rstd, b_rstd, 7, T, lnexp=True)
            for dc in range(DC):
                P.op("dve", lambda e, dc=dc: e.scalar_tensor_tensor(
                    yT[:, dc, :], yT[:, dc, :], k.nv_t[:, 3 * DC + dc: 3 * DC + dc + 1], rstd[:],
                    ALU.mult, ALU.mult),
                    reads=[b_yT[dc], b_rstd, k.cb], writes=[b_yT[dc]])
                P.op("pool", lambda e, dc=dc, s=s: e.tensor_tensor(
                    xT[s][:, dc, :], xT[s][:, dc, :], yT[:, dc, :], ALU.add),
                    reads=[b_yT[dc], b_xT[s][dc]], writes=[b_xT[s][dc]])
                P.dma("sp", k.h2T[dc, :, tok0:tok0 + T], xT[s][:, dc, :], b_xT[s][dc], reads=[b_xT[s][dc]])
        P.barrier()


def hgrn_phase(k):
    nc, P = k.nc, k.P
    S = k.S
    NCH = S // 128
    ps = k.ps
    with ExitStack() as st:
        sb = lambda name, shape, dt: st.enter_context(nc.sbuf_tensor(f"sb_hg_{name}", shape, dt))
        oacc = sb("oacc", [128, NCH, 512], F32)
        b_oacc = bufs(NCH, "oacc")
        hmat = sb("hmat", [128, 768], BF16)
        hmask = sb("hmask", [128, 256], F32)
        rhn = sb("rhn", [128, 128], F32)
        b_c = bufs(3, "hgc")
        Sf = sb("Sf", [128, NH, 128], F32)
        Sb = sb("Sb", [128, NH, 128], BF16)
        b_Sf = bufs(NH, "Sf")
        b_Sb = bufs(NH, "Sb")
        NIN = 3
        in_t = [dict(ghi=sb(f"ghi{i}", [128, 512], BF16), glo=sb(f"glo{i}", [128, 512], BF16),
                     ktm=sb(f"ktm{i}", [128, 512], BF16),
                     v=sb(f"v{i}", [128, 512], BF16), qT=sb(f"qT{i}", [128, NH, 128], BF16),
                     kT=sb(f"kT{i}", [128, NH, 128], BF16), gate=sb(f"gate{i}", [128, 512], BF16))
                for i in range(NIN)]
        b_inr = bufs(NIN, "hgin")
        EdR = Ring([sb(f"Ed{i}", [128, 512], F32) for i in range(2)], "Ed")
        kdecR = Ring([sb(f"kdec{i}", [128, 512], BF16) for i in range(2)], "kdec")
        EqiR = Ring([sb(f"Eqi{i}", [128, 256], F32) for i in range(8)], "Eqi")
        EkR = Ring([sb(f"Ek{i}", [128, 128], F32) for i in range(4)], "Ek")
        qinR = Ring([sb(f"qin{i}", [128, 128], BF16) for i in range(8)], "qin")
        qitR = Ring([sb(f"qit{i}", [128, 128], BF16) for i in range(8)], "qit")
        kinR = Ring([sb(f"kin{i}", [128, 128], BF16) for i in range(8)], "kin")
        ATmR = Ring([sb(f"ATm{i}", [128, 128], BF16) for i in range(8)], "ATm")
        osumR = Ring([sb(f"osum{i}", [128, 512], F32) for i in range(2)], "osum")
        jkR = Ring([sb(f"jk{i}", [128, 128], F32) for i in range(2)], "hjk")
        ss4R = Ring([sb(f"ss4{i}", [128, 4], F32) for i in range(2)], "ss4")
        onR = Ring([sb(f"on{i}", [128, 512], F32) for i in range(2)], "on")
        oabR = Ring([sb(f"oab{i}", [128, 512], BF16) for i in range(2)], "hoab")
        fmbR = Ring([sb(f"fmb{i}", [128, 512], BF16) for i in range(2)], "hfmb")
        psT = _ps_bf16(k, 6)
        b_gd = Buf("psgd")
        b_gg = bufs(2, "psgg")
        b_at = Buf("psat")
        b_o = Buf("pso")
        b_ds = Buf("psds")
        b_tr = Buf("pstr")

        P.dma("sp", hmat[:], k.hmat, b_c[0], writes=[b_c[0]])
        P.dma("sp", hmask[:], k.hmask, b_c[1], writes=[b_c[1]])
        P.dma("sp", rhn[:], k.hnorm[:, 128:256], b_c[2], writes=[b_c[2]])
        ictr = [0]
        import os
        HG_LEVEL = int(os.environ.get("HG_LEVEL", "5"))
        HG_SKIP = os.environ.get("HG_SKIP", "")

        def step(b, n, d, first, last):
            sl = n * 128
            r = ictr[0] % NIN
            ictr[0] += 1
            tl, bi = in_t[r], b_inr[r]
            P.dma("sp", tl["ghi"][:], k.ghi_s[d][b, sl:sl + 128, :], bi, writes=[bi])
            P.dma("sp", tl["glo"][:], k.glo_s[d][b, sl:sl + 128, :], bi, writes=[bi])
            P.dma("sp", tl["ktm"][:], k.k_s[d][b, sl:sl + 128, :], bi, writes=[bi])
            P.dma("sp", tl["v"][:], k.ir_s[b, sl:sl + 128, :], bi, writes=[bi])
            P.dma("sp", tl["qT"][:], k.qrT_s[b, :, :, sl:sl + 128].rearrange("h p t -> p h t"), bi, writes=[bi])
            P.dma("sp", tl["kT"][:], k.kT_s[d][b, :, :, sl:sl + 128].rearrange("h p t -> p h t"), bi, writes=[bi])
            if d == 1:
                P.dma("sp", tl["gate"][:], k.gate_s[b, sl:sl + 128, :], bi, writes=[bi])
            ghi, glo, ktm, v, qT, kT, gate = (tl["ghi"], tl["glo"], tl["ktm"], tl["v"], tl["qT"], tl["kT"],
                                              tl["gate"])
            M_d = hmat[:, d * 256:(d + 1) * 256]
            U_d = hmat[:, 512 + d * 128: 512 + (d + 1) * 128]
            mask_d = hmask[:, d * 128:(d + 1) * 128]
            yield "loaded"
            P.op("pe", lambda e: e.matmul(ps[0][:, :], U_d, ghi[:], start=True, stop=False),
                 reads=[bi, b_c[0]], writes=[b_gd])
            P.op("pe", lambda e: e.matmul(ps[0][:, :], U_d, glo[:], start=False, stop=True),
                 reads=[bi, b_c[0]], writes=[b_gd])
            Ed, bEd = EdR.next()
            P.op("act", lambda e: e.activation(Ed[:], ps[0][:, :], AF.Exp), reads=[b_gd], writes=[bEd])
            kdec, bkd = kdecR.next()
            P.op("pool", lambda e: e.tensor_tensor(kdec[:], ktm[:], Ed[:], ALU.mult), reads=[bi, bEd], writes=[bkd])
            hs = []
            for pr in range(2):
                for h in (2 * pr, 2 * pr + 1):
                    gg = ps[1 + pr][:, (h % 2) * 256:(h % 2) * 256 + 256]
                    P.op("pe", lambda e, gg=gg, h=h: e.matmul(gg, ghi[:, h * 128:(h + 1) * 128], M_d, start=True, stop=False),
                         reads=[bi, b_c[0]], writes=[b_gg[pr]])
                    P.op("pe", lambda e, gg=gg, h=h: e.matmul(gg, glo[:, h * 128:(h + 1) * 128], M_d, start=False, stop=True),
                         reads=[bi, b_c[0]], writes=[b_gg[pr]])
            for h in range(NH):
                pr = h // 2
                gg = ps[1 + pr][:, (h % 2) * 256:(h % 2) * 256 + 256]
                Eqi, bEqi = EqiR.next()
                Ek, bEk = EkR.next()
                P.op("act", lambda e, gg=gg, Eqi=Eqi: e.activation(Eqi[:], gg, AF.Exp), reads=[b_gg[pr]], writes=[bEqi])
                P.op("act", lambda e, gg=gg, Ek=Ek: e.activation(Ek[:], gg[:, 0:128], AF.Exp, scale=-1.0),
                     reads=[b_gg[pr]], writes=[bEk])
                qin, bqin = qinR.next()
                qit, bqit = qitR.next()
                kin, bkin = kinR.next()
                P.op("dve", lambda e, qin=qin, Eqi=Eqi, h=h: e.tensor_tensor(qin[:], qT[:, h, :], Eqi[:, 0:128], ALU.mult),
                     reads=[bi, bEqi], writes=[bqin])
                P.op("dve", lambda e, qit=qit, Eqi=Eqi, h=h: e.tensor_tensor(qit[:], qT[:, h, :], Eqi[:, 128:256], ALU.mult),
                     reads=[bi, bEqi], writes=[bqit])
                P.op("pool", lambda e, kin=kin, Ek=Ek, h=h: e.tensor_tensor(kin[:], kT[:, h, :], Ek[:], ALU.mult),
                     reads=[bi, bEk], writes=[bkin])
                hs.append((Eqi, bEqi, qin, bqin, qit, bqit, kin, bkin))
            yield "decays"
            ats = []
            for h in range(NH):
                Eqi, bEqi, qin, bqin, qit, bqit, kin, bkin = hs[h]
                at = ps[3][:, h * 128:(h + 1) * 128]
                P.op("pe", lambda e, at=at, kin=kin, qin=qin: e.matmul(at, kin[:], qin[:], start=True, stop=True),
                     reads=[bkin, bqin], writes=[b_at])
            for h in range(NH):
                at = ps[3][:, h * 128:(h + 1) * 128]
                ATm, bATm = ATmR.next()
                P.op("dve", lambda e, at=at, ATm=ATm: e.tensor_tensor(ATm[:], at, mask_d, ALU.mult),
                     reads=[b_at, b_c[1]], writes=[bATm])
                ats.append((ATm, bATm))
            for h in range(NH):
                Eqi, bEqi, qin, bqin, qit, bqit, kin, bkin = hs[h]
                ATm, bATm = ats[h]
                hsl = slice(h * 128, (h + 1) * 128)
                P.op("pe", lambda e, ATm=ATm, hsl=hsl: e.matmul(ps[4][:, hsl], ATm[:], v[:, hsl], start=True, stop=first),
                     reads=[bATm, bi], writes=[b_o])
                if not first:
                    P.op("pe", lambda e, qit=qit, hsl=hsl, h=h: e.matmul(ps[4][:, hsl], qit[:], Sb[:, h, :],
                                                                   start=False, stop=True),
                         reads=[bqit, b_Sb[h]], writes=[b_o])
            if not last:
                for h in range(NH):
                    hsl = slice(h * 128, (h + 1) * 128)
                    P.op("pe", lambda e, hsl=hsl: e.matmul(ps[5][:, hsl], kdec[:, hsl], v[:, hsl], start=True, stop=True),
                         reads=[bkd, bi], writes=[b_ds])
                for h in range(NH):
                    Eqi, bEqi = hs[h][0], hs[h][1]
                    hsl = slice(h * 128, (h + 1) * 128)
                    if first:
                        P.op("dve", lambda e, hsl=hsl, h=h: e.tensor_copy(Sf[:, h, :], ps[5][:, hsl]),
                             reads=[b_ds], writes=[b_Sf[h]])
                    else:
                        dc_ = 255 if d == 0 else 128
                        P.op("dve", lambda e, hsl=hsl, h=h, Eqi=Eqi, dc_=dc_: e.scalar_tensor_tensor(
                            Sf[:, h, :], Sf[:, h, :], Eqi[:, dc_:dc_ + 1], ps[5][:, hsl], ALU.mult, ALU.add),
                            reads=[b_Sf[h], bEqi, b_ds], writes=[b_Sf[h]])
                    P.op("act", lambda e, h=h: e.activation(Sb[:, h, :], Sf[:, h, :], AF.Copy),
                         reads=[b_Sf[h]], writes=[b_Sb[h]])
            if d == 0:
                P.op("act", lambda e: e.activation(oacc[:, n, :], ps[4][:, :], AF.Copy), reads=[b_o], writes=[b_oacc[n]])
                return
            osum, bos = osumR.next()
            P.op("dve", lambda e: e.tensor_tensor(osum[:], oacc[:, n, :], ps[4][:, :], ALU.add),
                 reads=[b_o, b_oacc[n]], writes=[bos])
            ss4, bss = ss4R.next()
            jk, bjk = jkR.next()
            for h in range(NH):
                P.op("act", lambda e, h=h: e.activation(jk[:], osum[:, h * 128:(h + 1) * 128], AF.Square,
                                                        accum_out=ss4[:, h:h + 1]),
                     reads=[bos], writes=[bjk, bss])
            P.op("act", lambda e: e.activation(ss4[:], ss4[:], AF.Ln, bias=k.eps_t[:], scale=1.0 / 128),
                 reads=[bss, k.cb], writes=[bss])
            P.op("act", lambda e: e.activation(ss4[:], ss4[:], AF.Exp, scale=-0.5), reads=[bss], writes=[bss])
            on, bon = onR.next()
            for h in range(NH):
                P.op("dve", lambda e, h=h: e.scalar_tensor_tensor(
                    on[:, h * 128:(h + 1) * 128], osum[:, h * 128:(h + 1) * 128], ss4[:, h:h + 1], rhn[:],
                    ALU.mult, ALU.mult), reads=[bos, bss, b_c[2]], writes=[bon])
            oab, boab = oabR.next()
            P.op("pool", lambda e: e.tensor_tensor(oab[:], on[:], gate[:], ALU.mult), reads=[bon, bi], writes=[boab])
            for h in range(NH):
                P.op("pe", lambda e, h=h: e.transpose(psT[:, h * 128:(h + 1) * 128], oab[:, h * 128:(h + 1) * 128],
                                                      k.idb_t[:]),
                     reads=[boab, k.cb], writes=[b_tr])
            o, bo = fmbR.next()
            P.op("act", lambda e: e.activation(o[:], psT[:, 0:512], AF.Copy), reads=[b_tr], writes=[bo])
            tok0 = b * S + sl
            P.dma("sp", k.mixT_s[4:8, :, tok0:tok0 + 128].rearrange("h p t -> p h t"),
                  o[:].rearrange("p (h t) -> p h t", h=NH), bo, reads=[bo])

        gens = []
        for b in range(k.NB):
            for n in range(NCH):
                gens.append(step(b, n, 0, n == 0, n == NCH - 1))
            for n in range(NCH - 1, -1, -1):
                gens.append(step(b, n, 1, n == NCH - 1, n == 0))
        adv = lambda i: next(gens[i], None)
        NS = len(gens)
        adv(0)
        if NS > 1:
            adv(1)
        adv(0)
        for i in range(NS):
            if i + 2 < NS:
                adv(i + 2)
            if i + 1 < NS:
                adv(i + 1)
            adv(i)
        P.barrier()


_NC_CACHE = {}


def _consts():
    import ml_dtypes
    bf = ml_dtypes.bfloat16
    return {
        "ident_f": np.eye(128, dtype=np.float32),
        "ident_b": np.eye(128, dtype=np.float32).astype(bf),
        "ones_b": np.ones((128, 128), dtype=np.float32).astype(bf),
    }


def _pack_normvecs(vecs):
    a = np.stack([np.asarray(v, np.float32).reshape(DC, 128) for v in vecs], 0)
    return np.ascontiguousarray(a.transpose(2, 0, 1).reshape(128, 7 * DC))


def _rel_bucket_np(rel):
    nb, max_exact = 16, 8
    side = np.where(rel > 0, nb, 0)
    n = np.abs(rel)
    nf = np.maximum(n, 1).astype(np.float32)
    large = max_exact + (np.log(nf / np.float32(max_exact)) / np.float32(math.log(128 / max_exact))
                         * np.float32(nb - max_exact)).astype(np.int32)
    large = np.minimum(large, nb - 1)
    return side + np.where(n < max_exact, n, large)


def _hgrn_consts():
    s_ = np.arange(128)[:, None]
    t_ = np.arange(128)[None, :]
    f = np.float32
    Lf = (s_ <= t_).astype(f)
    Mqf = Lf - (s_ <= 63).astype(f)
    Lb = (s_ >= t_).astype(f)
    Mqb = Lb - (s_ >= 64).astype(f)
    Uf = (s_ > t_).astype(f)
    Ub = (s_ < t_).astype(f)
    hmat = np.concatenate([Mqf, Lf, Mqb, Lb, Uf, Ub], axis=1)
    hmask = np.concatenate([Lf, Lb], axis=1)
    import ml_dtypes
    return np.ascontiguousarray(hmat).astype(ml_dtypes.bfloat16), np.ascontiguousarray(hmask)


def _rep(v, n=128):
    v = np.asarray(v, np.float32).reshape(1, -1)
    return np.ascontiguousarray(np.repeat(v, n, axis=0))


def run(inputs, NB, S, n_cores, stage="full", debug=False):
    key = (NB, S, stage, debug)
    if key not in _NC_CACHE:
        _NC_CACHE[key] = build(NB, S, stage, debug)
    nc = _NC_CACHE[key]
    x = np.asarray(inputs["x"], np.float32)
    xs = x.reshape(n_cores, NB * S, D)
    shared = dict(_consts())
    for nm in ("ffn1_w_in", "ffn1_w_out", "ffn2_w_in", "ffn2_w_out", "w_mix_in", "w_mix_out"):
        shared[nm] = np.ascontiguousarray(np.asarray(inputs[nm], np.float32)[0])
    shared["normvecs"] = _pack_normvecs([
        inputs["ffn1_pre_norm"][0], inputs["ffn1_post_norm"][0], inputs["mix_pre_norm"][0],
        inputs["mix_post_norm"][0], inputs["mix_post_norm"][0], inputs["ffn2_pre_norm"][0],
        inputs["ffn2_post_norm"][0]])
    rb = np.asarray(inputs["rel_bias"], np.float32)
    kl = np.arange(128)[:, None]
    m = np.arange(1152)[None, :]
    bidx = _rel_bucket_np(kl - (m - 512))
    gb = rb[bidx]
    shared["gbias"] = np.ascontiguousarray(gb.transpose(0, 2, 1).reshape(128, NH * 1152))
    shared["cfar"] = _rep(np.stack([rb[15], rb[31]], axis=1).reshape(-1))
    shared["lamv"] = _rep(np.concatenate([np.asarray(inputs[n_], np.float32)[0] for n_ in
                                          ("lambda_q1", "lambda_k1", "lambda_q2", "lambda_k2")]))
    shared["lbl"] = _rep(np.asarray(inputs["lb_logits"], np.float32).reshape(-1))
    shared["hnorm"] = _rep(np.concatenate([np.asarray(inputs["attn_head_norm"], np.float32)[0],
                                           np.asarray(inputs["rnn_head_norm"], np.float32)[0]]))
    shared["hmat"], shared["hmask"] = _hgrn_consts()
    in_maps = []
    for c in range(n_cores):
        m_ = dict(shared)
        m_["x"] = np.ascontiguousarray(xs[c])
        in_maps.append(m_)
    res = run_bass_kernel_spmd(nc, in_maps, core_ids=list(range(n_cores)))
    outs = [np.asarray(r["out"], np.float32).reshape(NB, S, D) for r in res.results]
    if debug:
        return np.concatenate(outs, 0), res.results
    return np.concatenate(outs, 0)


def kernel(**inputs):
    return run(inputs, 2, 4096, N_CORES, "full")
```

```python
import math
from contextlib import ExitStack

import numpy as np
import concourse.bass as bass
import concourse.mybir as mybir
from concourse.bass_utils import run_bass_kernel_spmd

F32 = mybir.dt.float32
BF16 = mybir.dt.bfloat16
AF = mybir.ActivationFunctionType
ALU = mybir.AluOpType
AX = mybir.AxisListType

D = 1024
DC = 8
DFF = 2816
FC = 22
NH = 4
EPS = 1e-6
N_CORES = 8
ENGS = ("pe", "act", "dve", "pool", "sp")
EPOCH = 30000


class Buf:
    __slots__ = ("name", "lw", "rd", "chan")

    def __init__(self, name=""):
        self.name = name
        self.lw = None
        self.rd = {}
        self.chan = None


def bufs(n, name=""):
    return [Buf(f"{name}{i}") for i in range(n)]


class Chan:
    __slots__ = ("sem", "count", "id")
    _n = 0

    def __init__(self):
        self.sem = None
        self.count = 0
        Chan._n += 1
        self.id = Chan._n


class Prog:
    def __init__(self, nc):
        self.nc = nc
        self.ops = {e: [] for e in ENGS}
        self.known = {e: {} for e in ENGS}
        self.chans = []
        self.free_chans = []
        self.live_chans = []

    def _need(self, eng, tok, kind, waits, cur_chan=None):
        if tok is None:
            return
        if tok[0] == "d" and kind == "waw" and tok[1] is cur_chan:
            return
        if tok[0] == "e":
            _, f, i = tok
            if f == eng:
                if eng == "pe" or kind != "raw":
                    return
            key = ("e", f)
            val = i
        else:
            _, ch, v = tok
            key = ("d", ch.id)
            val = v
        kn = self.known[eng]
        if kn.get(key, -1) >= val:
            return
        kn[key] = val
        waits.append(tok)
        if tok[0] == "e":
            self.ops[tok[1]][tok[2]]["inc"] = True

    def _mk(self, eng, fn, reads, writes, dma_chan=None):
        idx = len(self.ops[eng])
        waits = []
        for b in reads:
            self._need(eng, b.lw, "raw", waits)
        for b in writes:
            self._need(eng, b.lw, "waw", waits, dma_chan)
            for t in b.rd.values():
                self._need(eng, t, "war", waits)
        rec = dict(fn=fn, waits=waits, inc=False, chan=dma_chan, val=None)
        self.ops[eng].append(rec)
        if dma_chan is None:
            tok = ("e", eng, idx)
        else:
            dma_chan.count += 16
            tok = ("d", dma_chan, dma_chan.count)
        for b in writes:
            b.lw = tok
            b.rd = {}
        for b in reads:
            k = ("e", tok[1]) if tok[0] == "e" else ("d", tok[1].id)
            b.rd[k] = tok
        return tok

    def op(self, eng, fn, reads=(), writes=()):
        return self._mk(eng, fn, list(reads), list(writes))

    def dma(self, q, out_ap, in_ap, sb, reads=(), writes=()):
        if sb.chan is None:
            if self.free_chans:
                sb.chan = self.free_chans.pop()
            else:
                sb.chan = Chan()
                self.chans.append(sb.chan)
            self.live_chans.append(sb.chan)
        return self._mk(q, lambda e: e.dma_start(out=out_ap, in_=in_ap),
                        list(reads), list(writes), dma_chan=sb.chan)

    def barrier(self):
        bb = Buf("barrier")
        waits = []
        for e in ENGS:
            if e != "sp" and self.ops[e]:
                self._need("sp", ("e", e, len(self.ops[e]) - 1), "raw", waits)
        for ch in self.chans:
            if ch.count:
                self._need("sp", ("d", ch, ch.count), "raw", waits)
        idx = len(self.ops["sp"])
        self.ops["sp"].append(dict(fn=lambda e: e.nop(), waits=waits, inc=False, chan=None, val=None))
        tok = ("e", "sp", idx)
        bb.lw = tok
        for e in ENGS:
            if e != "sp":
                self._mk(e, lambda en: en.nop(), [bb], [])
        self.free_chans.extend(self.live_chans)
        self.live_chans = []

    def emit(self, stack):
        nc = self.nc
        esems = {}
        for e in ENGS:
            n = 0
            for rec in self.ops[e]:
                if rec["inc"]:
                    rec["val"] = (n // EPOCH, n % EPOCH + 1)
                    n += 1
            nep = (n + EPOCH - 1) // EPOCH
            esems[e] = [stack.enter_context(nc.semaphore(f"s_{e}{k}")) for k in range(max(nep, 1))]
        for ch in self.chans:
            assert ch.count < 60000, ch.count
            ch.sem = stack.enter_context(nc.semaphore(f"s_ch{ch.id}"))
        ops = self.ops

        def run(eng_name):
            def body(eng):
                for rec in ops[eng_name]:
                    for tok in rec["waits"]:
                        if tok[0] == "e":
                            ep, v = ops[tok[1]][tok[2]]["val"]
                            eng.wait_ge(esems[tok[1]][ep], v)
                        else:
                            eng.wait_ge(tok[1].sem, tok[2])
                    ins = rec["fn"](eng)
                    if rec["chan"] is not None:
                        ins.then_inc(rec["chan"].sem, 16)
                    elif rec["inc"]:
                        ins.then_inc(esems[eng_name][rec["val"][0]], 1)
            return body

        with nc.Block() as block:
            block.sync(run("sp"))
            block.scalar(run("act"))
            block.vector(run("dve"))
            block.gpsimd(run("pool"))
            block.tensor(run("pe"))


class K:
    pass


def build(NB, S, stage="full", debug=False):
    nc = bass.Bass("TRN2", target_bir_lowering=False)
    NTOK = NB * S
    P = Prog(nc)
    k = K()
    k.nc, k.P, k.NB, k.S, k.NTOK = nc, P, NB, S, NTOK
    import os
    k.attn_prefetch = os.environ.get('ATTN_PREFETCH', '1') == '1'

    def din(name, shape, dt=F32):
        return nc.dram_tensor(name, list(shape), dt, kind="ExternalInput").ap()

    def dscr(name, shape, dt):
        return nc.dram_tensor(name, list(shape), dt, kind="ExternalOutput" if debug else "Internal").ap()

    k.x = din("x", [NTOK, D])
    k.out = nc.dram_tensor("out", [NTOK, D], F32, kind="ExternalOutput").ap()
    k.w = {}
    for nm, shp in (("ffn1_w_in", [D, 2 * DFF]), ("ffn1_w_out", [DFF, D]),
                    ("ffn2_w_in", [D, 2 * DFF]), ("ffn2_w_out", [DFF, D]),
                    ("w_mix_in", [D, 4096]), ("w_mix_out", [D, D])):
        k.w[nm] = din(nm, shp)
    k.nv = din("normvecs", [128, 7 * DC])
    k.ident_f = din("ident_f", [128, 128])
    k.ident_b = din("ident_b", [128, 128], BF16)
    k.ones_b = din("ones_b", [128, 128], BF16)
    k.gbias = din("gbias", [128, NH * 1152])
    k.cfar = din("cfar", [128, 2 * NH])
    k.lamv = din("lamv", [128, 4 * 64])
    k.lbl = din("lbl", [128, 2 * 2 * 512])
    k.hnorm = din("hnorm", [128, 2 * 128])
    k.hmat = din("hmat", [128, 2 * 256 + 2 * 128], BF16)
    k.hmask = din("hmask", [128, 2 * 128])
    k.h1T = dscr("h1T", [DC, 128, NTOK], F32)
    k.h2T = dscr("h2T", [DC, 128, NTOK], F32)
    k.QT_s = dscr("QT_s", [NB, NH, 128, S], BF16)
    k.KT_s = dscr("KT_s", [NB, NH, 128, S], BF16)
    k.V_s = dscr("V_s", [NB, S, 512], BF16)
    k.qrT_s = dscr("qrT_s", [NB, NH, 128, S], BF16)
    k.kT_s = [dscr(f"kT_s{d}", [NB, NH, 128, S], BF16) for d in range(2)]
    k.k_s = [dscr(f"k_s{d}", [NB, S, 512], BF16) for d in range(2)]
    k.ghi_s = [dscr(f"ghi_s{d}", [NB, S, 512], BF16) for d in range(2)]
    k.glo_s = [dscr(f"glo_s{d}", [NB, S, 512], BF16) for d in range(2)]
    k.ir_s = dscr("ir_s", [NB, S, 512], BF16)
    k.gate_s = dscr("gate_s", [NB, S, 512], BF16)
    k.mixT_s = dscr("mixT_s", [DC, 128, NTOK], BF16)

    with ExitStack() as gs:
        psall = gs.enter_context(nc.psum_tensor("psall", [128, 4096], F32))
        k.psall = psall
        k.ps = [psall[:, i * 512:(i + 1) * 512] for i in range(8)]
        k.psb = bufs(8, "psb")
        k.nv_t = gs.enter_context(nc.sbuf_tensor("nv_t", [128, 7 * DC], F32))
        k.nvh_t = gs.enter_context(nc.sbuf_tensor("nvh_t", [128, 7 * DC], F32))
        k.idf_t = gs.enter_context(nc.sbuf_tensor("idf_t", [128, 128], F32))
        k.idb_t = gs.enter_context(nc.sbuf_tensor("idb_t", [128, 128], BF16))
        k.ones_t = gs.enter_context(nc.sbuf_tensor("ones_t", [128, 128], BF16))
        k.eps_t = gs.enter_context(nc.sbuf_tensor("eps_t", [128, 1], F32))
        k.one_t = gs.enter_context(nc.sbuf_tensor("one_t", [128, 1], F32))
        k.cb = Buf("consts")
        cb2 = bufs(4, "cld")
        P.dma("sp", k.nv_t[:], k.nv, cb2[0], writes=[cb2[0]])
        P.dma("sp", k.idf_t[:], k.ident_f, cb2[1], writes=[cb2[1]])
        P.dma("sp", k.idb_t[:], k.ident_b, cb2[2], writes=[cb2[2]])
        P.dma("sp", k.ones_t[:], k.ones_b, cb2[3], writes=[cb2[3]])
        P.op("dve", lambda e: e.memset(k.eps_t[:], EPS), writes=[k.cb])
        P.op("dve", lambda e: e.memset(k.one_t[:], 1.0), writes=[k.cb])
        P.op("dve", lambda e: e.tensor_scalar(k.nvh_t[:], k.nv_t[:], 0.5, None, ALU.mult),
             reads=cb2, writes=[k.cb])

        if stage == "ffn1":
            ffn_phase(k, "ffn1", ("tm", k.x), ("tm", k.out), 0, 1)
        elif stage.startswith("mix"):
            parts = stage.split(":")[1] if ":" in stage else "paho"
            ffn_load_only(k, ("tm", k.x), ("fm", k.h1T))
            if "p" in parts:
                proj_phase(k)
            if "a" in parts:
                attn_phase(k)
            if "h" in parts:
                hgrn_phase(k)
            if "o" in parts:
                outproj_phase(k)
            ffn_load_only(k, ("fm", k.h2T if "o" in parts else k.h1T), ("tm", k.out))
        else:
            ffn_phase(k, "ffn1", ("tm", k.x), ("fm", k.h1T), 0, 1)
            proj_phase(k)
            attn_phase(k)
            hgrn_phase(k)
            outproj_phase(k)
            ffn_phase(k, "ffn2", ("fm", k.h2T), ("tm", k.out), 5, 6)

        P.barrier()
        P.emit(gs)
    return nc


def ffn_phase(k, wname, src, dst, iv_pre, iv_post):
    nc, P = k.nc, k.P
    T = 256
    NT = k.NTOK // T
    TT = T // 128
    passthru = wname is None
    if passthru:
        k.pt_ctr = getattr(k, "pt_ctr", 0) + 1
        wname = f"pt{k.pt_ctr}"
    else:
        w_in_d = k.w[wname + "_w_in"]
        w_out_d = k.w[wname + "_w_out"]
    with ExitStack() as st:
        sb = lambda name, shape, dt: st.enter_context(nc.sbuf_tensor(f"sb_{wname}_{name}", shape, dt))
        w_in = sb("w_in", [128, DC, 2 * DFF if not passthru else 2], BF16)
        w_out = sb("w_out", [128, FC, D if not passthru else 2], BF16)
        xtm = [sb(f"xtm{i}", [128, D], F32) for i in range(3)]
        xT = [sb(f"xT{i}", [128, DC, T], F32) for i in range(2)]
        uT = [sb(f"uT{i}", [128, DC, T], BF16) for i in range(2)]
        sq = sb("sq", [128, DC, T], BF16)
        hid = sb("hid", [128, FC, T], BF16)
        yT = sb("yT", [128, DC, T], F32)
        sg = [sb(f"sg{i}", [128, T], BF16) for i in range(2)]
        rstd = [sb(f"rstd{i}", [128, T], F32) for i in range(2)]
        b_win = bufs(4, "win")
        b_wout = bufs(2, "wout")
        wgrp = lambda fc: min(fc // 6, 3)
        wogrp = lambda fc: 0 if fc < 11 else 1
        b_xtm = bufs(3, "xtm")
        b_xT = [bufs(DC, f"xT{i}_") for i in range(2)]
        b_uT = [bufs(DC, f"uT{i}_") for i in range(2)]
        b_sq = bufs(DC, "sq")
        b_hid = bufs(FC, "hid")
        b_yT = bufs(DC, "yT")
        b_sg = bufs(2, "sg")
        b_rstd = bufs(2, "rstd")
        ps, psb = k.ps, k.psb
        PS_G, PS_U, PS_Y, PS_SS, PS_TR = (0, 1), (2, 3), (4, 5), 6, 7

        for g in range(4 if not passthru else 0):
            c0, c1 = g * 6 * 128, min((g + 1) * 6, FC) * 128
            for half in (0, DFF):
                for dc in range(DC):
                    P.dma("pool", w_in[:, dc, half + c0: half + c1],
                          w_in_d[dc * 128:(dc + 1) * 128, half + c0: half + c1], b_win[g], writes=[b_win[g]])
        for fc in range(FC if not passthru else 0):
            P.dma("pool", w_out[:, fc, :], w_out_d[fc * 128:(fc + 1) * 128, :], b_wout[wogrp(fc)],
                  writes=[b_wout[wogrp(fc)]])

        xt_ctr = [0]

        def prologue(c):
            s = c % 2
            tok0 = c * T
            if src[0] == "tm":
                for tt in range(TT):
                    r = xt_ctr[0] % 3
                    xt_ctr[0] += 1
                    P.dma("sp", xtm[r][:], src[1][tok0 + tt * 128: tok0 + (tt + 1) * 128, :], b_xtm[r],
                          writes=[b_xtm[r]])
                    for g in range(2):
                        for j in range(4):
                            dc = g * 4 + j
                            P.op("pe", lambda e, r=r, dc=dc, j=j: e.transpose(
                                ps[PS_TR][:, j * 128:(j + 1) * 128], xtm[r][:, dc * 128:(dc + 1) * 128], k.idf_t[:]),
                                reads=[b_xtm[r], k.cb], writes=[psb[PS_TR]])
                        P.op("act", lambda e, s=s, g=g, tt=tt: e.activation(
                            xT[s][:, g * 4:(g + 1) * 4, tt * 128:(tt + 1) * 128],
                            ps[PS_TR][:].rearrange("p (j t) -> p j t", j=4), AF.Copy),
                            reads=[psb[PS_TR]], writes=b_xT[s][g * 4:(g + 1) * 4])
            else:
                for dc in range(DC):
                    P.dma("sp", xT[s][:, dc, :], src[1][dc, :, tok0:tok0 + T], b_xT[s][dc], writes=[b_xT[s][dc]])
            if passthru:
                return
            rms_scale(k, xT[s], b_xT[s], sq, b_sq, rstd[s], b_rstd[s], PS_SS, T)
            for dc in range(DC):
                P.op("dve", lambda e, s=s, dc=dc: e.scalar_tensor_tensor(
                    uT[s][:, dc, :], xT[s][:, dc, :], k.nv_t[:, iv_pre * DC + dc: iv_pre * DC + dc + 1], rstd[s][:],
                    ALU.mult, ALU.mult),
                    reads=[b_xT[s][dc], b_rstd[s], k.cb], writes=[b_uT[s][dc]])

        def gate_up(c):
            s = c % 2
            for fc in range(FC):
                pg, pu = PS_G[fc % 2], PS_U[fc % 2]
                for dc in range(DC):
                    P.op("pe", lambda e, pg=pg, dc=dc, fc=fc, s=s: e.matmul(
                        ps[pg][:, 0:T], w_in[:, dc, fc * 128:(fc + 1) * 128], uT[s][:, dc, :],
                        start=(dc == 0), stop=(dc == DC - 1)),
                        reads=[b_win[wgrp(fc)], b_uT[s][dc]], writes=[psb[pg]])
                for dc in range(DC):
                    P.op("pe", lambda e, pu=pu, dc=dc, fc=fc, s=s: e.matmul(
                        ps[pu][:, 0:T], w_in[:, dc, DFF + fc * 128: DFF + (fc + 1) * 128], uT[s][:, dc, :],
                        start=(dc == 0), stop=(dc == DC - 1)),
                        reads=[b_win[wgrp(fc)], b_uT[s][dc]], writes=[psb[pu]])
                P.op("act", lambda e, pg=pg, fc=fc: e.activation(sg[fc % 2][:], ps[pg][:, 0:T], AF.Silu),
                     reads=[psb[pg]], writes=[b_sg[fc % 2]])
                P.op("dve", lambda e, pu=pu, fc=fc: e.tensor_tensor(
                    hid[:, fc, :], sg[fc % 2][:], ps[pu][:, 0:T], ALU.mult),
                    reads=[b_sg[fc % 2], psb[pu]], writes=[b_hid[fc]])

        def down(c):
            for dc in range(DC):
                py = PS_Y[dc % 2]
                for fc in range(FC):
                    P.op("pe", lambda e, py=py, dc=dc, fc=fc: e.matmul(
                        ps[py][:, 0:T], w_out[:, fc, dc * 128:(dc + 1) * 128], hid[:, fc, :],
                        start=(fc == 0), stop=(fc == FC - 1)),
                        reads=[b_wout[wogrp(fc)], b_hid[fc]], writes=[psb[py]])
                P.op("act", lambda e, py=py, dc=dc: e.activation(yT[:, dc, :], ps[py][:, 0:T], AF.Copy),
                     reads=[psb[py]], writes=[b_yT[dc]])

        def epilogue(c):
            s = c % 2
            tok0 = c * T
            r2, b_r2 = rstd[s], b_rstd[s]
            if not passthru:
                rms_scale(k, yT, b_yT, sq, b_sq, r2, b_r2, PS_SS, T)
            for dc in range(DC if not passthru else 0):
                P.op("dve", lambda e, dc=dc: e.scalar_tensor_tensor(
                    yT[:, dc, :], yT[:, dc, :], k.nvh_t[:, iv_post * DC + dc: iv_post * DC + dc + 1], r2[:],
                    ALU.mult, ALU.mult),
                    reads=[b_yT[dc], b_r2, k.cb], writes=[b_yT[dc]])
                P.op("pool", lambda e, dc=dc, s=s: e.tensor_tensor(
                    xT[s][:, dc, :], xT[s][:, dc, :], yT[:, dc, :], ALU.add),
                    reads=[b_yT[dc], b_xT[s][dc]], writes=[b_xT[s][dc]])
            if dst[0] == "fm":
                for dc in range(DC):
                    P.dma("sp", dst[1][dc, :, tok0:tok0 + T], xT[s][:, dc, :], b_xT[s][dc], reads=[b_xT[s][dc]])
            else:
                for tt in range(TT):
                    r = xt_ctr[0] % 3
                    xt_ctr[0] += 1
                    for g in range(2):
                        for j in range(4):
                            dc = g * 4 + j
                            P.op("pe", lambda e, s=s, dc=dc, j=j, tt=tt: e.transpose(
                                ps[PS_TR][:, j * 128:(j + 1) * 128], xT[s][:, dc, tt * 128:(tt + 1) * 128], k.idf_t[:]),
                                reads=[b_xT[s][dc], k.cb], writes=[psb[PS_TR]])
                        P.op("act", lambda e, r=r, g=g: e.activation(
                            xtm[r][:, g * 512:(g + 1) * 512], ps[PS_TR][:], AF.Copy),
                            reads=[psb[PS_TR]], writes=[b_xtm[r]])
                    P.dma("sp", dst[1][tok0 + tt * 128: tok0 + (tt + 1) * 128, :], xtm[r][:], b_xtm[r],
                          reads=[b_xtm[r]])

        prologue(0)
        for c in range(NT):
            if not passthru:
                gate_up(c)
            if c + 1 < NT:
                prologue(c + 1)
            if not passthru:
                down(c)
            epilogue(c)
        P.barrier()


def ffn_load_only(k, src, dst):
    ffn_phase(k, None, src, dst, None, None)


def rms_scale(k, src, b_src, sq, b_sq, rstd, b_rstd, ps_i, T, lnexp=False):
    P = k.P
    ps, psb = k.ps, k.psb
    for dc in range(DC):
        P.op("act", lambda e, dc=dc: e.activation(sq[:, dc, :], src[:, dc, :], AF.Square),
             reads=[b_src[dc]], writes=[b_sq[dc]])
    for dc in range(DC):
        P.op("pe", lambda e, dc=dc: e.matmul(ps[ps_i][:, 0:T], k.ones_t[:], sq[:, dc, :],
                                              start=(dc == 0), stop=(dc == DC - 1)),
             reads=[b_sq[dc], k.cb], writes=[psb[ps_i]])
    if lnexp:
        P.op("act", lambda e: e.activation(rstd[:], ps[ps_i][:, 0:T], AF.Ln, bias=k.eps_t[:], scale=1.0 / D),
             reads=[psb[ps_i], k.cb], writes=[b_rstd])
        P.op("act", lambda e: e.activation(rstd[:], rstd[:], AF.Exp, scale=-0.5),
             reads=[b_rstd], writes=[b_rstd])
        return
    P.op("act", lambda e: e.activation(rstd[:], ps[ps_i][:, 0:T], AF.Sqrt, bias=k.eps_t[:], scale=1.0 / D),
         reads=[psb[ps_i], k.cb], writes=[b_rstd])
    P.op("dve", lambda e: e.reciprocal(rstd[:], rstd[:]),
         reads=[b_rstd], writes=[b_rstd])


class Ring:
    def __init__(self, tiles, name):
        self.t = tiles
        self.b = bufs(len(tiles), name)
        self.i = 0

    def next(self):
        r = self.i % len(self.t)
        self.i += 1
        return self.t[r], self.b[r]


def _ps_bf16(k, bank):
    return k.psall.bitcast(BF16)[:, bank * 1024:(bank + 1) * 1024]


def proj_phase(k):
    nc, P = k.nc, k.P
    T = 512
    NT = k.NTOK // T
    S = k.S
    wd = k.w["w_mix_in"]
    ps, psb = k.ps, k.psb
    with ExitStack() as st:
        sb = lambda name, shape, dt: st.enter_context(nc.sbuf_tensor(f"sb_pj_{name}", shape, dt))
        w = sb("w", [128, DC, 4096], BF16)
        xT = [sb(f"xT{i}", [128, DC, T], F32) for i in range(2)]
        uT = [sb(f"uT{i}", [128, DC, T], BF16) for i in range(2)]
        sq = sb("sq", [128, DC, T], BF16)
        rstd = [sb(f"rstd{i}", [128, T], F32) for i in range(2)]
        lbl_t = sb("lbl", [128, 2048], F32)
        lb_t = sb("lb", [128, 2, 512], F32)
        oml_t = sb("oml", [128, 2, 512], F32)
        fmbR = Ring([sb(f"fmb{i}", [128, T], BF16) for i in range(4)], "fmb")
        tmbR = Ring([sb(f"tmb{i}", [128, 512], BF16) for i in range(14)], "tmb")
        tmfR = Ring([sb(f"tmf{i}", [128, 512], F32) for i in range(4)], "tmf")
        tmpR = Ring([sb(f"tmp{i}", [128, 512], F32) for i in range(8)], "tmp")
        b_w = bufs(2, "pjw")
        b_xT = bufs(2, "pjxT")
        b_uT = bufs(2, "pjuT")
        b_sq = bufs(DC, "pjsq")
        b_rstd = bufs(2, "pjrstd")
        b_lbl, b_lb, b_oml = Buf("lbl"), Buf("lb"), Buf("oml")
        psT = _ps_bf16(k, 5)
        ctr = {"fm": 0, "tm": 0}

        for g in range(2):
            for dc in range(DC):
                P.dma("pool", w[:, dc, g * 2048:(g + 1) * 2048], wd[dc * 128:(dc + 1) * 128, g * 2048:(g + 1) * 2048],
                      b_w[g], writes=[b_w[g]])
        wg = lambda col: b_w[col // 2048]

        P.dma("sp", lbl_t[:], k.lbl, b_lbl, writes=[b_lbl])
        lv = lbl_t[:].rearrange("p (d l f) -> p d l f", d=2, l=2)
        P.op("dve", lambda e: e.tensor_tensor(lb_t[:], lv[:, :, 0, :], lv[:, :, 1, :], ALU.subtract),
             reads=[b_lbl], writes=[b_lb])
        P.op("act", lambda e: e.activation(lb_t[:], lb_t[:], AF.Exp, scale=-1.0), reads=[b_lb], writes=[b_lb])
        P.op("dve", lambda e: e.tensor_scalar(lb_t[:], lb_t[:], 1.0, None, ALU.add), reads=[b_lb], writes=[b_lb])
        P.op("dve", lambda e: e.reciprocal(lb_t[:], lb_t[:]), reads=[b_lb], writes=[b_lb])
        P.op("dve", lambda e: e.tensor_scalar(oml_t[:], lb_t[:], -1.0, 1.0, ALU.mult, ALU.add),
             reads=[b_lb], writes=[b_oml])

        def sig_act(psrc, b_psrc, t, bt):
            P.op("act", lambda e: e.activation(t[:], psrc, AF.Exp, scale=-1.0), reads=[b_psrc], writes=[bt])
            P.op("act", lambda e: e.activation(t[:], t[:], AF.Ln, bias=k.one_t[:]), reads=[bt, k.cb], writes=[bt])
            P.op("act", lambda e: e.activation(t[:], t[:], AF.Exp, scale=-1.0), reads=[bt], writes=[bt])

        def silu_evac(psrc, b_psrc, dst, b_dst):
            t, bt = tmpR.next()
            sig_act(psrc, b_psrc, t, bt)
            P.op("dve", lambda e: e.tensor_tensor(dst, psrc, t[:], ALU.mult), reads=[bt, b_psrc], writes=[b_dst])

        import collections
        pend = collections.deque()
        LAG = 2

        def drain(n):
            while len(pend) > n:
                pend.popleft()()

        def pload(c):
            s = c % 2
            tok0 = c * T
            for dc in range(DC):
                P.dma("sp", xT[s][:, dc, :], k.h1T[dc, :, tok0:tok0 + T], b_xT[s], writes=[b_xT[s]])

        def prologue(c):
            s = c % 2
            rms_scale(k, xT[s], [b_xT[s]] * DC, sq, b_sq, rstd[s], b_rstd[s], 7, T, lnexp=True)
            for dc in range(DC):
                P.op("dve", lambda e, s=s, dc=dc: e.scalar_tensor_tensor(
                    uT[s][:, dc, :], xT[s][:, dc, :], k.nv_t[:, 2 * DC + dc: 2 * DC + dc + 1], rstd[s][:],
                    ALU.mult, ALU.mult),
                    reads=[b_xT[s], b_rstd[s], k.cb], writes=[b_uT[s]])

        def fm_part(c):
            s = c % 2
            tok0 = c * T
            b, sl = tok0 // S, tok0 % S
            for kind, colbase, dstT in (("qa", 0, k.QT_s), ("ka", 512, k.KT_s), ("qr", 1536, k.qrT_s)):
                for h in range(NH):
                    pb = ctr["fm"] % 2
                    ctr["fm"] += 1
                    col = colbase + h * 128
                    for dc in range(DC):
                        P.op("pe", lambda e, pb=pb, dc=dc, col=col, s=s: e.matmul(
                            ps[pb][:, 0:T], w[:, dc, col:col + 128], uT[s][:, dc, :],
                            start=(dc == 0), stop=(dc == DC - 1)),
                            reads=[wg(col), b_uT[s]], writes=[psb[pb]])
                    o, bo = fmbR.next()
                    if kind == "qr":
                        silu_evac(ps[pb][:, 0:T], psb[pb], o[:], bo)
                    else:
                        P.op("act", lambda e, o=o, pb=pb: e.activation(o[:], ps[pb][:, 0:T], AF.Copy),
                             reads=[psb[pb]], writes=[bo])
                    P.dma("sp", dstT[b, h, :, sl:sl + T], o[:], bo, reads=[bo])

        def tm_part(c):
            s = c % 2
            tok0 = c * T
            b, sl = tok0 // S, tok0 % S
            for tt in range(T // 128):
                sl2 = sl + tt * 128
                for kind, colbase in (("v", 1024), ("ir", 2048), ("f0", 2560), ("f1", 3072), ("gr", 3584)):
                    pb = 2 + ctr["tm"] % 3
                    ctr["tm"] += 1
                    for dc in range(DC):
                        P.op("pe", lambda e, pb=pb, dc=dc, colbase=colbase, s=s, tt=tt: e.matmul(
                            ps[pb][:, :], uT[s][:, dc, tt * 128:(tt + 1) * 128], w[:, dc, colbase:colbase + 512],
                            start=(dc == 0), stop=(dc == DC - 1)),
                            reads=[wg(colbase), b_uT[s]], writes=[psb[pb]])
                    if kind in ("v", "ir"):
                        dst = k.V_s if kind == "v" else k.ir_s
                        o, bo = tmbR.next()
                        P.op("act", lambda e, o=o, pb=pb: e.activation(o[:], ps[pb][:, :], AF.Copy),
                             reads=[psb[pb]], writes=[bo])
                        P.dma("sp", dst[b, sl2:sl2 + 128, :], o[:], bo, reads=[bo])
                    elif kind == "gr":
                        o, bo = tmbR.next()
                        silu_evac(ps[pb][:, :], psb[pb], o[:], bo)
                        P.dma("sp", k.gate_s[b, sl2:sl2 + 128, :], o[:], bo, reads=[bo])
                    else:
                        d = int(kind[1])
                        t, bt = tmpR.next()
                        sig_act(ps[pb][:, :], psb[pb], t, bt)
                        P.op("dve", lambda e, t=t, d=d: e.tensor_tensor(t[:], t[:], oml_t[:, d, :], ALU.mult),
                             reads=[bt, b_oml], writes=[bt])
                        P.op("dve", lambda e, t=t, d=d: e.tensor_tensor(t[:], t[:], lb_t[:, d, :], ALU.add),
                             reads=[bt, b_lb], writes=[bt])

                        def stage2(t=t, bt=bt, d=d, b=b, sl2=sl2):
                            g, bg = tmfR.next()
                            P.op("act", lambda e: e.activation(g[:], t[:], AF.Ln), reads=[bt], writes=[bg])
                            ghi, bghi = tmbR.next()
                            glo, bglo = tmbR.next()
                            P.op("pool", lambda e: e.tensor_copy(ghi[:], g[:]), reads=[bg], writes=[bghi])
                            P.op("pool", lambda e: e.tensor_tensor(glo[:], g[:], ghi[:], ALU.subtract),
                                 reads=[bg, bghi], writes=[bglo])
                            P.dma("sp", k.ghi_s[d][b, sl2:sl2 + 128, :], ghi[:], bghi, reads=[bghi])
                            P.dma("sp", k.glo_s[d][b, sl2:sl2 + 128, :], glo[:], bglo, reads=[bglo])
                            kb, bkb = tmbR.next()
                            P.op("pool", lambda e: e.tensor_scalar(kb[:], t[:], -1.0, 1.0, ALU.mult, ALU.add),
                                 reads=[bt], writes=[bkb])
                            P.dma("sp", k.k_s[d][b, sl2:sl2 + 128, :], kb[:], bkb, reads=[bkb])
                            for h in range(NH):
                                P.op("pe", lambda e, h=h: e.transpose(
                                    psT[:, h * 128:(h + 1) * 128], kb[:, h * 128:(h + 1) * 128], k.idb_t[:]),
                                    reads=[bkb, k.cb], writes=[psb[5]])
                            o, bo = fmbR.next()
                            P.op("act", lambda e: e.activation(o[:], psT[:, 0:512], AF.Copy),
                                 reads=[psb[5]], writes=[bo])
                            P.dma("sp", k.kT_s[d][b, :, :, sl2:sl2 + 128].rearrange("h p t -> p h t"),
                                  o[:].rearrange("p (h t) -> p h t", h=NH), bo, reads=[bo])
                        pend.append(stage2)
                    drain(LAG)

        pload(0)
        prologue(0)
        for c in range(NT):
            if c + 1 < NT:
                pload(c + 1)
            fm_part(c)
            if c + 1 < NT:
                prologue(c + 1)
            tm_part(c)
        drain(0)
        P.barrier()


def attn_phase(k):
    nc, P = k.nc, k.P
    S = k.S
    NJ = S // 128
    NQC = S // 512
    SCALE = 64 ** -0.5
    ps, psb = k.ps, k.psb
    with ExitStack() as st:
        sb = lambda name, shape, dt: st.enter_context(nc.sbuf_tensor(f"sb_at_{name}", shape, dt))
        QT = [sb(f"QT{i}", [128, S], BF16) for i in range(2)]
        KT = [sb(f"KT{i}", [128, S], BF16) for i in range(2)]
        V = [sb(f"V{i}", [128, NJ, 132], BF16) for i in range(2)]
        b_in = bufs(2, "atin")
        b_vone = bufs(2, "vone")
        G = sb("G", [128, NH, 1152], F32)
        cfar = sb("cfar", [128, 2 * NH], F32)
        lamv = sb("lamv", [128, 4, 64], F32)
        lamt = sb("lamt", [128, 8], F32)
        ljunk = sb("ljunk", [128, 64], F32)
        ahn = sb("ahn", [128, 128], F32)
        b_G, b_cfar, b_lamv, b_lam, b_ahn = Buf("G"), Buf("cfar"), Buf("lamv"), Buf("lam"), Buf("ahn")
        PTR = Ring([sb(f"PT{i}", [128, 1024], BF16) for i in range(3)], "PT")
        TNR = Ring([sb(f"TN{i}", [128, 1024], F32) for i in range(2)], "TN")
        OsbR = Ring([sb(f"Osb{i}", [128, 3, 396], F32) for i in range(2)], "Osb")
        rrR = Ring([sb(f"rr{i}", [128, 2], F32) for i in range(4)], "rr")
        t0R = Ring([sb(f"t0{i}", [128, 128], F32) for i in range(4)], "t0")
        odR = Ring([sb(f"od{i}", [128, 128], F32) for i in range(4)], "od")
        jkR = Ring([sb(f"jk{i}", [128, 128], F32) for i in range(2)], "jk")
        ssR = Ring([sb(f"ss{i}", [128, 1], F32) for i in range(4)], "ss")
        oabR = Ring([sb(f"oab{i}", [128, 128], BF16) for i in range(4)], "oab")
        fmbR = Ring([sb(f"fmb{i}", [128, 512], BF16) for i in range(2)], "atfmb")
        psT = _ps_bf16(k, 7)

        for i in range(2):
            P.op("pool", lambda e, i=i: e.memset(V[i][:, :, 128:132], 1.0), writes=[b_vone[i]])
        P.dma("sp", G[:].rearrange("p h m -> p (h m)"), k.gbias, b_G, writes=[b_G])
        P.dma("sp", cfar[:], k.cfar, b_cfar, writes=[b_cfar])
        P.dma("sp", lamv[:].rearrange("p a f -> p (a f)"), k.lamv, b_lamv, writes=[b_lamv])
        P.dma("sp", ahn[:], k.hnorm[:, 0:128], b_ahn, writes=[b_ahn])
        for i in range(2):
            P.op("dve", lambda e, i=i: e.tensor_tensor(ljunk[:], lamv[:, 2 * i, :], lamv[:, 2 * i + 1, :], ALU.mult),
                 reads=[b_lamv], writes=[b_lam])
            P.op("dve", lambda e, i=i: e.reduce_sum(lamt[:, i:i + 1], ljunk[:], AX.X), reads=[b_lam], writes=[b_lam])
        P.op("act", lambda e: e.activation(lamt[:, 2:4], lamt[:, 0:2], AF.Exp), reads=[b_lam], writes=[b_lam])
        P.op("dve", lambda e: e.tensor_tensor(lamt[:, 4:5], lamt[:, 2:3], lamt[:, 3:4], ALU.subtract),
             reads=[b_lam], writes=[b_lam])
        P.op("dve", lambda e: e.tensor_scalar(lamt[:, 5:6], lamt[:, 4:5], 0.2, -1.0, ALU.add, ALU.mult),
             reads=[b_lam], writes=[b_lam])
        P.op("dve", lambda e: e.tensor_scalar(ahn[:], ahn[:], 0.8, None, ALU.mult), reads=[b_ahn], writes=[b_ahn])

        def load(idx):
            b, h = divmod(idx, NH)
            sl = idx % 2
            P.dma("sp", QT[sl][:], k.QT_s[b, h, :, :], b_in[sl], writes=[b_in[sl]])
            P.dma("sp", KT[sl][:], k.KT_s[b, h, :, :], b_in[sl], writes=[b_in[sl]])
            P.dma("sp", V[sl][:, :, 0:128], k.V_s[b, :, h * 128:(h + 1) * 128].rearrange("(j p) e -> p j e", p=128),
                  b_in[sl], writes=[b_in[sl]])

        def qk(sl, qc, j):
            p = j % 2
            for c in range(2):
                P.op("pe", lambda e, c=c, p=p: e.matmul(
                    ps[2 * p + c][:, :], KT[sl][c * 64:(c + 1) * 64, j * 128:(j + 1) * 128],
                    QT[sl][c * 64:(c + 1) * 64, qc * 512:(qc + 1) * 512], start=True, stop=True),
                    reads=[b_in[sl]], writes=[psb[2 * p + c]])

        def epi_copy(b, h, qc):
            Osb, bO = OsbR.next()
            for bk in range(3):
                n = 396 if bk < 2 else 264
                P.op("dve", lambda e, bk=bk, n=n, Osb=Osb: e.tensor_copy(Osb[:, bk, 0:n], ps[4 + bk][:, 0:n]),
                     reads=[psb[4 + bk]], writes=[bO])
            return (b, h, qc, Osb, bO)

        def epi_rest(ctx):
            b, h, qc, Osb, bO = ctx
            for qt in range(4):
                def acc(c):
                    a = c * 4 + qt
                    return Osb[:, a // 3, (a % 3) * 132:(a % 3) * 132 + 129]
                O0, O1 = acc(0), acc(1)
                rr, brr = rrR.next()
                t0, bt0 = t0R.next()
                od, bod = odR.next()
                jk, bjk = jkR.next()
                ss, bss = ssR.next()
                oab, boab = oabR.next()
                P.op("dve", lambda e, rr=rr, O0=O0: e.reciprocal(rr[:, 0:1], O0[:, 128:129]), reads=[bO], writes=[brr])
                P.op("dve", lambda e, rr=rr, O1=O1: e.reciprocal(rr[:, 1:2], O1[:, 128:129]), reads=[bO], writes=[brr])
                P.op("dve", lambda e, rr=rr: e.tensor_tensor(rr[:, 1:2], rr[:, 1:2], lamt[:, 5:6], ALU.mult),
                     reads=[brr, b_lam], writes=[brr])
                P.op("dve", lambda e, rr=rr, t0=t0, O0=O0: e.tensor_scalar(t0[:], O0[:, 0:128], rr[:, 0:1], None, ALU.mult),
                     reads=[brr, bO], writes=[bt0])
                P.op("dve", lambda e, rr=rr, t0=t0, od=od, O1=O1: e.scalar_tensor_tensor(
                    od[:], O1[:, 0:128], rr[:, 1:2], t0[:], ALU.mult, ALU.add),
                    reads=[brr, bO, bt0], writes=[bod])
                P.op("act", lambda e, jk=jk, od=od, ss=ss: e.activation(jk[:], od[:], AF.Square, accum_out=ss[:]),
                     reads=[bod], writes=[bjk, bss])
                P.op("act", lambda e, ss=ss: e.activation(ss[:], ss[:], AF.Ln, bias=k.eps_t[:], scale=1.0 / 128),
                     reads=[bss, k.cb], writes=[bss])
                P.op("act", lambda e, ss=ss: e.activation(ss[:], ss[:], AF.Exp, scale=-0.5), reads=[bss], writes=[bss])
                P.op("dve", lambda e, oab=oab, od=od, ss=ss: e.scalar_tensor_tensor(
                    oab[:], od[:], ss[:], ahn[:], ALU.mult, ALU.mult),
                    reads=[bod, bss, b_ahn], writes=[boab])
                P.op("pe", lambda e, oab=oab, qt=qt: e.transpose(psT[:, qt * 128:(qt + 1) * 128], oab[:], k.idb_t[:]),
                     reads=[boab, k.cb], writes=[psb[7]])
            o, bo = fmbR.next()
            P.op("act", lambda e, o=o: e.activation(o[:], psT[:, 0:512], AF.Copy), reads=[psb[7]], writes=[bo])
            tok0 = b * S + qc * 512
            P.dma("sp", k.mixT_s[h, :, tok0:tok0 + 512], o[:], bo, reads=[bo])

        NBH = k.NB * NH
        pending = [None]
        PREFETCH = getattr(k, "attn_prefetch", True)
        if PREFETCH:
            load(0)
        for idx in range(NBH):
            b, h = divmod(idx, NH)
            sl = idx % 2
            if PREFETCH:
                if idx + 1 < NBH:
                    load(idx + 1)
            else:
                load(idx)
            for qc in range(NQC):
                qk(sl, qc, 0)
                for j in range(NJ):
                    if j + 1 < NJ:
                        qk(sl, qc, j + 1)
                    if j == min(3, NJ - 1) and pending[0] is not None:
                        epi_rest(pending[0])
                        pending[0] = None
                    p = j % 2
                    d = j - 4 * qc
                    PT, bPT = PTR.next()
                    pair = k.psall[:, p * 1024:(p + 1) * 1024]
                    if -1 <= d <= 4:
                        TN, bTN = TNR.next()
                        g0 = 512 - 128 * d
                        for c in range(2):
                            P.op("dve", lambda e, TN=TN, c=c, p=p, g0=g0, h=h: e.scalar_tensor_tensor(
                                TN[:, c * 512:(c + 1) * 512], ps[2 * p + c][:, :], SCALE, G[:, h, g0:g0 + 512],
                                ALU.mult, ALU.add),
                                reads=[psb[2 * p + c], b_G], writes=[bTN])
                        P.op("act", lambda e, PT=PT, TN=TN: e.activation(PT[:], TN[:], AF.Exp),
                             reads=[bTN], writes=[bPT])
                    else:
                        side = 0 if d < 0 else 1
                        P.op("act", lambda e, PT=PT, pair=pair, side=side, h=h: e.activation(
                            PT[:], pair, AF.Exp, bias=cfar[:, 2 * h + side: 2 * h + side + 1], scale=SCALE),
                            reads=[psb[2 * p], psb[2 * p + 1], b_cfar], writes=[bPT])
                    for c in range(2):
                        for qt in range(4):
                            a = c * 4 + qt
                            bank, col = 4 + a // 3, (a % 3) * 132
                            P.op("pe", lambda e, PT=PT, c=c, qt=qt, bank=bank, col=col, j=j, a=a, sl=sl: e.matmul(
                                ps[bank][:, col:col + 129], PT[:, c * 512 + qt * 128: c * 512 + (qt + 1) * 128],
                                V[sl][:, j, 0:129], start=(j == 0 and a % 3 == 0), stop=(j == NJ - 1),
                                skip_group_check=True),
                                reads=[bPT, b_in[sl], b_vone[sl]], writes=[psb[bank]])
                pending[0] = epi_copy(b, h, qc)
        epi_rest(pending[0])
        P.barrier()


def outproj_phase(k):
    nc, P = k.nc, k.P
    T = 512
    NT = k.NTOK // T
    ps, psb = k.ps, k.psb
    wd = k.w["w_mix_out"]
    with ExitStack() as st:
        sb = lambda name, shape, dt: st.enter_context(nc.sbuf_tensor(f"sb_op_{name}", shape, dt))
        w = sb("w", [128, DC, D], BF16)
        mT = [sb(f"mT{i}", [128, DC, T], BF16) for i in range(2)]
        xT = [sb(f"xT{i}", [128, DC, T], F32) for i in range(2)]
        yT = sb("yT", [128, DC, T], F32)
        sq = sb("sq", [128, DC, T], BF16)
        rstd = sb("rstd", [128, T], F32)
        b_w = Buf("opw")
        b_mT = bufs(2, "opmT")
        b_xT = [bufs(DC, f"opxT{i}_") for i in range(2)]
        b_yT = bufs(DC, "opyT")
        b_sq = bufs(DC, "opsq")
        b_rstd = Buf("oprstd")
        for fc in range(DC):
            P.dma("pool", w[:, fc, :], wd[fc * 128:(fc + 1) * 128, :], b_w, writes=[b_w])

        def load(c):
            s = c % 2
            tok0 = c * T
            for fc in range(DC):
                P.dma("sp", mT[s][:, fc, :], k.mixT_s[fc, :, tok0:tok0 + T], b_mT[s], writes=[b_mT[s]])
            for dc in range(DC):
                P.dma("sp", xT[s][:, dc, :], k.h1T[dc, :, tok0:tok0 + T], b_xT[s][dc], writes=[b_xT[s][dc]])

        load(0)
        for c in range(NT):
            s = c % 2
            tok0 = c * T
            if c + 1 < NT:
                load(c + 1)
            for dc in range(DC):
                pb = dc % 2
                for fc in range(DC):
                    P.op("pe", lambda e, pb=pb, dc=dc, fc=fc, s=s: e.matmul(
                        ps[pb][:, 0:T], w[:, fc, dc * 128:(dc + 1) * 128], mT[s][:, fc, :],
                        start=(fc == 0), stop=(fc == DC - 1)),
                        reads=[b_w, b_mT[s]], writes=[psb[pb]])
                P.op("act", lambda e, pb=pb, dc=dc: e.activation(yT[:, dc, :], ps[pb][:, 0:T], AF.Copy),
                     reads=[psb[pb]], writes=[b_yT[dc]])
            rms_scale(k, yT, b_yT, sq, b_sq, rstd, b_rstd, 7, T, lnexp=True)
            for dc in range(DC):
                P.op("dve", lambda e, dc=dc: e.scalar_tensor_tensor(
                    yT[:, dc, :], yT[:, dc, :], k.nv_t[:, 3 * DC + dc: 3 * DC + dc + 1], rstd[:],
                    ALU.mult, ALU.mult),
                    reads=[b_yT[dc], b_rstd, k.cb], writes=[b_yT[dc]])
                P.op("pool", lambda e, dc=dc, s=s: e.tensor_tensor(
                    xT[s][:, dc, :], xT[s][:, dc, :], yT[:, dc, :], ALU.add),
                    reads=[b_yT[dc], b_xT[s][dc]], writes=[b_xT[s][dc]])
                P.dma("sp", k.h2T[dc, :, tok0:tok0 + T], xT[s][:, dc, :], b_xT[s][dc], reads=[b_xT[s][dc]])
        P.barrier()


def hgrn_phase(k):
    nc, P = k.nc, k.P
    S = k.S
    NCH = S // 128
    ps = k.ps
    with ExitStack() as st:
        sb = lambda name, shape, dt: st.enter_context(nc.sbuf_tensor(f"sb_hg_{name}", shape, dt))
        oacc = sb("oacc", [128, NCH, 512], F32)
        b_oacc = bufs(NCH, "oacc")
        hmat = sb("hmat", [128, 768], BF16)
        hmask = sb("hmask", [128, 256], F32)
        rhn = sb("rhn", [128, 128], F32)
        b_c = bufs(3, "hgc")
        Sf = sb("Sf", [128, NH, 128], F32)
        Sb = sb("Sb", [128, NH, 128], BF16)
        b_Sf = bufs(NH, "Sf")
        b_Sb = bufs(NH, "Sb")
        NIN = 3
        in_t = [dict(ghi=sb(f"ghi{i}", [128, 512], BF16), glo=sb(f"glo{i}", [128, 512], BF16),
                     ktm=sb(f"ktm{i}", [128, 512], BF16),
                     v=sb(f"v{i}", [128, 512], BF16), qT=sb(f"qT{i}", [128, NH, 128], BF16),
                     kT=sb(f"kT{i}", [128, NH, 128], BF16), gate=sb(f"gate{i}", [128, 512], BF16))
                for i in range(NIN)]
        b_inr = bufs(NIN, "hgin")
        EdR = Ring([sb(f"Ed{i}", [128, 512], F32) for i in range(2)], "Ed")
        kdecR = Ring([sb(f"kdec{i}", [128, 512], BF16) for i in range(2)], "kdec")
        EqiR = Ring([sb(f"Eqi{i}", [128, 256], F32) for i in range(8)], "Eqi")
        EkR = Ring([sb(f"Ek{i}", [128, 128], F32) for i in range(4)], "Ek")
        qinR = Ring([sb(f"qin{i}", [128, 128], BF16) for i in range(8)], "qin")
        qitR = Ring([sb(f"qit{i}", [128, 128], BF16) for i in range(8)], "qit")
        kinR = Ring([sb(f"kin{i}", [128, 128], BF16) for i in range(8)], "kin")
        ATmR = Ring([sb(f"ATm{i}", [128, 128], BF16) for i in range(8)], "ATm")
        osumR = Ring([sb(f"osum{i}", [128, 512], F32) for i in range(2)], "osum")
        jkR = Ring([sb(f"jk{i}", [128, 128], F32) for i in range(2)], "hjk")
        ss4R = Ring([sb(f"ss4{i}", [128, 4], F32) for i in range(2)], "ss4")
        onR = Ring([sb(f"on{i}", [128, 512], F32) for i in range(2)], "on")
        oabR = Ring([sb(f"oab{i}", [128, 512], BF16) for i in range(2)], "hoab")
        fmbR = Ring([sb(f"fmb{i}", [128, 512], BF16) for i in range(2)], "hfmb")
        psT = _ps_bf16(k, 6)
        b_gd = Buf("psgd")
        b_gg = bufs(2, "psgg")
        b_at = Buf("psat")
        b_o = Buf("pso")
        b_ds = Buf("psds")
        b_tr = Buf("pstr")

        P.dma("sp", hmat[:], k.hmat, b_c[0], writes=[b_c[0]])
        P.dma("sp", hmask[:], k.hmask, b_c[1], writes=[b_c[1]])
        P.dma("sp", rhn[:], k.hnorm[:, 128:256], b_c[2], writes=[b_c[2]])
        ictr = [0]
        import os
        HG_LEVEL = int(os.environ.get("HG_LEVEL", "5"))
        HG_SKIP = os.environ.get("HG_SKIP", "")

        def step(b, n, d, first, last):
            sl = n * 128
            r = ictr[0] % NIN
            ictr[0] += 1
            tl, bi = in_t[r], b_inr[r]
            P.dma("sp", tl["ghi"][:], k.ghi_s[d][b, sl:sl + 128, :], bi, writes=[bi])
            P.dma("sp", tl["glo"][:], k.glo_s[d][b, sl:sl + 128, :], bi, writes=[bi])
            P.dma("sp", tl["ktm"][:], k.k_s[d][b, sl:sl + 128, :], bi, writes=[bi])
            P.dma("sp", tl["v"][:], k.ir_s[b, sl:sl + 128, :], bi, writes=[bi])
            P.dma("sp", tl["qT"][:], k.qrT_s[b, :, :, sl:sl + 128].rearrange("h p t -> p h t"), bi, writes=[bi])
            P.dma("sp", tl["kT"][:], k.kT_s[d][b, :, :, sl:sl + 128].rearrange("h p t -> p h t"), bi, writes=[bi])
            if d == 1:
                P.dma("sp", tl["gate"][:], k.gate_s[b, sl:sl + 128, :], bi, writes=[bi])
            ghi, glo, ktm, v, qT, kT, gate = (tl["ghi"], tl["glo"], tl["ktm"], tl["v"], tl["qT"], tl["kT"],
                                              tl["gate"])
            M_d = hmat[:, d * 256:(d + 1) * 256]
            U_d = hmat[:, 512 + d * 128: 512 + (d + 1) * 128]
            mask_d = hmask[:, d * 128:(d + 1) * 128]
            yield "loaded"
            P.op("pe", lambda e: e.matmul(ps[0][:, :], U_d, ghi[:], start=True, stop=False),
                 reads=[bi, b_c[0]], writes=[b_gd])
            P.op("pe", lambda e: e.matmul(ps[0][:, :], U_d, glo[:], start=False, stop=True),
                 reads=[bi, b_c[0]], writes=[b_gd])
            Ed, bEd = EdR.next()
            P.op("act", lambda e: e.activation(Ed[:], ps[0][:, :], AF.Exp), reads=[b_gd], writes=[bEd])
            kdec, bkd = kdecR.next()
            P.op("pool", lambda e: e.tensor_tensor(kdec[:], ktm[:], Ed[:], ALU.mult), reads=[bi, bEd], writes=[bkd])
            hs = []
            for pr in range(2):
                for h in (2 * pr, 2 * pr + 1):
                    gg = ps[1 + pr][:, (h % 2) * 256:(h % 2) * 256 + 256]
                    P.op("pe", lambda e, gg=gg, h=h: e.matmul(gg, ghi[:, h * 128:(h + 1) * 128], M_d, start=True, stop=False),
                         reads=[bi, b_c[0]], writes=[b_gg[pr]])
                    P.op("pe", lambda e, gg=gg, h=h: e.matmul(gg, glo[:, h * 128:(h + 1) * 128], M_d, start=False, stop=True),
                         reads=[bi, b_c[0]], writes=[b_gg[pr]])
            for h in range(NH):
                pr = h // 2
                gg = ps[1 + pr][:, (h % 2) * 256:(h % 2) * 256 + 256]
                Eqi, bEqi = EqiR.next()
                Ek, bEk = EkR.next()
                P.op("act", lambda e, gg=gg, Eqi=Eqi: e.activation(Eqi[:], gg, AF.Exp), reads=[b_gg[pr]], writes=[bEqi])
                P.op("act", lambda e, gg=gg, Ek=Ek: e.activation(Ek[:], gg[:, 0:128], AF.Exp, scale=-1.0),
                     reads=[b_gg[pr]], writes=[bEk])
                qin, bqin = qinR.next()
                qit, bqit = qitR.next()
                kin, bkin = kinR.next()
                P.op("dve", lambda e, qin=qin, Eqi=Eqi, h=h: e.tensor_tensor(qin[:], qT[:, h, :], Eqi[:, 0:128], ALU.mult),
                     reads=[bi, bEqi], writes=[bqin])
                P.op("dve", lambda e, qit=qit, Eqi=Eqi, h=h: e.tensor_tensor(qit[:], qT[:, h, :], Eqi[:, 128:256], ALU.mult),
                     reads=[bi, bEqi], writes=[bqit])
                P.op("pool", lambda e, kin=kin, Ek=Ek, h=h: e.tensor_tensor(kin[:], kT[:, h, :], Ek[:], ALU.mult),
                     reads=[bi, bEk], writes=[bkin])
                hs.append((Eqi, bEqi, qin, bqin, qit, bqit, kin, bkin))
            yield "decays"
            ats = []
            for h in range(NH):
                Eqi, bEqi, qin, bqin, qit, bqit, kin, bkin = hs[h]
                at = ps[3][:, h * 128:(h + 1) * 128]
                P.op("pe", lambda e, at=at, kin=kin, qin=qin: e.matmul(at, kin[:], qin[:], start=True, stop=True),
                     reads=[bkin, bqin], writes=[b_at])
            for h in range(NH):
                at = ps[3][:, h * 128:(h + 1) * 128]
                ATm, bATm = ATmR.next()
                P.op("dve", lambda e, at=at, ATm=ATm: e.tensor_tensor(ATm[:], at, mask_d, ALU.mult),
                     reads=[b_at, b_c[1]], writes=[bATm])
                ats.append((ATm, bATm))
            for h in range(NH):
                Eqi, bEqi, qin, bqin, qit, bqit, kin, bkin = hs[h]
                ATm, bATm = ats[h]
                hsl = slice(h * 128, (h + 1) * 128)
                P.op("pe", lambda e, ATm=ATm, hsl=hsl: e.matmul(ps[4][:, hsl], ATm[:], v[:, hsl], start=True, stop=first),
                     reads=[bATm, bi], writes=[b_o])
                if not first:
                    P.op("pe", lambda e, qit=qit, hsl=hsl, h=h: e.matmul(ps[4][:, hsl], qit[:], Sb[:, h, :],
                                                                   start=False, stop=True),
                         reads=[bqit, b_Sb[h]], writes=[b_o])
            if not last:
                for h in range(NH):
                    hsl = slice(h * 128, (h + 1) * 128)
                    P.op("pe", lambda e, hsl=hsl: e.matmul(ps[5][:, hsl], kdec[:, hsl], v[:, hsl], start=True, stop=True),
                         reads=[bkd, bi], writes=[b_ds])
                for h in range(NH):
                    Eqi, bEqi = hs[h][0], hs[h][1]
                    hsl = slice(h * 128, (h + 1) * 128)
                    if first:
                        P.op("dve", lambda e, hsl=hsl, h=h: e.tensor_copy(Sf[:, h, :], ps[5][:, hsl]),
                             reads=[b_ds], writes=[b_Sf[h]])
                    else:
                        dc_ = 255 if d == 0 else 128
                        P.op("dve", lambda e, hsl=hsl, h=h, Eqi=Eqi, dc_=dc_: e.scalar_tensor_tensor(
                            Sf[:, h, :], Sf[:, h, :], Eqi[:, dc_:dc_ + 1], ps[5][:, hsl], ALU.mult, ALU.add),
                            reads=[b_Sf[h], bEqi, b_ds], writes=[b_Sf[h]])
                    P.op("act", lambda e, h=h: e.activation(Sb[:, h, :], Sf[:, h, :], AF.Copy),
                         reads=[b_Sf[h]], writes=[b_Sb[h]])
            if d == 0:
                P.op("act", lambda e: e.activation(oacc[:, n, :], ps[4][:, :], AF.Copy), reads=[b_o], writes=[b_oacc[n]])
                return
            osum, bos = osumR.next()
            P.op("dve", lambda e: e.tensor_tensor(osum[:], oacc[:, n, :], ps[4][:, :], ALU.add),
                 reads=[b_o, b_oacc[n]], writes=[bos])
            ss4, bss = ss4R.next()
            jk, bjk = jkR.next()
            for h in range(NH):
                P.op("act", lambda e, h=h: e.activation(jk[:], osum[:, h * 128:(h + 1) * 128], AF.Square,
                                                        accum_out=ss4[:, h:h + 1]),
                     reads=[bos], writes=[bjk, bss])
            P.op("act", lambda e: e.activation(ss4[:], ss4[:], AF.Ln, bias=k.eps_t[:], scale=1.0 / 128),
                 reads=[bss, k.cb], writes=[bss])
            P.op("act", lambda e: e.activation(ss4[:], ss4[:], AF.Exp, scale=-0.5), reads=[bss], writes=[bss])
            on, bon = onR.next()
            for h in range(NH):
                P.op("dve", lambda e, h=h: e.scalar_tensor_tensor(
                    on[:, h * 128:(h + 1) * 128], osum[:, h * 128:(h + 1) * 128], ss4[:, h:h + 1], rhn[:],
                    ALU.mult, ALU.mult), reads=[bos, bss, b_c[2]], writes=[bon])
            oab, boab = oabR.next()
            P.op("pool", lambda e: e.tensor_tensor(oab[:], on[:], gate[:], ALU.mult), reads=[bon, bi], writes=[boab])
            for h in range(NH):
                P.op("pe", lambda e, h=h: e.transpose(psT[:, h * 128:(h + 1) * 128], oab[:, h * 128:(h + 1) * 128],
                                                      k.idb_t[:]),
                     reads=[boab, k.cb], writes=[b_tr])
            o, bo = fmbR.next()
            P.op("act", lambda e: e.activation(o[:], psT[:, 0:512], AF.Copy), reads=[b_tr], writes=[bo])
            tok0 = b * S + sl
            P.dma("sp", k.mixT_s[4:8, :, tok0:tok0 + 128].rearrange("h p t -> p h t"),
                  o[:].rearrange("p (h t) -> p h t", h=NH), bo, reads=[bo])

        gens = []
        for b in range(k.NB):
            for n in range(NCH):
                gens.append(step(b, n, 0, n == 0, n == NCH - 1))
            for n in range(NCH - 1, -1, -1):
                gens.append(step(b, n, 1, n == NCH - 1, n == 0))
        adv = lambda i: next(gens[i], None)
        NS = len(gens)
        adv(0)
        if NS > 1:
            adv(1)
        adv(0)
        for i in range(NS):
            if i + 2 < NS:
                adv(i + 2)
            if i + 1 < NS:
                adv(i + 1)
            adv(i)
        P.barrier()


_NC_CACHE = {}


def _consts():
    import ml_dtypes
    bf = ml_dtypes.bfloat16
    return {
        "ident_f": np.eye(128, dtype=np.float32),
        "ident_b": np.eye(128, dtype=np.float32).astype(bf),
        "ones_b": np.ones((128, 128), dtype=np.float32).astype(bf),
    }


def _pack_normvecs(vecs):
    a = np.stack([np.asarray(v, np.float32).reshape(DC, 128) for v in vecs], 0)
    return np.ascontiguousarray(a.transpose(2, 0, 1).reshape(128, 7 * DC))


def _rel_bucket_np(rel):
    nb, max_exact = 16, 8
    side = np.where(rel > 0, nb, 0)
    n = np.abs(rel)
    nf = np.maximum(n, 1).astype(np.float32)
    large = max_exact + (np.log(nf / np.float32(max_exact)) / np.float32(math.log(128 / max_exact))
                         * np.float32(nb - max_exact)).astype(np.int32)
    large = np.minimum(large, nb - 1)
    return side + np.where(n < max_exact, n, large)


def _hgrn_consts():
    s_ = np.arange(128)[:, None]
    t_ = np.arange(128)[None, :]
    f = np.float32
    Lf = (s_ <= t_).astype(f)
    Mqf = Lf - (s_ <= 63).astype(f)
    Lb = (s_ >= t_).astype(f)
    Mqb = Lb - (s_ >= 64).astype(f)
    Uf = (s_ > t_).astype(f)
    Ub = (s_ < t_).astype(f)
    hmat = np.concatenate([Mqf, Lf, Mqb, Lb, Uf, Ub], axis=1)
    hmask = np.concatenate([Lf, Lb], axis=1)
    import ml_dtypes
    return np.ascontiguousarray(hmat).astype(ml_dtypes.bfloat16), np.ascontiguousarray(hmask)


def _rep(v, n=128):
    v = np.asarray(v, np.float32).reshape(1, -1)
    return np.ascontiguousarray(np.repeat(v, n, axis=0))


def run(inputs, NB, S, n_cores, stage="full", debug=False):
    key = (NB, S, stage, debug)
    if key not in _NC_CACHE:
        _NC_CACHE[key] = build(NB, S, stage, debug)
    nc = _NC_CACHE[key]
    x = np.asarray(inputs["x"], np.float32)
    xs = x.reshape(n_cores, NB * S, D)
    shared = dict(_consts())
    for nm in ("ffn1_w_in", "ffn1_w_out", "ffn2_w_in", "ffn2_w_out", "w_mix_in", "w_mix_out"):
        shared[nm] = np.ascontiguousarray(np.asarray(inputs[nm], np.float32)[0])
    shared["normvecs"] = _pack_normvecs([
        inputs["ffn1_pre_norm"][0], inputs["ffn1_post_norm"][0], inputs["mix_pre_norm"][0],
        inputs["mix_post_norm"][0], inputs["mix_post_norm"][0], inputs["ffn2_pre_norm"][0],
        inputs["ffn2_post_norm"][0]])
    rb = np.asarray(inputs["rel_bias"], np.float32)
    kl = np.arange(128)[:, None]
    m = np.arange(1152)[None, :]
    bidx = _rel_bucket_np(kl - (m - 512))
    gb = rb[bidx]
    shared["gbias"] = np.ascontiguousarray(gb.transpose(0, 2, 1).reshape(128, NH * 1152))
    shared["cfar"] = _rep(np.stack([rb[15], rb[31]], axis=1).reshape(-1))
    shared["lamv"] = _rep(np.concatenate([np.asarray(inputs[n_], np.float32)[0] for n_ in
                                          ("lambda_q1", "lambda_k1", "lambda_q2", "lambda_k2")]))
    shared["lbl"] = _rep(np.asarray(inputs["lb_logits"], np.float32).reshape(-1))
    shared["hnorm"] = _rep(np.concatenate([np.asarray(inputs["attn_head_norm"], np.float32)[0],
                                           np.asarray(inputs["rnn_head_norm"], np.float32)[0]]))
    shared["hmat"], shared["hmask"] = _hgrn_consts()
    in_maps = []
    for c in range(n_cores):
        m_ = dict(shared)
        m_["x"] = np.ascontiguousarray(xs[c])
        in_maps.append(m_)
    res = run_bass_kernel_spmd(nc, in_maps, core_ids=list(range(n_cores)))
    outs = [np.asarray(r["out"], np.float32).reshape(NB, S, D) for r in res.results]
    if debug:
        return np.concatenate(outs, 0), res.results
    return np.concatenate(outs, 0)


def kernel(**inputs):
    return run(inputs, 2, 4096, N_CORES, "full")
```
